# Optimizing a Trainium2 kernel written in Bass

```python
import jax, jax.numpy as jnp
from jax import lax
import numpy as np

D_MODEL = 1024
BATCH = 32
SEQ = 256
DEPTH = 2
DEC_BATCH = 8
DEC_SEQ = 1024
PAST_LEN = 512

GRID_W = 64
HEAD_DIM = 64
CONV_WIDTH = 256
CONV_K = 3
GQA_HEADS = 8
GQA_KV_HEADS = 2
GQA_GROUP = GQA_HEADS // GQA_KV_HEADS
WINDOW = 128
BAND_BLOCK = 128
NA_HEADS = 4
NA_WIN_H = 8
NA_WIN_W = 16
MLA_HEADS = 4
MLA_Q_RANK = 256
MLA_KV_RANK = 128
MLA_NOPE = 64
MLA_ROPE = 32
MLA_V = 64
Q_BLOCK = 128
D_FF = 4 * D_MODEL
N_BRANCH = 4
ROPE_BASE = 10000.0
EPS = 1e-6
NEG_INF = -1e30
ATTN_SCALE = HEAD_DIM ** -0.5
MLA_SCALE = (MLA_NOPE + MLA_ROPE) ** -0.5
GQA_WIDTH = GQA_HEADS * HEAD_DIM
GQA_KV_WIDTH = GQA_KV_HEADS * HEAD_DIM
NA_WIDTH = NA_HEADS * HEAD_DIM
MLA_WIDTH = MLA_HEADS * MLA_V
IN_SIZES = (CONV_WIDTH, CONV_WIDTH, CONV_WIDTH,
            GQA_WIDTH, GQA_KV_WIDTH, GQA_KV_WIDTH,
            NA_WIDTH, NA_WIDTH, NA_WIDTH,
            MLA_Q_RANK, MLA_KV_RANK, MLA_ROPE,
            N_BRANCH * D_MODEL)
D_IN = sum(IN_SIZES)

kernel_name = "hybrid_diffusion_prefix_step"


def rms_norm(x, g):
    xf = x.astype(jnp.float32)
    y = xf * lax.rsqrt(jnp.mean(xf * xf, axis=-1, keepdims=True) + EPS)
    return (y * g.astype(jnp.float32)).astype(x.dtype)


def adaln(c_vec, w_mod, b_mod):
    m = jax.nn.silu(c_vec) @ w_mod + b_mod
    return jnp.split(m[:, None, :], 6, axis=-1)


def modulate(h, shift, scale):
    return h * (1 + scale) + shift


def grid_positions(n):
    t = jnp.arange(n)
    return (t // GRID_W).astype(jnp.float32), (t % GRID_W).astype(jnp.float32)


def rope_axis(x, pos):
    half = x.shape[-1] // 2
    inv = ROPE_BASE ** (-jnp.arange(half, dtype=jnp.float32) / half)
    ang = pos[:, None] * inv[None, :]
    cos = jnp.cos(ang)[None, :, None, :].astype(x.dtype)
    sin = jnp.sin(ang)[None, :, None, :].astype(x.dtype)
    x1, x2 = x[..., :half], x[..., half:]
    return jnp.concatenate([x1 * cos - x2 * sin, x1 * sin + x2 * cos], axis=-1)


def rope_2d(x, rows, cols):
    d = x.shape[-1] // 2
    return jnp.concatenate([rope_axis(x[..., :d], rows), rope_axis(x[..., d:], cols)], axis=-1)


def in_project(h, w_in):
    points = [int(v) for v in np.cumsum(IN_SIZES)[:-1]]
    return jnp.split(h @ w_in, points, axis=-1)


def short_conv(b_gate, c_gate, v, w_conv):
    u = c_gate * v
    s = u.shape[1]
    pad = CONV_K // 2
    up = jnp.pad(u, ((0, 0), (pad, pad), (0, 0)))
    y = sum(up[:, i:i + s] * w_conv[i] for i in range(CONV_K))
    return b_gate * y


def softmax_sink(s, sink):
    if sink is None:
        return jax.nn.softmax(s, axis=-1)
    col = jnp.broadcast_to(sink.astype(jnp.float32)[:, :, None, None], s.shape[:-1] + (1,))
    return jax.nn.softmax(jnp.concatenate([s, col], axis=-1), axis=-1)[..., :-1]


def dense_attention(q, k, v, scale, sink=None):
    n, sq, kh, g, dq = q.shape
    nb = sq // Q_BLOCK
    qb = jnp.moveaxis(q.reshape(n, nb, Q_BLOCK, kh, g, dq), 1, 0)

    def one_block(qi):
        s = jnp.einsum('nqkgd,nskd->nkgqs', qi, k).astype(jnp.float32) * scale
        p = softmax_sink(s, sink)
        return jnp.einsum('nkgqs,nskd->nqkgd', p.astype(v.dtype), v)

    o = lax.map(one_block, qb)
    return jnp.moveaxis(o, 0, 1).reshape(n, sq, kh, g, v.shape[-1])


def banded_window_attention(q, k, v, k_ctx, v_ctx, sink, scale):
    n, s, kh, g, d = q.shape
    nb = s // BAND_BLOCK
    span = BAND_BLOCK + 2 * WINDOW
    kp = jnp.pad(k, ((0, 0), (WINDOW, WINDOW), (0, 0), (0, 0)))
    vp = jnp.pad(v, ((0, 0), (WINDOW, WINDOW), (0, 0), (0, 0)))
    idx = jnp.arange(nb)[:, None] * BAND_BLOCK + jnp.arange(span)[None, :]
    kb = kp[:, idx]
    vb = vp[:, idx]
    qb = q.reshape(n, nb, BAND_BLOCK, kh, g, d)
    s_loc = jnp.einsum('nbqkgd,nbskd->nbkgqs', qb, kb).astype(jnp.float32) * scale
    qpos = jnp.arange(nb)[:, None] * BAND_BLOCK + jnp.arange(BAND_BLOCK)[None, :]
    kpos = idx - WINDOW
    valid = (jnp.abs(qpos[:, :, None] - kpos[:, None, :]) <= WINDOW) \
        & (kpos[:, None, :] >= 0) & (kpos[:, None, :] < s)
    s_loc = jnp.where(valid[None, :, None, None, :, :], s_loc, NEG_INF)
    s_ctx = jnp.einsum('nbqkgd,nskd->nbkgqs', qb, k_ctx).astype(jnp.float32) * scale
    p = softmax_sink(jnp.concatenate([s_loc, s_ctx], axis=-1), sink)
    p_loc = p[..., :span].astype(v.dtype)
    p_ctx = p[..., span:].astype(v.dtype)
    o = jnp.einsum('nbkgqs,nbskd->nbqkgd', p_loc, vb) + jnp.einsum('nbkgqs,nskd->nbqkgd', p_ctx, v_ctx)
    return o.reshape(n, s, kh, g, d)


def neighbourhood_attention(q, k, v, k_ctx, v_ctx, rpb, scale):
    n, s, h, d = q.shape
    rows = s // GRID_W
    wr = min(NA_WIN_H, rows)
    r = jnp.arange(rows)
    r0 = jnp.clip(r - wr // 2, 0, rows - wr)
    krow = r0[:, None] + jnp.arange(wr)[None, :]
    col = jnp.arange(GRID_W)
    c0 = jnp.clip(col - NA_WIN_W // 2, 0, GRID_W - NA_WIN_W)
    col_ok = (col[None, :] >= c0[:, None]) & (col[None, :] < c0[:, None] + NA_WIN_W)
    kg = k.reshape(n, rows, GRID_W, h, d)[:, krow]
    vg = v.reshape(n, rows, GRID_W, h, d)[:, krow]
    qg = q.reshape(n, rows, GRID_W, h, d)
    s_loc = jnp.einsum('nrqhd,nrawhd->nhrqaw', qg, kg).astype(jnp.float32) * scale
    drow = krow - r[:, None] + (NA_WIN_H - 1)
    dcol = jnp.clip(col[None, :] - col[:, None] + (NA_WIN_W - 1), 0, 2 * NA_WIN_W - 2)
    bias = rpb[:, drow[:, None, :, None], dcol[None, :, None, :]].astype(jnp.float32)
    s_loc = jnp.where(col_ok[:, None, :], s_loc + bias[None], NEG_INF)
    s_loc = s_loc.reshape(n, h, rows, GRID_W, wr * GRID_W)
    s_ctx = jnp.einsum('nrqhd,nshd->nhrqs', qg, k_ctx).astype(jnp.float32) * scale
    p = jax.nn.softmax(jnp.concatenate([s_loc, s_ctx], axis=-1), axis=-1)
    p_loc = p[..., :wr * GRID_W].reshape(n, h, rows, GRID_W, wr, GRID_W).astype(v.dtype)
    p_ctx = p[..., wr * GRID_W:].astype(v.dtype)
    o = jnp.einsum('nhrqaw,nrawhd->nrqhd', p_loc, vg) + jnp.einsum('nhrqs,nshd->nrqhd', p_ctx, v_ctx)
    return o.reshape(n, s, h, d)


def mla_queries(mq, g_q, w_uq):
    n, s, _ = mq.shape
    q = (rms_norm(mq, g_q) @ w_uq).reshape(n, s, MLA_HEADS, MLA_NOPE + MLA_ROPE)
    return q[..., :MLA_NOPE], q[..., MLA_NOPE:]


def mla_keys_values(ckv, k_rope, w_ukv):
    n, s, _ = ckv.shape
    kv = (ckv @ w_ukv).reshape(n, s, MLA_HEADS, MLA_NOPE + MLA_V)
    k = jnp.concatenate([kv[..., :MLA_NOPE],
                         jnp.broadcast_to(k_rope[:, :, None, :], (n, s, MLA_HEADS, MLA_ROPE))], axis=-1)
    return k, kv[..., MLA_NOPE:]


def merge_branches(gates, y_conv, y_gqa, y_na, y_mla, lw):
    g1, g2, g3, g4 = jnp.split(jax.nn.sigmoid(gates), N_BRANCH, axis=-1)
    merged = (g1 * (y_conv @ lw['w_branch_conv']) + g2 * (y_gqa @ lw['w_branch_gqa'])
              + g3 * (y_na @ lw['w_branch_na']) + g4 * (y_mla @ lw['w_branch_mla']))
    return merged @ lw['w_o']


def squared_relu_mlp(h, w1, w2):
    return jnp.square(jax.nn.relu(h @ w1)) @ w2


def context_layer(x, c_vec, lw):
    sh1, sc1, gt1, sh2, sc2, gt2 = adaln(c_vec, lw['w_mod'], lw['b_mod'])
    n, l, _ = x.shape
    h = modulate(rms_norm(x, lw['g_attn']), sh1, sc1)
    cb, cc, cv, gq, gk, gv, nq, nk, nv, mq, mkv, mkr, gates = in_project(h, lw['w_in'])
    y_conv = short_conv(cb, cc, cv, lw['w_conv'])
    k_gqa = gk.reshape(n, l, GQA_KV_HEADS, HEAD_DIM)
    v_gqa = gv.reshape(n, l, GQA_KV_HEADS, HEAD_DIM)
    y_gqa = dense_attention(gq.reshape(n, l, GQA_KV_HEADS, GQA_GROUP, HEAD_DIM), k_gqa, v_gqa,
                            ATTN_SCALE, lw['gqa_sink'].reshape(GQA_KV_HEADS, GQA_GROUP))
    k_na = nk.reshape(n, l, NA_HEADS, HEAD_DIM)
    v_na = nv.reshape(n, l, NA_HEADS, HEAD_DIM)
    y_na = dense_attention(nq.reshape(n, l, NA_HEADS, 1, HEAD_DIM), k_na, v_na, ATTN_SCALE)
    ckv = rms_norm(mkv, lw['mla_g_kv'])
    q_nope, q_rope = mla_queries(mq, lw['mla_g_q'], lw['mla_w_uq'])
    k_mla, v_mla = mla_keys_values(ckv, mkr, lw['mla_w_ukv'])
    q_mla = jnp.concatenate([q_nope, q_rope], axis=-1)[:, :, :, None, :]
    y_mla = dense_attention(q_mla, k_mla, v_mla, MLA_SCALE)
    out = merge_branches(gates, y_conv, y_gqa.reshape(n, l, GQA_WIDTH), y_na.reshape(n, l, NA_WIDTH),
                         y_mla.reshape(n, l, MLA_WIDTH), lw)
    x = x + gt1 * out
    x = x + gt2 * squared_relu_mlp(modulate(rms_norm(x, lw['g_mlp']), sh2, sc2), lw['w_ff1'], lw['w_ff2'])
    return x, (k_gqa, v_gqa, k_na, v_na, ckv, mkr)


def latent_layer(x, c, cache, lw):
    k_gqa_c, v_gqa_c, k_na_c, v_na_c, ckv_c, krope_c = cache
    sh1, sc1, gt1, sh2, sc2, gt2 = adaln(c, lw['w_mod'], lw['b_mod'])
    n, s, _ = x.shape
    rows, cols = grid_positions(s)
    h = modulate(rms_norm(x, lw['g_attn']), sh1, sc1)
    cb, cc, cv, gq, gk, gv, nq, nk, nv, mq, mkv, mkr, gates = in_project(h, lw['w_in'])
    y_conv = short_conv(cb, cc, cv, lw['w_conv'])
    q_g = rope_2d(gq.reshape(n, s, GQA_HEADS, HEAD_DIM), rows, cols).reshape(n, s, GQA_KV_HEADS, GQA_GROUP, HEAD_DIM)
    k_g = rope_2d(gk.reshape(n, s, GQA_KV_HEADS, HEAD_DIM), rows, cols)
    v_g = gv.reshape(n, s, GQA_KV_HEADS, HEAD_DIM)
    y_gqa = banded_window_attention(q_g, k_g, v_g, k_gqa_c, v_gqa_c,
                                    lw['gqa_sink'].reshape(GQA_KV_HEADS, GQA_GROUP), ATTN_SCALE)
    y_na = neighbourhood_attention(nq.reshape(n, s, NA_HEADS, HEAD_DIM), nk.reshape(n, s, NA_HEADS, HEAD_DIM),
                                   nv.reshape(n, s, NA_HEADS, HEAD_DIM), k_na_c, v_na_c, lw['na_rpb'], ATTN_SCALE)
    ckv = rms_norm(mkv, lw['mla_g_kv'])
    q_nope, q_rope = mla_queries(mq, lw['mla_g_q'], lw['mla_w_uq'])
    q_rope = rope_2d(q_rope, rows, cols)
    k_rope = rope_2d(mkr[:, :, None, :], rows, cols)[:, :, 0, :]
    k_lat, v_lat = mla_keys_values(ckv, k_rope, lw['mla_w_ukv'])
    k_ctx, v_ctx = mla_keys_values(ckv_c, krope_c, lw['mla_w_ukv'])
    q_mla = jnp.concatenate([q_nope, q_rope], axis=-1)[:, :, :, None, :]
    y_mla = dense_attention(q_mla, jnp.concatenate([k_lat, k_ctx], axis=1),
                            jnp.concatenate([v_lat, v_ctx], axis=1), MLA_SCALE)
    out = merge_branches(gates, y_conv, y_gqa.reshape(n, s, GQA_WIDTH), y_na.reshape(n, s, NA_WIDTH),
                         y_mla.reshape(n, s, MLA_WIDTH), lw)
    x = x + gt1 * out
    x = x + gt2 * squared_relu_mlp(modulate(rms_norm(x, lw['g_mlp']), sh2, sc2), lw['w_ff1'], lw['w_ff2'])
    return x


def setup_inputs(seed: int = 0) -> dict:
    key = jax.random.key(seed)
    ks = jax.random.split(key, 32)

    def nrm(k, shape, scale=1.0):
        return jax.random.normal(k, shape, dtype=jnp.float32) * scale

    d = D_MODEL
    return {
        "x_prompt": nrm(ks[0], (BATCH, SEQ, d)),
        "x_sample": nrm(ks[1], (DEC_BATCH, DEC_SEQ, d)),
        "cache_gqa_k": nrm(ks[2], (DEC_BATCH, DEPTH, PAST_LEN, GQA_KV_HEADS, HEAD_DIM)),
        "cache_gqa_v": nrm(ks[3], (DEC_BATCH, DEPTH, PAST_LEN, GQA_KV_HEADS, HEAD_DIM)),
        "cache_na_k": nrm(ks[4], (DEC_BATCH, DEPTH, PAST_LEN, NA_HEADS, HEAD_DIM)),
        "cache_na_v": nrm(ks[5], (DEC_BATCH, DEPTH, PAST_LEN, NA_HEADS, HEAD_DIM)),
        "cache_mla_ckv": nrm(ks[6], (DEC_BATCH, DEPTH, PAST_LEN, MLA_KV_RANK)),
        "cache_mla_krope": nrm(ks[7], (DEC_BATCH, DEPTH, PAST_LEN, MLA_ROPE)),
        "c": nrm(ks[8], (DEC_BATCH, d)),
        "c_ctx": nrm(ks[9], (d,)),
        "w_mod": nrm(ks[10], (DEPTH, d, 6 * d), 0.5 * d ** -0.5),
        "b_mod": nrm(ks[11], (DEPTH, 6 * d), 0.01),
        "g_attn": 1.0 + nrm(ks[12], (DEPTH, d), 0.02),
        "g_mlp": 1.0 + nrm(ks[13], (DEPTH, d), 0.02),
        "w_in": nrm(ks[14], (DEPTH, d, D_IN), d ** -0.5),
        "w_conv": nrm(ks[15], (DEPTH, CONV_K, CONV_WIDTH), CONV_K ** -0.5),
        "gqa_sink": nrm(ks[16], (DEPTH, GQA_HEADS), 0.5),
        "na_rpb": nrm(ks[17], (DEPTH, NA_HEADS, 2 * NA_WIN_H - 1, 2 * NA_WIN_W - 1), 0.1),
        "mla_g_q": 1.0 + nrm(ks[18], (DEPTH, MLA_Q_RANK), 0.02),
        "mla_w_uq": nrm(ks[19], (DEPTH, MLA_Q_RANK, MLA_HEADS * (MLA_NOPE + MLA_ROPE)), MLA_Q_RANK ** -0.5),
        "mla_g_kv": 1.0 + nrm(ks[20], (DEPTH, MLA_KV_RANK), 0.02),
        "mla_w_ukv": nrm(ks[21], (DEPTH, MLA_KV_RANK, MLA_HEADS * (MLA_NOPE + MLA_V)), MLA_KV_RANK ** -0.5),
        "w_branch_conv": nrm(ks[22], (DEPTH, CONV_WIDTH, d), CONV_WIDTH ** -0.5),
        "w_branch_gqa": nrm(ks[23], (DEPTH, GQA_WIDTH, d), GQA_WIDTH ** -0.5),
        "w_branch_na": nrm(ks[24], (DEPTH, NA_WIDTH, d), NA_WIDTH ** -0.5),
        "w_branch_mla": nrm(ks[25], (DEPTH, MLA_WIDTH, d), MLA_WIDTH ** -0.5),
        "w_o": nrm(ks[26], (DEPTH, d, d), d ** -0.5),
        "w_ff1": nrm(ks[27], (DEPTH, d, D_FF), d ** -0.5),
        "w_ff2": nrm(ks[28], (DEPTH, D_FF, d), D_FF ** -0.5),
        "g_final": 1.0 + nrm(ks[29], (d,), 0.02),
    }


def reference(x_prompt, x_sample, cache_gqa_k, cache_gqa_v, cache_na_k, cache_na_v, cache_mla_ckv,
              cache_mla_krope, c, c_ctx, w_mod, b_mod, g_attn, g_mlp, w_in, w_conv, gqa_sink, na_rpb,
              mla_g_q, mla_w_uq, mla_g_kv, mla_w_ukv, w_branch_conv, w_branch_gqa, w_branch_na,
              w_branch_mla, w_o, w_ff1, w_ff2, g_final):
    h_ctx = x_prompt
    h_lat = x_sample
    st_gqa_k, st_gqa_v, st_na_k, st_na_v, st_ckv, st_krope = [], [], [], [], [], []
    for l in range(DEPTH):
        lw = {
            'w_mod': w_mod[l], 'b_mod': b_mod[l], 'g_attn': g_attn[l], 'g_mlp': g_mlp[l],
            'w_in': w_in[l], 'w_conv': w_conv[l], 'gqa_sink': gqa_sink[l], 'na_rpb': na_rpb[l],
            'mla_g_q': mla_g_q[l], 'mla_w_uq': mla_w_uq[l], 'mla_g_kv': mla_g_kv[l],
            'mla_w_ukv': mla_w_ukv[l], 'w_branch_conv': w_branch_conv[l],
            'w_branch_gqa': w_branch_gqa[l], 'w_branch_na': w_branch_na[l],
            'w_branch_mla': w_branch_mla[l], 'w_o': w_o[l], 'w_ff1': w_ff1[l], 'w_ff2': w_ff2[l],
        }
        h_ctx, st = context_layer(h_ctx, c_ctx[None, :], lw)
        st_gqa_k.append(st[0]); st_gqa_v.append(st[1])
        st_na_k.append(st[2]); st_na_v.append(st[3])
        st_ckv.append(st[4]); st_krope.append(st[5])
        cache_l = (cache_gqa_k[:, l], cache_gqa_v[:, l], cache_na_k[:, l], cache_na_v[:, l],
                   cache_mla_ckv[:, l], cache_mla_krope[:, l])
        h_lat = latent_layer(h_lat, c, cache_l, lw)
    y_prompt = rms_norm(h_ctx, g_final)
    y_sample = rms_norm(h_lat, g_final)
    return (y_prompt, y_sample,
            jnp.stack(st_gqa_k, axis=1), jnp.stack(st_gqa_v, axis=1),
            jnp.stack(st_na_k, axis=1), jnp.stack(st_na_v, axis=1),
            jnp.stack(st_ckv, axis=1), jnp.stack(st_krope, axis=1))
```

```python
import os
import numpy as np
import concourse.bass as bass
import concourse.mybir as mybir
from concourse.ap import AP
from concourse.bass_utils import run_bass_kernel_spmd

F32 = mybir.dt.float32
F32R = mybir.dt.float32r
BF16 = mybir.dt.bfloat16
AF = mybir.ActivationFunctionType
ALU = mybir.AluOpType

ENGS = ("pe", "act", "dve", "pool", "sp")
NCORES = 8
D = 1024
KC = 8
T = 1024
BL = 512
DEPTH = 2
PAST = 512
DIN = 6816
EPS = 1e-6
NEG = -30000.0
MLA_SCALE = 96 ** -0.5


class Unit:
    __slots__ = ("name", "w", "rs")

    def __init__(self, name):
        self.name = name
        self.w = {}
        self.rs = {}


class Prog:
    NRING = 32

    def __init__(self, nc):
        self.nc = nc
        self.streams = {e: [] for e in ENGS}
        self.cnt = {e: 0 for e in ENGS}
        self.sems = {e: nc.alloc_semaphore("sem_" + e) for e in ENGS}
        self.ring = [nc.alloc_semaphore("dq%d" % i) for i in range(self.NRING)]
        self.ring_val = [0] * self.NRING
        self.ring_rng = {"sp": (0, self.NRING // 2), "act": (0, self.NRING // 2), "pool": (self.NRING // 2, self.NRING)}
        self.ring_pos = {"sp": 0, "act": 0, "pool": 0}
        self.seen = {e: {} for e in ENGS}
        self.units = {}
        self.fence = []

    def U(self, *key):
        u = self.units.get(key)
        if u is None:
            u = Unit(key)
            self.units[key] = u
        return u

    def _sem(self, key):
        return self.sems[key] if isinstance(key, str) else self.ring[key]

    def _deps(self, eng, reads, writes):
        evs = []
        for u in reads:
            for w in u.w.values():
                if not (w[2] == eng and eng == "pe"):
                    evs.append(w)
            if u.name[0] == "bank":
                for r in u.rs.values():
                    if r[2] != eng:
                        evs.append(r)
        for u in writes:
            if u.name[0] not in self.PERSIST:
                for f in self.fence:
                    if f[2] != eng:
                        evs.append(f)
            for w in u.w.values():
                if w[2] != eng:
                    evs.append(w)
            for r in u.rs.values():
                if r[2] != eng:
                    evs.append(r)
        need = {}
        for (k, v, e) in evs:
            if self.seen[eng].get(k, 0) >= v:
                continue
            if need.get(k, 0) < v:
                need[k] = v
        for k, v in need.items():
            self.seen[eng][k] = v
        return list(need.items())

    def op(self, eng, fn, reads=(), writes=(), mark=True):
        waits = self._deps(eng, reads, writes)
        self.streams[eng].append((waits, fn, [(eng, 1)] if mark else []))
        if mark:
            self.cnt[eng] += 1
            ev = (eng, self.cnt[eng], eng)
            for u in reads:
                u.rs[eng] = ev
            for u in writes:
                u.w = {eng: ev}
                u.rs = {}

    def dma(self, q, out_ap, in_ap, reads=(), writes=(), **kw):
        waits = self._deps(q, reads, writes)
        lo, hi = self.ring_rng[q]
        key_ = "pool" if q == "pool" else "sp"
        i = lo + self.ring_pos[key_] % (hi - lo)
        self.ring_pos[key_] += 1
        prev = self.ring_val[i]
        if prev > 0 and self.seen[q].get(i, 0) < prev:
            waits.append((i, prev))
            self.seen[q][i] = prev
        self.ring_val[i] = prev + 16
        ev = (i, prev + 16, "dma")

        def fn(engine, out_ap=out_ap, in_ap=in_ap, kw=kw):
            return engine.dma_start(out=out_ap, in_=in_ap, **kw)

        self.streams[q].append((waits, fn, [(i, 16)]))
        for u in reads:
            u.rs[("d", i)] = ev
        for u in writes:
            u.w[("d", i)] = ev
            u.rs = {}

    PERSIST = frozenset(("identf", "onesf", "identb", "gT", "gfin", "cvf", "scb", "bmodT", "mod", "A12", "wconvT",
                         "esk", "gmla", "rstd", "eps", "x", "h", "slot", "bank", "sq", "tmp", "pT"))

    def set_fence(self):
        ev = [(o, self.cnt[o], o) for o in ENGS if self.cnt[o] > 0]
        ev += [(i, self.ring_val[i], "dma") for i in range(self.NRING) if self.ring_val[i] > 0]
        self.fence = ev

    def barrier(self):
        for e in ENGS:
            waits = []
            for o in ENGS:
                if o != e and self.cnt[o] > 0 and self.seen[e].get(o, 0) < self.cnt[o]:
                    waits.append((o, self.cnt[o]))
                    self.seen[e][o] = self.cnt[o]
            for i in range(self.NRING):
                v = self.ring_val[i]
                if v > 0 and self.seen[e].get(i, 0) < v:
                    waits.append((i, v))
                    self.seen[e][i] = v
            if waits:
                self.streams[e].append((waits, None, []))

    def finish(self):
        for i in range(self.NRING):
            v = self.ring_val[i]
            if v > 0 and self.seen["sp"].get(i, 0) < v:
                self.streams["sp"].append(([(i, v)], None, []))
                self.seen["sp"][i] = v
        for e in ENGS:
            if e != "sp" and self.cnt[e] > 0:
                self.streams["sp"].append(([(e, self.cnt[e])], None, []))

    def emit(self):
        nc = self.nc
        engobj = {"pe": "tensor", "act": "scalar", "dve": "vector", "pool": "gpsimd", "sp": "sync"}
        with nc.Block() as block:
            for e in ENGS:
                stream = self.streams[e]

                def body(engine, stream=stream):
                    for waits, fn, incs in stream:
                        for k, v in waits:
                            engine.wait_ge(self._sem(k), v)
                        if fn is None:
                            continue
                        ins = fn(engine)
                        for k, n in incs:
                            ins.then_inc(self._sem(k), n)

                getattr(block, engobj[e])(body)


def _mk(name, *a, **k):
    return lambda e: getattr(e, name)(*a, **k)


class Arena:
    def __init__(self, nc, nbytes):
        self.nc = nc
        self.start, self.end = nc.bump_sbuf(nbytes)
        self.cur = self.start
        self.n = 0
        self.on_release = None

    def alloc(self, name, shape, dt):
        nb = int(np.prod(shape[1:])) * (4 if dt == F32 else 2)
        off = (self.cur + 63) // 64 * 64
        assert off + nb <= self.end, ("SBUF arena overflow", name, off + nb - self.end)
        self.cur = off + nb
        self.n += 1
        return self.nc.alloc_sbuf_tensor_at("%s_%d" % (name, self.n), list(shape), dt, offset=off)

    def mark(self):
        return self.cur

    def release(self, m):
        self.cur = m
        if self.on_release is not None:
            self.on_release()


def _consts():
    c = {}
    c["identf"] = np.eye(128, dtype=np.float32)
    c["onesf"] = np.ones((128, 128), np.float32)
    t = np.arange(T)
    rows = (t // 64).astype(np.float32)
    cols = (t % 64).astype(np.float32)
    Cg = np.zeros((128, T), np.float32)
    Sg = np.zeros((128, T), np.float32)
    Pg = np.zeros((128, 128), np.float32)
    for p in range(128):
        d = p % 64
        b = d // 32
        i = d % 32
        inv = 10000.0 ** (-(i % 16) / 16.0)
        pos = rows if b == 0 else cols
        ang = (pos * np.float32(inv)).astype(np.float32)
        Cg[p] = np.cos(ang)
        Sg[p] = np.sin(ang) * (-1.0 if i < 16 else 1.0)
        partner = p + 16 if i < 16 else p - 16
        Pg[partner, p] = 1.0
    c["ropeCg"], c["ropeSg"], c["permg"] = Cg, Sg, Pg
    Cm = np.ones((128, T), np.float32)
    Sm = np.zeros((128, T), np.float32)
    Pm = np.zeros((128, 128), np.float32)
    for p in range(64):
        Pm[p, p] = 1.0
    for p in range(64, 96):
        d = p - 64
        b = d // 16
        i = d % 16
        inv = 10000.0 ** (-(i % 8) / 8.0)
        pos = rows if b == 0 else cols
        ang = (pos * np.float32(inv)).astype(np.float32)
        Cm[p] = np.cos(ang)
        Sm[p] = np.sin(ang) * (-1.0 if i < 8 else 1.0)
        partner = p + 8 if i < 8 else p - 8
        Pm[partner, p] = 1.0
    c["ropeCm"], c["ropeSm"], c["permm"] = Cm, Sm, Pm
    kk = np.arange(128)[:, None]
    qq = np.arange(128)[None, :]
    mprev = np.where(kk >= qq, 0.0, NEG).astype(np.float32)
    mnext = np.where(kk <= qq, 0.0, NEG).astype(np.float32)
    c["mprev"] = np.tile(mprev, (1, 4))
    c["mnext"] = np.tile(mnext, (1, 4))
    k = np.arange(64)[:, None]
    cq = np.arange(64)[None, :]
    cp = 63 - k
    c0 = np.clip(cq - 8, 0, 48)
    ok = (cp >= c0) & (cp < c0 + 16)
    m01 = np.zeros((128, 64), np.float32)
    mneg = np.zeros((128, 64), np.float32)
    m01[:64] = ok
    mneg[:64] = np.where(ok, 0.0, NEG)
    c["na_m01"], c["na_mneg"] = m01, mneg
    J = np.zeros((128, 256), np.float32)
    for kk_ in range(64):
        J[kk_, 63 - kk_] = 1.0
        J[kk_, 128 + 127 - kk_] = 1.0
    c["na_J"] = J
    c["negt"] = np.full((128, 512), NEG, np.float32)
    return c


CONST_SHAPES = {"identf": (128, 128), "onesf": (128, 128), "ropeCg": (128, T), "ropeSg": (128, T),
                "permg": (128, 128), "ropeCm": (128, T), "ropeSm": (128, T), "permm": (128, 128),
                "mprev": (128, 512), "mnext": (128, 512), "na_m01": (128, 64), "na_mneg": (128, 64),
                "na_J": (128, 256), "negt": (128, 512),
                "gmask": (6, 128, 512), "namask": (1024, 1024), "J128": (128, 128), "J2": (128, 128)}

CONST_BF16 = ("namask", "gmask", "J128", "na_J")

IN_SHAPES = {
    "xp": (T, D), "xs": (T, D),
    "cgk": (DEPTH, PAST, 128), "cgv": (DEPTH, PAST, 128), "cnk": (DEPTH, PAST, 256), "cnv": (DEPTH, PAST, 256),
    "cckv": (DEPTH, PAST, 128), "ckr": (DEPTH, PAST, 32), "cvec": (2, D),
    "w_mod": (DEPTH, D, 6 * D), "b_mod": (DEPTH, 6 * D), "g_attn": (DEPTH, D), "g_mlp": (DEPTH, D),
    "w_in": (DEPTH, D, DIN), "w_conv": (DEPTH, 3, 256), "gqa_sink": (DEPTH, 8), "nat": (DEPTH, 4, 31, 128),
    "mla_g_q": (DEPTH, 256), "mla_w_uq": (DEPTH, 256, 384), "mla_g_kv": (DEPTH, 128),
    "mla_w_ukv": (DEPTH, 128, 512), "w_branch_conv": (DEPTH, 256, D), "w_branch_gqa": (DEPTH, 512, D),
    "w_branch_na": (DEPTH, 256, D), "w_branch_mla": (DEPTH, 256, D), "w_o": (DEPTH, D, D),
    "w_ff1": (DEPTH, D, 4 * D), "w_ff2": (DEPTH, 4 * D, D), "g_final": (D,),
}
OUT_SHAPES = {
    "yp": (T, D), "ys": (T, D), "ogk": (4, DEPTH, 256, 128), "ogv": (4, DEPTH, 256, 128),
    "onk": (4, DEPTH, 256, 256), "onv": (4, DEPTH, 256, 256), "ockv": (4, DEPTH, 256, 128),
    "okr": (4, DEPTH, 256, 32),
}


class _Stop(Exception):
    pass


PREF = 1


def build(debug=None, nlayers=DEPTH, stop=None, plan=None, plan_out=None):
    nc = bass.Bass("TRN2", target_bir_lowering=False)
    P = Prog(nc)
    U = P.U
    I = {n: nc.dram_tensor(n, list(s), F32, kind="ExternalInput").ap() for n, s in IN_SHAPES.items()}
    for n, s in CONST_SHAPES.items():
        I[n] = nc.dram_tensor("c_" + n, list(s), BF16 if n in CONST_BF16 else F32, kind="ExternalInput").ap()
    O = {n: nc.dram_tensor(n, list(s), F32, kind="ExternalOutput").ap() for n, s in OUT_SHAPES.items()}
    DBG = {}
    if debug:
        for n, s in debug.items():
            DBG[n] = nc.dram_tensor("dbg_" + n, list(s), F32, kind="ExternalOutput").ap()

    A = Arena(nc, min(212000, nc.sbuf_bytes_remaining - 512))
    A.on_release = P.set_fence
    banks = [nc.alloc_psum_tensor("bank%d" % i, [128, 512], F32) for i in range(8)]
    bank_i = [0]

    def bank():
        i = bank_i[0]
        bank_i[0] = (i + 1) % 5
        return banks[i], U("bank", i)

    xT = [A.alloc("xT%d" % s, [128, KC, T], F32) for s in range(2)]
    hT = A.alloc("hT", [128, KC, T], BF16)
    NSLOT = 3
    slots = [A.alloc("slot%d" % i, [128, 4096], BF16) for i in range(NSLOT)]
    slot_i = [0]
    identf = A.alloc("identf", [128, 128], F32)
    onesf = A.alloc("onesf", [128, 128], F32)
    identb = A.alloc("identb", [128, 128], BF16)
    gT = A.alloc("gT", [128, 2, DEPTH, KC], F32)
    gfin = A.alloc("gfin", [128, KC], F32)
    cvf = A.alloc("cvf", [128, KC, 2], F32)
    scb = A.alloc("scb", [128, KC, 2], BF16)
    bmodL = [A.alloc("bmodT%d" % i, [128, 48], F32) for i in range(2)]
    modL = [A.alloc("mod%d" % i, [128, 48, 2], F32) for i in range(2)]
    A1L = [A.alloc("A1_%d" % i, [128, KC, 2], F32) for i in range(2)]
    A2L = [A.alloc("A2_%d" % i, [128, KC, 2], F32) for i in range(2)]
    wconvT = A.alloc("wconvT", [128, 2, 3], F32)
    esk = A.alloc("esk", [128, 8], F32)
    gq_mla = A.alloc("gq_mla", [128, 2], F32)
    gkv_col = A.alloc("gkv_col", [128, 1], F32)
    gkv_bc = A.alloc("gkv_bc", [128, 128], F32)
    tmpb = [A.alloc("tmpb%d" % i, [128, BL], F32) for i in range(4)]
    rstd = A.alloc("rstd", [128, BL], F32)
    eps_t = A.alloc("eps_t", [128, 1], F32)
    P.op("dve", _mk("memset", eps_t[:], EPS), writes=[U("eps")])
    pTb = [A.alloc("pT%d" % i, [128, BL], BF16) for i in range(4)]
    rot = {"sq": 0, "tmp": 0, "pT": 0}

    def nxt(kind, lst):
        i = rot[kind]
        rot[kind] = (i + 1) % len(lst)
        return lst[i], U(kind, i)

    def slot():
        i = slot_i[0]
        slot_i[0] = (i + 1) % NSLOT
        return slots[i], U("slot", i)

    def wload(dst_ap, src_ap, u):
        P.dma("pool", dst_ap, src_ap, writes=[u])

    req_n = [0]
    issued = [0]

    def _issue(n):
        name, l_, nrows, c0, ncols = plan[n] if plan is not None else plan_out[n]
        i = n % NSLOT
        sl, su = slots[i], U("slot", i)
        kcn = nrows // 128
        slv = sl[:, 0:kcn * ncols].rearrange("p (kc n) -> p kc n", kc=kcn)
        for k0 in range(0, kcn, 8):
            k1 = min(kcn, k0 + 8)
            wload(slv[:, k0:k1, :], I[name][l_][k0 * 128:k1 * 128, c0:c0 + ncols].rearrange("(kc p) n -> p kc n", p=128), su)

    def req_slot(name, l_, nrows, c0, ncols):
        n = req_n[0]
        req_n[0] += 1
        if plan is None:
            plan_out.append((name, l_, nrows, c0, ncols))
        else:
            assert plan[n] == (name, l_, nrows, c0, ncols), (n, plan[n], name, l_, nrows, c0, ncols)
        while issued[0] <= min(n + PREF, (len(plan) - 1) if plan is not None else n):
            _issue(issued[0])
            issued[0] += 1
        i = n % NSLOT
        kcn = nrows // 128
        return slots[i][:, 0:kcn * ncols].rearrange("p (kc n) -> p kc n", kc=kcn), U("slot", i)

    def small_load(dst_ap, src_ap, u, q="sp"):
        P.dma(q, dst_ap, src_ap, writes=[u], allow_slow_non_contiguous=True)

    small_load(identf[:], I["identf"], U("identf"))
    small_load(onesf[:], I["onesf"], U("onesf"))
    P.op("dve", _mk("tensor_copy", out=identb[:], in_=identf[:]), reads=[U("identf")], writes=[U("identb")])
    for j, nm in enumerate(("g_attn", "g_mlp")):
        for l in range(DEPTH):
            small_load(gT[:, j, l, :], I[nm][l].rearrange("(kc p) -> p kc", p=128), U("gT"))
    small_load(gfin[:], I["g_final"].rearrange("(kc p) -> p kc", p=128), U("gfin"))
    for s in range(2):
        small_load(cvf[:, :, s], I["cvec"][s].rearrange("(kc p) -> p kc", p=128), U("cvf"))
    P.op("act", _mk("activation", out=scb[:], in_=cvf[:], func=AF.Silu), reads=[U("cvf")], writes=[U("scb")])

    def mm_group(mms, reads, writes):
        n = len(mms)
        for i, (o, l, r, st, sp) in enumerate(mms):
            f = (_mk("matmul", o, l, r, start=st, stop=sp))
            if i == n - 1:
                P.op("pe", f, reads=reads, writes=writes, mark=True)
            elif i == 0:
                P.op("pe", f, reads=reads, writes=writes, mark=False)
            else:
                P.op("pe", f, mark=False)

    def mm_first_wait(reads, writes):
        pass

    def load_x(s, src):
        m = A.mark()
        stage = [A.alloc("xstage%d" % i, [128, D], F32) for i in range(2)]
        for tt in range(8):
            st, su = stage[tt % 2], U("xstage", tt % 2)
            P.dma("sp", st[:], src[tt * 128:(tt + 1) * 128, :], writes=[su])
            for half in range(2):
                bk, bu = bank()
                for q in range(4):
                    kc = half * 4 + q
                    P.op("pe", _mk("transpose",
                        bk[:, q * 128:(q + 1) * 128], st[:, kc * 128:(kc + 1) * 128], identf[:]),
                        reads=[su, U("identf")], writes=[bu], mark=(q == 3))
                eng = "act" if half == 0 else "dve"
                dst = xT[s][:, half * 4:half * 4 + 4, tt * 128:(tt + 1) * 128]
                srcp = bk[:].rearrange("p (a b) -> p a b", a=4)
                if eng == "act":
                    P.op("act", _mk("activation", out=dst, in_=srcp, func=AF.Copy),
                         reads=[bu], writes=[U("x", s, tt // 4)])
                else:
                    P.op("dve", _mk("tensor_copy", out=dst, in_=srcp),
                         reads=[bu], writes=[U("x", s, tt // 4)])
        A.release(m)

    load_x(0, I["xp"])

    def adaln_piece(l, j):
        lp = l % 2
        mod, bmodT = modL[lp], bmodL[lp]
        if j == 0:
            for j6 in range(6):
                small_load(bmodT[:, j6 * 8:(j6 + 1) * 8], I["b_mod"][l][j6 * 1024:(j6 + 1) * 1024].rearrange("(c p) -> p c", p=128),
                           U("bmodT", lp))
        slv, su = req_slot("w_mod", l, D, j * 512, 512)
        bk, bu = bank()
        for n in range(4):
            mm_group([(bk[:, n * 2:n * 2 + 2], slv[:, kc, n * 128:(n + 1) * 128], scb[:, kc, :], kc == 0, kc == KC - 1)
                      for kc in range(KC)], reads=[su, U("scb")], writes=[bu])
        P.op("dve", _mk("tensor_tensor", out=mod[:, j * 4:j * 4 + 4, :], in0=bk[:, 0:8].rearrange("p (c s) -> p c s", s=2),
                        in1=bmodT[:, j * 4:j * 4 + 4].unsqueeze(2).broadcast_to([128, 4, 2]), op=ALU.add),
             reads=[bu, U("bmodT", lp)], writes=[U("mod", lp)])
        for (Ax, jj, c0, jdone) in ((A1L[lp], 0, 8, 3), (A2L[lp], 1, 32, 9)):
            if j == jdone:
                P.op("dve", _mk("scalar_tensor_tensor",
                    out=Ax[:], in0=mod[:, c0:c0 + 8, :], scalar=1.0,
                    in1=gT[:, jj, l, :].unsqueeze(2).broadcast_to([128, KC, 2]), op0=ALU.add, op1=ALU.mult),
                    reads=[U("mod", lp), U("gT")], writes=[U("A12", lp)])

    ncall = [0]

    def norm_stats(src_fn, nchunks, blk, src_units, scale):
        bk, bu = bank()
        for kc in range(nchunks):
            sq, squ = nxt("tmp", tmpb)
            src = src_fn(kc)
            P.op("act", _mk("activation", out=sq[:], in_=src, func=AF.Square),
                 reads=src_units(kc), writes=[squ])
            P.op("pe", _mk("matmul", bk[:], onesf[:], sq[:], start=(kc == 0), stop=(kc == nchunks - 1)),
                 reads=[squ, U("onesf")], writes=[bu], mark=True)
        P.op("act", _mk("activation", out=rstd[:], in_=bk[:], func=AF.Ln, bias=eps_t[:, 0:1], scale=scale),
             reads=[bu, U("eps")], writes=[U("rstd")])
        P.op("act", _mk("activation", out=rstd[:], in_=rstd[:], func=AF.Exp, scale=-0.5), reads=[U("rstd")], writes=[U("rstd")])
        if debug and ("rs%d" % ncall[0]) in DBG:
            P.dma("sp", DBG["rs%d" % ncall[0]], rstd[:], reads=[U("rstd")])
        ncall[0] += 1

    def norm_mod(s, Ax, bcol0, mod, lp):
        for blk in range(2):
            norm_stats(lambda kc: xT[s][:, kc, blk * BL:(blk + 1) * BL], KC, blk, lambda kc: [U("x", s, blk)], 1.0 / D)
            for kc in range(KC):
                tm, tu = nxt("tmp", tmpb)
                P.op("dve", _mk("scalar_tensor_tensor",
                    out=tm[:], in0=xT[s][:, kc, blk * BL:(blk + 1) * BL], scalar=Ax[:, kc, s:s + 1], in1=rstd[:],
                    op0=ALU.mult, op1=ALU.mult), reads=[U("x", s, blk), U("A12", lp), U("rstd")], writes=[tu])
                P.op("act", _mk("activation",
                    out=hT[:, kc, blk * BL:(blk + 1) * BL], in_=tm[:], func=AF.Identity,
                    bias=mod[:, bcol0 + kc, s:s + 1], scale=1.0), reads=[tu, U("mod", lp)], writes=[U("h", blk)])

    def proj(slv, c0, ncols, blk, su, out_rows=None):
        bk, bu = bank()
        mm_group([(bk[0:ncols, :], slv[:, kc, c0:c0 + ncols], hT[:, kc, blk * BL:(blk + 1) * BL], kc == 0, kc == KC - 1)
                  for kc in range(KC)], reads=[su, U("h", blk)], writes=[bu])
        return bk, bu

    def load_win(l, c0, ncols):
        return req_slot("w_in", l, D, c0, ncols)

    def rope(bk, bu, rows, dst, dst_units, blk, Ct, St, Pt, pre_scale):
        xs, xsu = nxt("tmp", tmpb)
        P.op("act", _mk("activation", out=xs[0:rows, :], in_=bk[0:rows, :], func=AF.Identity, scale=pre_scale),
             reads=[bu], writes=[xsu])
        b2, b2u = bank()
        P.op("pe", _mk("matmul", b2[0:rows, :], Pt[0:rows, 0:rows], xs[0:rows, :], start=True, stop=True),
             reads=[xsu, U("ropeP")], writes=[b2u])
        t2, t2u = nxt("tmp", tmpb)
        P.op("dve", _mk("tensor_tensor", out=t2[0:rows, :], in0=b2[0:rows, :], in1=St[0:rows, blk * BL:(blk + 1) * BL],
                                              op=ALU.mult), reads=[b2u, U("ropeT")], writes=[t2u])
        P.op("dve", _mk("tensor_tensor", out=xs[0:rows, :], in0=xs[0:rows, :], in1=Ct[0:rows, blk * BL:(blk + 1) * BL],
                                               op=ALU.mult), reads=[xsu, U("ropeT")], writes=[xsu])
        P.op("dve", _mk("tensor_tensor", out=dst, in0=xs[0:rows, :], in1=t2[0:rows, :], op=ALU.add),
             reads=[xsu, t2u], writes=dst_units)

    obank_i = [0]
    att_dbg = [0]

    def obank():
        i = 5 + obank_i[0]
        obank_i[0] = (obank_i[0] + 1) % 3
        return banks[i], U("bank", i)

    def attend(q_rhs, N, ktiles, dst, dst_u, scale, rds, sink=None, norm_dve=True):
        ob, obu = obank()
        n = len(ktiles)
        pend = []

        def pv(p):
            i, pt, ptu, v, c0, c1 = p
            P.op("pe", _mk("matmul", ob[:, c0:c1], v, pt[:, c0:c1], start=(i == 0), stop=(i == n - 1)),
                 reads=[ptu] + rds, writes=[obu], mark=True)

        for i, kt in enumerate(ktiles):
            bk, bu = bank()
            bias = kt.get("bias", [])
            c0, c1 = kt.get("cols", (0, N))
            mms = [(bk[:, c0:c1], kt["k"], q_rhs[:, c0:c1], True, len(bias) == 0)]
            r2 = list(rds)
            for bi, (sel, bt, btu) in enumerate(bias):
                mms.append((bk[:, c0:c1], sel, bt[:, c0:c1], False, bi == len(bias) - 1))
                r2.append(btu)
            mm_group(mms, reads=r2, writes=[bu])
            pt, ptu = nxt("pT", pTb)
            P.op("act", _mk("activation", out=pt[:, c0:c1], in_=bk[:, c0:c1], func=AF.Exp, scale=scale),
                 reads=[bu], writes=[ptu])
            if debug and "att_pt" in DBG and att_dbg[0] == 1 and i < 6:
                tmd, tmdu = nxt("tmp", tmpb)
                P.op("dve", _mk("tensor_copy", out=tmd[:], in_=pt[:]), reads=[ptu], writes=[tmdu])
                P.dma("sp", DBG["att_pt"][:, i, :], tmd[:], reads=[tmdu])
                tmd, tmdu = nxt("tmp", tmpb)
                P.op("act", _mk("activation", out=tmd[:], in_=bk[:], func=AF.Copy), reads=[bu], writes=[tmdu])
                P.dma("sp", DBG["att_st"][:, i, :], tmd[:], reads=[tmdu])
            pend.append((i, pt, ptu, kt["v"], c0, c1))
            if len(pend) > 2:
                pv(pend.pop(0))
        while pend:
            pv(pend.pop(0))
        if debug and "att_ob" in DBG and att_dbg[0] == 1:
            att_dbg[0] = 2
            tmd, tmdu = nxt("tmp", tmpb)
            P.op("act", _mk("activation", out=tmd[:], in_=ob[:], func=AF.Copy), reads=[obu], writes=[tmdu])
            P.dma("sp", DBG["att_ob"], tmd[:], reads=[tmdu])
        rd, rdu = nxt("tmp", tmpb)
        if norm_dve:
            if sink is not None:
                P.op("dve", _mk("tensor_scalar", out=rd[64:128, 0:N], in0=ob[64:128, 0:N], scalar1=sink, scalar2=None, op0=ALU.add),
                     reads=[obu, U("esk")], writes=[rdu])
                P.op("dve", _mk("reciprocal", out=rd[64:128, 0:N], in_=rd[64:128, 0:N]), reads=[rdu], writes=[rdu])
            else:
                P.op("dve", _mk("reciprocal", out=rd[64:128, 0:N], in_=ob[64:128, 0:N]), reads=[obu], writes=[rdu])
        else:
            if sink is not None:
                P.op("act", _mk("activation", out=rd[64:128, 0:N], in_=ob[64:128, 0:N], func=AF.Ln, bias=sink, scale=1.0),
                     reads=[obu, U("esk")], writes=[rdu])
            else:
                P.op("act", _mk("activation", out=rd[64:128, 0:N], in_=ob[64:128, 0:N], func=AF.Ln), reads=[obu], writes=[rdu])
            P.op("act", _mk("activation", out=rd[64:128, 0:N], in_=rd[64:128, 0:N], func=AF.Exp, scale=-1.0), reads=[rdu], writes=[rdu])
        P.op("dve", _mk("tensor_tensor", out=dst, in0=ob[0:64, 0:N], in1=rd[64:128, 0:N], op=ALU.mult),
             reads=[obu, rdu], writes=[dst_u])

    def tok_proj(parts, tt, hu):
        bk, bu = bank()
        off = 0
        for (sv, c0, ncol, su) in parts:
            mm_group([(bk[:, off:off + ncol], hT[:, kc, tt * 128:(tt + 1) * 128], sv[:, kc, c0:c0 + ncol], kc == 0, kc == KC - 1)
                      for kc in range(KC)], reads=[su, hu], writes=[bu])
            off += ncol
        return bk, bu

    def transpose_cache(src_dram, width, dst_fn, stu_name):
        kst = A.alloc("kst", [128, 4, width], F32)
        P.dma("sp", kst[:], src_dram.rearrange("(t p) f -> p t f", p=128), writes=[U(stu_name)])
        for c in range((width + 127) // 128):
            w = min(128, width - c * 128)
            bk, bu = bank()
            for ct in range(4):
                P.op("pe", _mk("transpose", bk[0:w, ct * 128:(ct + 1) * 128], kst[:, ct, c * 128:c * 128 + w], identf[:]),
                     reads=[U(stu_name), U("identf")], writes=[bu], mark=(ct == 3))
            dst_fn(bk, bu, c)

    def dbg_dump(dst, tile, nch, units, c0=0):
        for c in range(nch):
            for blk in range(2):
                tm, tu = nxt("tmp", tmpb)
                P.op("dve", _mk("tensor_copy", out=tm[:], in_=tile[:, c, blk * BL:(blk + 1) * BL]), reads=units, writes=[tu])
                P.dma("sp", dst[:, c0 + c, blk * BL:(blk + 1) * BL], tm[:], reads=[tu])

    def layer_pass(l, s):
        lat = (s == 1)
        NSEQ = 1 if lat else 4
        SL = T // NSEQ
        m_pass = A.mark()
        ycv = A.alloc("ycv", [128, 2, T], BF16)
        ygq = A.alloc("ygq", [128, 4, T], BF16)
        yna = A.alloc("yna", [128, 2, T], BF16)
        yml = A.alloc("yml", [128, 2, T], BF16)
        lp = l % 2
        mod, A1, A2 = modL[lp], A1L[lp], A2L[lp]
        if not lat and l == 0:
            for j_ in range(4):
                adaln_piece(0, j_)
        if debug and "mod" in DBG and l == 0 and s == 0:
            P.dma("sp", DBG["mod"], mod[:], reads=[U("mod", lp)])
        if stop == ("adaln", l, s):
            raise _Stop()
        norm_mod(s, A1, 0, mod, lp)

        if debug and ("h_%d_%d" % (l, s)) in DBG:
            dbg_dump(DBG["h_%d_%d" % (l, s)], hT, KC, [U("h", 0), U("h", 1)])
        if debug and "rstd" in DBG and l == 0 and s == 0:
            P.dma("sp", DBG["rstd"], rstd[:], reads=[U("rstd")])
            P.dma("sp", DBG["A1"], A1[:], reads=[U("A12")])
        if stop == ("norm", l, s):
            raise _Stop()
        for c_ in range(2):
            small_load(wconvT[:, c_, :], I["w_conv"][l][:, c_ * 128:(c_ + 1) * 128].rearrange("k p -> p k"), U("wconvT"))
        s0v, s0u = load_win(l, 0, 512)
        s1v, s1u = load_win(l, 512, 512)
        m = A.mark()
        uc = A.alloc("uc", [128, T], F32)
        yc = A.alloc("yc", [128, T], F32)
        for c in range(2):
            for blk in range(2):
                bcc, bccu = proj(s0v, 256 + c * 128, 128, blk, s0u)
                tm, tu = nxt("tmp", tmpb)
                P.op("act", _mk("activation", out=tm[:], in_=bcc[:], func=AF.Copy), reads=[bccu], writes=[tu])
                bcv, bcvu = proj(s1v, c * 128, 128, blk, s1u)
                P.op("dve", _mk("tensor_tensor",
                    out=uc[:, blk * BL:(blk + 1) * BL], in0=bcv[:], in1=tm[:], op=ALU.mult),
                    reads=[bcvu, tu], writes=[U("uc")])
            ucv = uc[:].rearrange("p (q t) -> p q t", q=NSEQ)
            ycw = yc[:].rearrange("p (q t) -> p q t", q=NSEQ)
            P.op("dve", _mk("tensor_scalar", out=yc[:], in0=uc[:], scalar1=wconvT[:, c, 1:2], scalar2=None, op0=ALU.mult),
                 reads=[U("uc"), U("wconvT")], writes=[U("yc")])
            P.op("dve", _mk("scalar_tensor_tensor",
                out=ycw[:, :, 1:SL], in0=ucv[:, :, 0:SL - 1], scalar=wconvT[:, c, 0:1], in1=ycw[:, :, 1:SL],
                op0=ALU.mult, op1=ALU.add), reads=[U("uc"), U("yc")], writes=[U("yc")])
            P.op("dve", _mk("scalar_tensor_tensor",
                out=ycw[:, :, 0:SL - 1], in0=ucv[:, :, 1:SL], scalar=wconvT[:, c, 2:3], in1=ycw[:, :, 0:SL - 1],
                op0=ALU.mult, op1=ALU.add), reads=[U("uc"), U("yc")], writes=[U("yc")])
            for blk in range(2):
                bcb, bcbu = proj(s0v, c * 128, 128, blk, s0u)
                P.op("dve", _mk("tensor_tensor",
                    out=ycv[:, c, blk * BL:(blk + 1) * BL], in0=bcb[:], in1=yc[:, blk * BL:(blk + 1) * BL], op=ALU.mult),
                    reads=[bcbu, U("yc")], writes=[U("ycv")])
        A.release(m)

        if stop == ("conv", l, s):
            raise _Stop()
        m = A.mark()
        s2v, s2u = load_win(l, 1024, 512)
        qg = A.alloc("qg", [128, 4, T], BF16)
        kA = A.alloc("kA", [128, T], BF16)
        kB = A.alloc("kB", [128, T], BF16)
        kz = {}
        for key_ in ((0, 0), (1, 1), (1, 0), (0, 1)):
            kz[key_] = A.alloc("kz%d%d" % key_, [128, T], BF16)
            P.op("dve", _mk("memset", kz[key_][:], 0.0), writes=[U("kz")])
        vtg = A.alloc("vtg", [128, 8, 2, 128], BF16)
        P.op("dve", _mk("memset", vtg[:].rearrange("p a b c -> p (a b) c")[:, :, 64:128], 1.0), writes=[U("vtg")])
        if lat:
            ropeC = A.alloc("ropeC", [128, T], F32)
            ropeS = A.alloc("ropeS", [128, T], F32)
            ropeP = A.alloc("ropeP", [128, 128], F32)
            P.dma("sp", ropeC[:], I["ropeCg"], writes=[U("ropeT")])
            P.dma("sp", ropeS[:], I["ropeSg"], writes=[U("ropeT")])
            P.dma("sp", ropeP[:], I["permg"], writes=[U("ropeP")])
            gmk = A.alloc("gmk", [128, 6, 512], BF16)
            P.dma("sp", gmk[:], I["gmask"].rearrange("d p q -> p d q"), writes=[U("gmask")])
            kcz = {}
            for key_ in ((0, 0), (1, 1), (1, 0), (0, 1)):
                kcz[key_] = A.alloc("kcz%d%d" % key_, [128, PAST], BF16)
                P.op("dve", _mk("memset", kcz[key_][:], 0.0), writes=[U("kc")])
            vtgc = A.alloc("vtgc", [128, 4, 2, 128], BF16)
            P.op("dve", _mk("memset", vtgc[:].rearrange("p a b c -> p (a b) c")[:, :, 64:128], 1.0), writes=[U("vtgc")])
            for ct in range(4):
                P.dma("pool", vtgc[:, ct, :, 0:64], I["cgv"][l][ct * 128:(ct + 1) * 128, :].rearrange("p (h d) -> p h d", h=2), writes=[U("vtgc")])

            def kdst(bk, bu, c):
                P.op("act", _mk("activation", out=kcz[(0, 0)][0:64, :], in_=bk[0:64, :], func=AF.Copy), reads=[bu], writes=[U("kc")])
                P.op("act", _mk("activation", out=kcz[(1, 1)][64:128, :], in_=bk[64:128, :], func=AF.Copy), reads=[bu], writes=[U("kc")])
                P.op("dve", _mk("tensor_copy", out=kcz[(1, 0)][0:64, :], in_=bk[64:128, :]), reads=[bu], writes=[U("kc")])
                P.op("dve", _mk("tensor_copy", out=kcz[(0, 1)][64:128, :], in_=bk[0:64, :]), reads=[bu], writes=[U("kc")])
            transpose_cache(I["cgk"][l], 128, kdst, "kst_g")
        small_load(esk[:], I["gqa_sink"][l].partition_broadcast(128), U("esk"))
        P.op("act", _mk("activation", out=esk[:], in_=esk[:], func=AF.Exp), reads=[U("esk")], writes=[U("esk")])
        for blk in range(2):
            for ch in range(4):
                sv, su, c0 = (s1v, s1u, 256 + ch * 128) if ch < 2 else (s2v, s2u, (ch - 2) * 128)
                bk, bu = proj(sv, c0, 128, blk, su)
                dst = qg[:, ch, blk * BL:(blk + 1) * BL]
                if lat:
                    rope(bk, bu, 128, dst, [U("qg")], blk, ropeC, ropeS, ropeP, 0.125)
                else:
                    P.op("act", _mk("activation", out=dst, in_=bk[:], func=AF.Identity, scale=0.125),
                         reads=[bu], writes=[U("qg")])
            bk, bu = proj(s2v, 256, 128, blk, s2u)
            bk2, bu2 = bank()
            mm_group([(bk2[0:64, :], s2v[:, kc, 320:384], hT[:, kc, blk * BL:(blk + 1) * BL], kc == 0, kc == KC - 1) for kc in range(KC)],
                     reads=[s2u, U("h", blk)], writes=[bu2])
            mm_group([(bk2[64:128, :], s2v[:, kc, 256:320], hT[:, kc, blk * BL:(blk + 1) * BL], kc == 0, kc == KC - 1) for kc in range(KC)],
                     reads=[s2u, U("h", blk)], writes=[bu2])
            for (b_, bu_, kX) in ((bk, bu, kA), (bk2, bu2, kB)):
                dst = kX[:, blk * BL:(blk + 1) * BL]
                if lat:
                    rope(b_, bu_, 128, dst, [U("kAB")], blk, ropeC, ropeS, ropeP, 1.0)
                else:
                    P.op("act", _mk("activation", out=dst, in_=b_[:], func=AF.Copy), reads=[bu_], writes=[U("kAB")])
        for (key_, src_, r0_) in (((0, 0), kA, 0), ((1, 1), kA, 64), ((1, 0), kB, 0), ((0, 1), kB, 64)):
            P.op("dve", _mk("tensor_copy", out=kz[key_][r0_:r0_ + 64, :], in_=src_[r0_:r0_ + 64, :]), reads=[U("kAB")], writes=[U("kz")])
        if stop == ("gqa_fm", l, s):
            raise _Stop()
        for tt in range(8):
            if lat:
                bk, bu = tok_proj([(s2v, 384, 128, s2u)], tt, U("h", tt // 4))
                voff = 0
            else:
                bk, bu = tok_proj([(s2v, 256, 256, s2u)], tt, U("h", tt // 4))
                voff = 128
            if not os.environ.get("NOVCOPY"):
              P.op("dve", _mk("tensor_copy",
                out=vtg[:, tt, :, 0:64], in_=bk[:, voff:voff + 128].rearrange("p (h d) -> p h d", h=2)),
                reads=[bu], writes=[U("vtg")])
            if not lat and not os.environ.get("NOOUT"):
                st, stu = nxt("tmp", tmpb)
                P.op("act", _mk("activation", out=st[:, 0:256], in_=bk[:, 0:256], func=AF.Copy), reads=[bu], writes=[stu])
                sq_, r0 = tt // 2, (tt % 2) * 128
                P.dma("sp", O["ogk"][sq_, l, r0:r0 + 128, :], st[:, 0:128], reads=[stu])
                P.dma("sp", O["ogv"][sq_, l, r0:r0 + 128, :], st[:, 128:256], reads=[stu])

        if debug and "qg" in DBG and l == 0 and lat:
            dbg_dump(DBG["qg"], qg, 4, [U("qg")])
            dbg_dump(DBG["kAB"], kA[:].rearrange("p (c t) -> p c t", c=1), 1, [U("kAB")], 0)
            dbg_dump(DBG["kAB"], kB[:].rearrange("p (c t) -> p c t", c=1), 1, [U("kAB")], 1)
            pass
        if stop == ("gqa_proj", l, s):
            raise _Stop()

        def kver(kv, hf, cache=False):
            return (kcz if cache else kz)[(kv, hf)]

        rd_g = [U("qg"), U("kz"), U("vtg")]
        for h in range(8):
            kv, hf, ch = h // 4, h % 2, h // 2
            if not lat:
                for sq_ in range(4):
                    q0 = sq_ * 256
                    kts = [dict(k=kver(kv, hf)[:, q0 + kt * 128:q0 + (kt + 1) * 128], v=vtg[:, sq_ * 2 + kt, kv, :]) for kt in range(2)]
                    attend(qg[:, ch, q0:q0 + 256], 256, kts, ygq[hf * 64:(hf + 1) * 64, ch, q0:q0 + 256],
                           U("ygq"), 1.0, rd_g, sink=esk[64:128, h:h + 1], norm_dve=False)
                    if l == 0 and 4 <= 4 + h < 12 and sq_ == 0:
                        adaln_piece(0, 4 + h)
            else:
                for u in range(2):
                    if False and l == 0 and h == 1 and u == 0 and att_dbg[0] == 0:
                        att_dbg[0] = 1
                    kts = []
                    for d in range(-1, 5):
                        j = 4 * u + d
                        if 0 <= j < 8:
                            kts.append(dict(k=kver(kv, hf)[:, j * 128:(j + 1) * 128], v=vtg[:, j, kv, :],
                                            bias=[(identb[:], gmk[:, d + 1, :], U("gmask"))],
                                            cols=(max(0, 128 * d - 128), min(512, 128 * d + 256))))
                    for ct in range(4):
                        kts.append(dict(k=kver(kv, hf, True)[:, ct * 128:(ct + 1) * 128], v=vtgc[:, ct, kv, :]))
                    attend(qg[:, ch, u * BL:(u + 1) * BL], 512, kts, ygq[hf * 64:(hf + 1) * 64, ch, u * BL:(u + 1) * BL],
                           U("ygq"), 1.0, rd_g + [U("kc"), U("vtgc"), U("identb")], sink=esk[64:128, h:h + 1])
                    if l + 1 < nlayers and (h * 2 + u) < 12:
                        adaln_piece(l + 1, h * 2 + u)
        A.release(m)

        if stop == ("gqa", l, s):
            raise _Stop()
        m = A.mark()
        nq = A.alloc("nq", [128, 2, T], BF16)
        nk = A.alloc("nk", [128, 4, T], BF16)
        P.op("dve", _mk("memset", nk[:].rearrange("p a b -> p (a b)"), 0.0), writes=[U("nk")])
        vtn = A.alloc("vtn", [128, 8, 4, 128], BF16)
        P.op("dve", _mk("memset", vtn[:].rearrange("p a b c -> p (a b) c")[:, :, 64:128], 1.0), writes=[U("vtn")])
        s3v, s3u = load_win(l, 1536, 512)
        s4v, s4u = load_win(l, 2048, 512)
        if lat:
            nkc = A.alloc("nkc", [128, 4, PAST], BF16)
            P.op("dve", _mk("memset", nkc[:].rearrange("p a b -> p (a b)"), 0.0), writes=[U("nkc")])
            vtnc = A.alloc("vtnc", [128, 4, 4, 128], BF16)
            P.op("dve", _mk("memset", vtnc[:].rearrange("p a b c -> p (a b) c")[:, :, 64:128], 1.0), writes=[U("vtnc")])
            for ct in range(4):
                P.dma("pool", vtnc[:, ct, :, 0:64], I["cnv"][l][ct * 128:(ct + 1) * 128, :].rearrange("p (h d) -> p h d", h=4), writes=[U("vtnc")])
            def nkdst(bk, bu, c):
                P.op("act", _mk("activation", out=nkc[0:64, 2 * c, :], in_=bk[0:64, :], func=AF.Copy), reads=[bu], writes=[U("nkc")])
                P.op("act", _mk("activation", out=nkc[64:128, 2 * c + 1, :], in_=bk[64:128, :], func=AF.Copy), reads=[bu], writes=[U("nkc")])
            transpose_cache(I["cnk"][l], 256, nkdst, "kst_n")
            J128 = A.alloc("J128", [128, 128], BF16)
            P.dma("sp", J128[:], I["J128"], writes=[U("J128")])
            nbm = {}
            for u_ in range(2):
                for j_ in NA_TILES[u_]:
                    t_ = A.alloc("nbm_%d_%d" % (u_, j_), [128, 512], BF16)
                    jr_ = 7 - j_
                    P.dma("sp", t_[:], I["namask"][jr_ * 128:(jr_ + 1) * 128, u_ * 512:(u_ + 1) * 512], writes=[U("nbm")])
                    nbm[(u_, j_)] = t_
            J2t = A.alloc("J2t", [128, 256], BF16)
            P.dma("sp", J2t[:], I["na_J"], writes=[U("J128")])
            strips = [A.alloc("nstrip%d" % i, [128, 31 * 64], BF16) for i in range(2)]
            for i_ in range(2):
                P.op("dve", _mk("memset", strips[i_][:], 0.0), writes=[U("nstrip", i_)])

            def load_strip(h_):
                st_ = strips[h_ % 2]
                for ph in range(1):
                    for (ra, rb) in ((0, 8), (8, 16), (16, 24), (24, 31)):
                        src = AP(I["nat"].tensor, ((l * 4 + h_) * 31 + ra) * 128, [[1, 64], [128, rb - ra], [1, 64]])
                        P.dma("pool", st_[ph * 64:(ph + 1) * 64, ra * 64:rb * 64].rearrange("p (r c) -> p r c", c=64), src,
                              writes=[U("nstrip", h_ % 2)])
        for blk in range(2):
            for c in range(2):
                bk, bu = proj(s3v, c * 128, 128, blk, s3u)
                P.op("act", _mk("activation", out=nq[:, c, blk * BL:(blk + 1) * BL], in_=bk[:], func=AF.Identity, scale=0.125),
                     reads=[bu], writes=[U("nq")])
                bk, bu = proj(s3v, 256 + c * 128, 128, blk, s3u)
                P.op("dve", _mk("tensor_copy", out=nk[0:64, 2 * c, blk * BL:(blk + 1) * BL], in_=bk[0:64, :]),
                     reads=[bu], writes=[U("nk")])
                P.op("dve", _mk("tensor_copy", out=nk[64:128, 2 * c + 1, blk * BL:(blk + 1) * BL], in_=bk[64:128, :]),
                     reads=[bu], writes=[U("nk")])
        for tt in range(8):
            if lat:
                bk, bu = tok_proj([(s4v, 0, 256, s4u)], tt, U("h", tt // 4))
                voff = 0
            else:
                bk, bu = tok_proj([(s3v, 256, 256, s3u), (s4v, 0, 256, s4u)], tt, U("h", tt // 4))
                voff = 256
            P.op("dve", _mk("tensor_copy",
                out=vtn[:, tt, :, 0:64], in_=bk[:, voff:voff + 256].rearrange("p (h d) -> p h d", h=4)),
                reads=[bu], writes=[U("vtn")])
            if not lat:
                st, stu = nxt("tmp", tmpb)
                P.op("act", _mk("activation", out=st[:], in_=bk[:], func=AF.Copy), reads=[bu], writes=[stu])
                sq_, r0 = tt // 2, (tt % 2) * 128
                P.dma("sp", O["onk"][sq_, l, r0:r0 + 128, :], st[:, 0:256], reads=[stu])
                P.dma("sp", O["onv"][sq_, l, r0:r0 + 128, :], st[:, 256:512], reads=[stu])
        rd_n = [U("nq"), U("nk"), U("vtn")]
        if not lat:
            for h in range(4):
                hf, ch = h % 2, h // 2
                for sq_ in range(4):
                    q0 = sq_ * 256
                    kts = [dict(k=nk[:, h, q0 + kt * 128:q0 + (kt + 1) * 128], v=vtn[:, sq_ * 2 + kt, h, :]) for kt in range(2)]
                    attend(nq[:, ch, q0:q0 + 256], 256, kts, yna[hf * 64:(hf + 1) * 64, ch, q0:q0 + 256],
                           U("yna"), 1.0, rd_n)
        else:
            load_strip(0)
            for h in range(4):
                hf, ch = h % 2, h // 2
                if h + 1 < 4:
                    load_strip(h + 1)
                st_, stu_ = strips[h % 2], U("nstrip", h % 2)
                for u in range(2):
                    kts = []
                    for j in NA_TILES[u]:
                        jr = 7 - j
                        r_lo, r_hi = 2 * jr + 1 + 8 * u, 2 * jr + 8 * u
                        kts.append(dict(k=nk[:, h, j * 128:(j + 1) * 128], v=vtn[:, j, h, :],
                                        bias=[(J128[:], nbm[(u, j)][:], U("nbm")),
                                              (J2t[:, 0:128], st_[:, r_lo * 64:r_lo * 64 + 512], stu_),
                                              (J2t[:, 128:256], st_[:, r_hi * 64:r_hi * 64 + 512], stu_)],
                                        cols=NA_COLS[(u, j)]))
                    for ct in range(4):
                        kts.append(dict(k=nkc[:, h, ct * 128:(ct + 1) * 128], v=vtnc[:, ct, h, :]))
                    attend(nq[:, ch, u * BL:(u + 1) * BL], 512, kts, yna[hf * 64:(hf + 1) * 64, ch, u * BL:(u + 1) * BL],
                           U("yna"), 1.0, rd_n + [U("nkc"), U("vtnc"), U("J128")])
        A.release(m)

        if debug and "nqk" in DBG and l == 0 and lat:
            dbg_dump(DBG["nqk"], nq, 2, [U("nq")], 0)
        if stop == ("na", l, s):
            raise _Stop()
        m = A.mark()
        s5v, s5u = load_win(l, 2560, 160)
        NK = T + (PAST if lat else 0)
        qn = A.alloc("qn", [128, 2, T], BF16)
        qm = A.alloc("qm", [128, 4, T], BF16)
        km = A.alloc("km", [128, 4, NK], BF16)
        vtm = A.alloc("vtm", [128, NK // 128, 4, 128], BF16)
        ckvT = A.alloc("ckvT", [128, T], BF16)
        wuq = A.alloc("wuq", [128, 2, 384], BF16)
        wukv = A.alloc("wukv", [128, 512], BF16)
        mqf = A.alloc("mqf", [128, 2, BL], F32)
        P.op("dve", _mk("memset", vtm[:].rearrange("p a b c -> p (a b) c")[:, :, 64:128], 1.0), writes=[U("vtm")])
        P.dma("pool", wuq[:], I["mla_w_uq"][l].rearrange("(c p) n -> p c n", p=128), writes=[U("wuq")])
        P.dma("pool", wukv[:], I["mla_w_ukv"][l], writes=[U("wukv")])
        small_load(gq_mla[:], I["mla_g_q"][l].rearrange("(c p) -> p c", p=128), U("gmla"))
        small_load(gkv_col[:], I["mla_g_kv"][l].rearrange("(p o) -> p o", o=1), U("gmla"))
        small_load(gkv_bc[:], I["mla_g_kv"][l].partition_broadcast(128), U("gmla"))
        if lat:
            ropeC = A.alloc("ropeCm", [128, T], F32)
            ropeS = A.alloc("ropeSm", [128, T], F32)
            ropeP = A.alloc("ropePm", [128, 128], F32)
            P.dma("sp", ropeC[:], I["ropeCm"], writes=[U("ropeT")])
            P.dma("sp", ropeS[:], I["ropeSm"], writes=[U("ropeT")])
            P.dma("sp", ropeP[:], I["permm"], writes=[U("ropeP")])
            krt = A.alloc("krt", [128, BL], BF16)
            ckvcT = A.alloc("ckvcT", [128, PAST], BF16)
        for blk in range(2):
            tk = slice(blk * BL, (blk + 1) * BL)
            for c in range(2):
                bk, bu = proj(s4v, 256 + c * 128, 128, blk, s4u)
                P.op("act", _mk("activation", out=mqf[:, c, :], in_=bk[:], func=AF.Copy), reads=[bu], writes=[U("mqf")])
            norm_stats(lambda kc: mqf[:, kc, :], 2, blk, lambda kc: [U("mqf")], 1.0 / 256)
            for c in range(2):
                P.op("dve", _mk("scalar_tensor_tensor", out=qn[:, c, tk], in0=mqf[:, c, :], scalar=gq_mla[:, c:c + 1],
                                                                        in1=rstd[:], op0=ALU.mult, op1=ALU.mult),
                     reads=[U("mqf"), U("gmla"), U("rstd")], writes=[U("qn")])
            for h in range(4):
                bk, bu = bank()
                mm_group([(bk[0:96, :], wuq[:, c, h * 96:(h + 1) * 96], qn[:, c, tk], c == 0, c == 1) for c in range(2)],
                         reads=[U("wuq"), U("qn")], writes=[bu])
                if lat:
                    rope(bk, bu, 96, qm[0:96, h, tk], [U("qm")], blk, ropeC, ropeS, ropeP, 1.0)
                else:
                    P.op("act", _mk("activation", out=qm[0:96, h, tk], in_=bk[0:96, :], func=AF.Copy),
                         reads=[bu], writes=[U("qm")])
            bk, bu = proj(s5v, 0, 128, blk, s5u)
            tm, tu = nxt("tmp", tmpb)
            P.op("act", _mk("activation", out=tm[:], in_=bk[:], func=AF.Copy), reads=[bu], writes=[tu])
            norm_stats(lambda kc: tm[:], 1, blk, lambda kc: [tu], 1.0 / 128)
            P.op("dve", _mk("scalar_tensor_tensor", out=ckvT[:, tk], in0=tm[:], scalar=gkv_col[:, 0:1], in1=rstd[:],
                                                                      op0=ALU.mult, op1=ALU.mult),
                 reads=[tu, U("gmla"), U("rstd")], writes=[U("ckvT")])
            bk, bu = proj(s5v, 64, 96, blk, s5u)
            if lat:
                rope(bk, bu, 96, krt[0:96, :], [U("krt")], blk, ropeC, ropeS, ropeP, 1.0)
                for h in range(4):
                    P.op("dve", _mk("tensor_copy", out=km[64:96, h, tk], in_=krt[64:96, :]), reads=[U("krt")], writes=[U("km")])
            else:
                for h in range(4):
                    P.op("act", _mk("activation", out=km[64:96, h, tk], in_=bk[64:96, :], func=AF.Copy),
                         reads=[bu], writes=[U("km")])
            for h in range(4):
                bk, bu = bank()
                mm_group([(bk[0:64, :], wukv[:, h * 128:h * 128 + 64], ckvT[:, tk], True, True)], reads=[U("wukv"), U("ckvT")], writes=[bu])
                P.op("dve", _mk("tensor_copy", out=km[0:64, h, tk], in_=bk[0:64, :]), reads=[bu], writes=[U("km")])
        for tt in range(8):
            bk, bu = bank()
            mm_group([(bk[:], ckvT[:, tt * 128:(tt + 1) * 128], wukv[:], True, True)], reads=[U("wukv"), U("ckvT")], writes=[bu])
            P.op("dve", _mk("tensor_copy", out=vtm[:, tt, :, 0:64],
                                                             in_=bk[:].rearrange("p (h d) -> p h d", h=4)[:, :, 64:128]),
                 reads=[bu], writes=[U("vtm")])
            if not lat:
                bk, bu = tok_proj([(s5v, 0, 160, s5u)], tt, U("h", tt // 4))
                st, stu = nxt("tmp", tmpb)
                P.op("act", _mk("activation", out=st[:, 256:384], in_=bk[:, 0:128], func=AF.Square, accum_out=st[:, 400:401]),
                     reads=[bu], writes=[stu])
                P.op("act", _mk("activation", out=st[:, 401:402], in_=st[:, 400:401], func=AF.Ln, bias=eps_t[:, 0:1], scale=1.0 / 128),
                     reads=[stu, U("eps")], writes=[stu])
                P.op("act", _mk("activation", out=st[:, 402:403], in_=st[:, 401:402], func=AF.Exp, scale=-0.5), reads=[stu], writes=[stu])
                P.op("dve", _mk("scalar_tensor_tensor", out=st[:, 0:128], in0=bk[:, 0:128], scalar=st[:, 402:403], in1=gkv_bc[:],
                                                                          op0=ALU.mult, op1=ALU.mult), reads=[bu, stu, U("gmla")], writes=[stu])
                P.op("act", _mk("activation", out=st[:, 128:160], in_=bk[:, 128:160], func=AF.Copy), reads=[bu, stu], writes=[stu])
                sq_, r0 = tt // 2, (tt % 2) * 128
                P.dma("sp", O["ockv"][sq_, l, r0:r0 + 128, :], st[:, 0:128], reads=[stu])
                P.dma("sp", O["okr"][sq_, l, r0:r0 + 128, :], st[:, 128:160], reads=[stu])
        if lat:
            transpose_cache(I["cckv"][l], 128, lambda bk, bu, c: P.op(
                "act", _mk("activation", out=ckvcT[:], in_=bk[:], func=AF.Copy), reads=[bu], writes=[U("ckvcT")]), "kst_m")
            for h in range(4):
                bk, bu = bank()
                mm_group([(bk[0:64, :], wukv[:, h * 128:h * 128 + 64], ckvcT[:], True, True)], reads=[U("wukv"), U("ckvcT")], writes=[bu])
                P.op("dve", _mk("tensor_copy", out=km[0:64, h, T:T + PAST], in_=bk[0:64, :]), reads=[bu], writes=[U("km")])
            for ct in range(4):
                bk, bu = bank()
                mm_group([(bk[:], ckvcT[:, ct * 128:(ct + 1) * 128], wukv[:], True, True)], reads=[U("wukv"), U("ckvcT")], writes=[bu])
                P.op("dve", _mk("tensor_copy", out=vtm[:, 8 + ct, :, 0:64],
                                                                 in_=bk[:].rearrange("p (h d) -> p h d", h=4)[:, :, 64:128]),
                     reads=[bu], writes=[U("vtm")])
            krs = A.alloc("krs", [128, 4, 96], F32)
            P.op("dve", _mk("memset", krs[:].rearrange("p a b -> p (a b)"), 0.0), writes=[U("krs")])
            P.dma("sp", krs[:, :, 64:96], I["ckr"][l].rearrange("(t p) f -> p t f", p=128), writes=[U("krs")])
            bk, bu = bank()
            for ct in range(4):
                P.op("pe", _mk("transpose", bk[0:96, ct * 128:(ct + 1) * 128], krs[:, ct, :], identf[:]),
                     reads=[U("krs"), U("identf")], writes=[bu], mark=(ct == 3))
            for h in range(4):
                P.op("act", _mk("activation", out=km[64:96, h, T:T + PAST], in_=bk[64:96, :], func=AF.Copy),
                     reads=[bu], writes=[U("km")])
        rd_m = [U("qm"), U("km"), U("vtm")]
        for h in range(4):
            hf, ch = h % 2, h // 2
            if not lat:
                for sq_ in range(4):
                    q0 = sq_ * 256
                    kts = [dict(k=km[0:96, h, q0 + kt * 128:q0 + (kt + 1) * 128], v=vtm[:, sq_ * 2 + kt, h, :]) for kt in range(2)]
                    attend(qm[0:96, h, q0:q0 + 256], 256, kts, yml[hf * 64:(hf + 1) * 64, ch, q0:q0 + 256], U("yml"), MLA_SCALE, rd_m)
            else:
                for u in range(2):
                    kts = [dict(k=km[0:96, h, j * 128:(j + 1) * 128], v=vtm[:, j, h, :]) for j in range(12)]
                    attend(qm[0:96, h, u * BL:(u + 1) * BL], 512, kts, yml[hf * 64:(hf + 1) * 64, ch, u * BL:(u + 1) * BL],
                           U("yml"), MLA_SCALE, rd_m)
        A.release(m)

        if debug and ("y_%d_%d" % (l, s)) in DBG:
            dd = DBG["y_%d_%d" % (l, s)]
            for i, (yt, nch, un) in enumerate(((ycv, 2, "ycv"), (ygq, 4, "ygq"), (yna, 2, "yna"), (yml, 2, "yml"))):
                c0 = (0, 2, 6, 8)[i]
                dbg_dump(dd, yt, nch, [U(un)], c0)

        if stop == ("mla", l, s):
            raise _Stop()
        m = A.mark()
        acc = A.alloc("acc", [128, KC, T], F32)
        mrg = A.alloc("mrg", [128, KC, T], BF16)
        wbr = [A.alloc("wbr%d" % i, [128, 4096], BF16) for i in range(2)]
        branches = (("w_branch_conv", 2, ycv, "ycv"), ("w_branch_gqa", 4, ygq, "ygq"), ("w_branch_na", 2, yna, "yna"), ("w_branch_mla", 2, yml, "yml"))
        for b, (wn, kb_n, yb, yu) in enumerate(branches):
            wbt, wsu = wbr[b % 2], U("wbr", b % 2)
            wbv = wbt[:, 0:kb_n * D].rearrange("p (kc n) -> p kc n", kc=kb_n)
            wload(wbv, I[wn][l].rearrange("(kc p) n -> p kc n", p=128), wsu)
            for mg in range(2):
                gsv, gsu = load_win(l, 2720 + b * D + mg * 512, 512)
                for mi in range(4):
                    mo = mg * 4 + mi
                    for blk in range(2):
                        tk = slice(blk * BL, (blk + 1) * BL)
                        gb, gbu = proj(gsv, mi * 128, 128, blk, gsu)
                        sg, sgu = nxt("tmp", tmpb)
                        P.op("act", _mk("activation", out=sg[:], in_=gb[:], func=AF.Sigmoid), reads=[gbu], writes=[sgu])
                        bb, bbu = bank()
                        mm_group([(bb[:], wbv[:, kb, mo * 128:(mo + 1) * 128], yb[:, kb, tk], kb == 0, kb == kb_n - 1) for kb in range(kb_n)],
                                 reads=[wsu, U(yu)], writes=[bbu])
                        if b == 0:
                            P.op("dve", _mk("tensor_tensor", out=acc[:, mo, tk], in0=bb[:], in1=sg[:], op=ALU.mult),
                                 reads=[bbu, sgu], writes=[U("acc", mo, blk)])
                        else:
                            P.op("dve", _mk("tensor_tensor", out=sg[:], in0=bb[:], in1=sg[:], op=ALU.mult),
                                 reads=[bbu, sgu], writes=[sgu])
                            if b < 3:
                                P.op("dve", _mk("tensor_tensor", out=acc[:, mo, tk], in0=acc[:, mo, tk], in1=sg[:], op=ALU.add),
                                     reads=[sgu, U("acc", mo, blk)], writes=[U("acc", mo, blk)])
                            else:
                                P.op("dve", _mk("tensor_tensor", out=mrg[:, mo, tk], in0=acc[:, mo, tk], in1=sg[:], op=ALU.add),
                                     reads=[sgu, U("acc", mo, blk)], writes=[U("mrg", blk)])
        for j in range(2):
            slv, su = req_slot("w_o", l, D, j * 512, 512)
            for mi in range(4):
                mo = j * 4 + mi
                for blk in range(2):
                    tk = slice(blk * BL, (blk + 1) * BL)
                    bk, bu = bank()
                    mm_group([(bk[:], slv[:, kc, mi * 128:(mi + 1) * 128], mrg[:, kc, tk], kc == 0, kc == KC - 1) for kc in range(KC)],
                             reads=[su, U("mrg", blk)], writes=[bu])
                    P.op("dve", _mk("scalar_tensor_tensor",
                        out=xT[s][:, mo, tk], in0=bk[:], scalar=mod[:, 16 + mo, s:s + 1], in1=xT[s][:, mo, tk], op0=ALU.mult, op1=ALU.add),
                        reads=[bu, U("mod", lp), U("x", s, blk)], writes=[U("x", s, blk)])
        A.release(m_pass)

        if debug and ("xa_%d_%d" % (l, s)) in DBG:
            P.dma("sp", DBG["xa_%d_%d" % (l, s)], xT[s][:], reads=[U("x", s, 0), U("x", s, 1)])
        if stop == ("attn", l, s):
            raise _Stop()
        if l == 0 and not lat:
            load_x(1, I["xs"])
        norm_mod(s, A2, 24, mod, lp)
        m = A.mark()
        hid = A.alloc("hid", [128, 32, T], BF16)
        for j in range(8):
            slv, su = req_slot("w_ff1", l, D, j * 512, 512)
            for mi in range(4):
                hc = j * 4 + mi
                for blk in range(2):
                    tk = slice(blk * BL, (blk + 1) * BL)
                    bk, bu = proj(slv, mi * 128, 128, blk, su)
                    r_, ru = nxt("tmp", tmpb)
                    P.op("act", _mk("activation", out=r_[:], in_=bk[:], func=AF.Relu), reads=[bu], writes=[ru])
                    eng = "dve"
                    P.op(eng, _mk("tensor_tensor", out=hid[:, hc, tk], in0=r_[:], in1=r_[:], op=ALU.mult),
                         reads=[ru], writes=[U("hid", blk)])
        for mo in range(8):
            slv, su = req_slot("w_ff2", l, 4 * D, mo * 128, 128)
            for blk in range(2):
                tk = slice(blk * BL, (blk + 1) * BL)
                bk, bu = bank()
                mm_group([(bk[:], slv[:, kc, :], hid[:, kc, tk], kc == 0, kc == 31) for kc in range(32)], reads=[su, U("hid", blk)], writes=[bu])
                P.op("dve", _mk("scalar_tensor_tensor",
                    out=xT[s][:, mo, tk], in0=bk[:], scalar=mod[:, 40 + mo, s:s + 1], in1=xT[s][:, mo, tk], op0=ALU.mult, op1=ALU.add),
                    reads=[bu, U("mod", lp), U("x", s, blk)], writes=[U("x", s, blk)])
        A.release(m)
        if debug and ("x_%d_%d" % (l, s)) in DBG:
            P.dma("sp", DBG["x_%d_%d" % (l, s)], xT[s][:], reads=[U("x", s, 0), U("x", s, 1)])
        if stop == ("end", l, s):
            raise _Stop()

    def final_out(s, dst):
        m = A.mark()
        tf = A.alloc("tfin", [128, KC, BL], F32)
        ost = [A.alloc("ost%d" % i, [128, D], F32) for i in range(2)]
        for blk in range(2):
            norm_stats(lambda kc: xT[s][:, kc, blk * BL:(blk + 1) * BL], KC, blk, lambda kc: [U("x", s, blk)], 1.0 / D)
            for kc in range(KC):
                P.op("dve", _mk("scalar_tensor_tensor", out=tf[:, kc, :], in0=xT[s][:, kc, blk * BL:(blk + 1) * BL],
                                                                   scalar=gfin[:, kc:kc + 1], in1=rstd[:], op0=ALU.mult, op1=ALU.mult),
                     reads=[U("x", s, blk), U("gfin"), U("rstd")], writes=[U("tfin")])
            for ts in range(4):
                tt = blk * 4 + ts
                o_, ou = ost[tt % 2], U("ost", tt % 2)
                for half in range(2):
                    bk, bu = bank()
                    for q in range(4):
                        kc = half * 4 + q
                        P.op("pe", _mk("transpose", bk[:, q * 128:(q + 1) * 128], tf[:, kc, ts * 128:(ts + 1) * 128], identf[:]),
                             reads=[U("tfin"), U("identf")], writes=[bu], mark=(q == 3))
                    if half == 0:
                        P.op("act", _mk("activation", out=o_[:, 0:512], in_=bk[:], func=AF.Copy), reads=[bu], writes=[ou])
                    else:
                        P.op("dve", _mk("tensor_copy", out=o_[:, 512:1024], in_=bk[:]), reads=[bu], writes=[ou])
                P.dma("sp", dst[tt * 128:(tt + 1) * 128, :], o_[:], reads=[ou])
        A.release(m)

    try:
        if debug and "xin" in DBG:
            P.dma("sp", DBG["xin"], xT[0][:], reads=[U("x", 0, 0), U("x", 0, 1)])
        if stop == ("load", 0, 0):
            raise _Stop()
        for l in range(nlayers):
            layer_pass(l, 0)
            layer_pass(l, 1)
        final_out(0, O["yp"])
        final_out(1, O["ys"])
    except _Stop:
        pass
    P.finish()
    P.emit()
    return nc


def _na_tables():
    r = np.arange(16)
    r0 = np.clip(r - 4, 0, 8)
    c = np.arange(64)
    c0 = np.clip(c - 8, 0, 48)
    rowok = (r[None, :] >= r0[:, None]) & (r[None, :] < r0[:, None] + 8)
    colok = (c[None, :] >= c0[:, None]) & (c[None, :] < c0[:, None] + 16)
    ok = rowok[:, None, :, None] & colok[None, :, None, :]
    ok = ok.reshape(1024, 1024)
    maskT = np.where(ok.T, 0.0, NEG).astype(np.float32)
    mask_rev = np.ascontiguousarray(maskT[::-1, :])
    tiles = []
    cols = {}
    for u in range(2):
        tl = [j for j in range(8) if ok[u * 512:(u + 1) * 512, j * 128:(j + 1) * 128].any()]
        tiles.append(tl)
        for j in tl:
            v = np.where(ok[u * 512:(u + 1) * 512, j * 128:(j + 1) * 128].any(1))[0]
            cols[(u, j)] = (int(v.min()) // 64 * 64, (int(v.max()) // 64 + 1) * 64)
    return mask_rev, tiles, cols


NA_MASK_REV, NA_TILES, NA_COLS = _na_tables()


def _gmask():
    g = np.zeros((6, 128, 512), np.float32)
    kk = np.arange(128)[:, None]
    qq = np.arange(512)[None, :]
    for d in range(-1, 5):
        g[d + 1] = np.where(np.abs(qq - 128 * d - kk) <= 128, 0.0, NEG)
    return g


_PROG = {}


def kernel(x_prompt, x_sample, cache_gqa_k, cache_gqa_v, cache_na_k, cache_na_v, cache_mla_ckv,
           cache_mla_krope, c, c_ctx, w_mod, b_mod, g_attn, g_mlp, w_in, w_conv, gqa_sink, na_rpb,
           mla_g_q, mla_w_uq, mla_g_kv, mla_w_ukv, w_branch_conv, w_branch_gqa, w_branch_na,
           w_branch_mla, w_o, w_ff1, w_ff2, g_final, _debug=None, _nlayers=DEPTH):
    f = lambda a: np.ascontiguousarray(np.asarray(a, dtype=np.float32))
    consts = _consts()
    consts["gmask"] = _gmask()
    consts["namask"] = NA_MASK_REV
    J = np.zeros((128, 128), np.float32)
    J[np.arange(128), 127 - np.arange(128)] = 1.0
    consts["J128"] = J
    J2 = np.zeros((128, 128), np.float32)
    for p_ in range(64):
        J2[p_, 63 - p_] = 1.0
        J2[64 + p_, 127 - p_] = 1.0
    consts["J2"] = J2
    rpb = f(na_rpb)
    nat = np.zeros((DEPTH, 4, 31, 128), np.float32)
    nat[:, :, 8:23, 48:79] = rpb[:, :, ::-1, ::-1]
    shared = {
        "w_mod": f(w_mod), "b_mod": f(b_mod), "g_attn": f(g_attn), "g_mlp": f(g_mlp), "w_in": f(w_in),
        "w_conv": f(w_conv), "gqa_sink": f(gqa_sink), "nat": nat, "mla_g_q": f(mla_g_q),
        "mla_w_uq": f(mla_w_uq), "mla_g_kv": f(mla_g_kv), "mla_w_ukv": f(mla_w_ukv),
        "w_branch_conv": f(w_branch_conv), "w_branch_gqa": f(w_branch_gqa), "w_branch_na": f(w_branch_na),
        "w_branch_mla": f(w_branch_mla), "w_o": f(w_o), "w_ff1": f(w_ff1), "w_ff2": f(w_ff2), "g_final": f(g_final),
    }
    import ml_dtypes
    for k_, v_ in consts.items():
        shared["c_" + k_] = np.ascontiguousarray(v_.astype(ml_dtypes.bfloat16)) if k_ in CONST_BF16 else f(v_)
    xp, xs = f(x_prompt), f(x_sample)
    cg = {"cgk": f(cache_gqa_k).reshape(8, DEPTH, PAST, 128), "cgv": f(cache_gqa_v).reshape(8, DEPTH, PAST, 128),
          "cnk": f(cache_na_k).reshape(8, DEPTH, PAST, 256), "cnv": f(cache_na_v).reshape(8, DEPTH, PAST, 256),
          "cckv": f(cache_mla_ckv), "ckr": f(cache_mla_krope)}
    cc, cx = f(c), f(c_ctx)
    in_maps = []
    for i in range(NCORES):
        d = dict(shared)
        d["xp"] = np.ascontiguousarray(xp[4 * i:4 * i + 4].reshape(T, D))
        d["xs"] = np.ascontiguousarray(xs[i])
        for k_, v_ in cg.items():
            d[k_] = np.ascontiguousarray(v_[i])
        d["cvec"] = np.ascontiguousarray(np.stack([cx, cc[i]]))
        in_maps.append(d)
    if _debug == "prep":
        return in_maps
    key = (repr(_debug), _nlayers)
    if key not in _PROG:
        po = []
        build(debug=_debug, nlayers=_nlayers, plan_out=po)
        _PROG[key] = build(debug=_debug, nlayers=_nlayers, plan=po)
    res = run_bass_kernel_spmd(_PROG[key], in_maps, core_ids=list(range(NCORES)))
    R = res.results
    y_prompt = np.concatenate([R[i]["yp"].reshape(4, 256, D) for i in range(NCORES)], axis=0)
    y_sample = np.stack([R[i]["ys"] for i in range(NCORES)], axis=0)
    cat = lambda n, shp: np.concatenate([R[i][n] for i in range(NCORES)], axis=0).reshape(shp)
    outs = (y_prompt.astype(np.float32), y_sample.astype(np.float32),
            cat("ogk", (32, DEPTH, 256, 2, 64)), cat("ogv", (32, DEPTH, 256, 2, 64)),
            cat("onk", (32, DEPTH, 256, 4, 64)), cat("onv", (32, DEPTH, 256, 4, 64)),
            cat("ockv", (32, DEPTH, 256, 128)), cat("okr", (32, DEPTH, 256, 32)))
    if _debug:
        kernel.dbg = [{n: R[i]["dbg_" + n] for n in _debug} for i in range(NCORES)]
    return outs
```

```python
import os
import numpy as np
import concourse.bass as bass
import concourse.mybir as mybir
from concourse.ap import AP
from concourse.bass_utils import run_bass_kernel_spmd

F32 = mybir.dt.float32
F32R = mybir.dt.float32r
BF16 = mybir.dt.bfloat16
AF = mybir.ActivationFunctionType
ALU = mybir.AluOpType

ENGS = ("pe", "act", "dve", "pool", "sp")
NCORES = 8
D = 1024
KC = 8
T = 1024
BL = 512
DEPTH = 2
PAST = 512
DIN = 6816
EPS = 1e-6
NEG = -30000.0
MLA_SCALE = 96 ** -0.5


class Unit:
    __slots__ = ("name", "w", "rs")

    def __init__(self, name):
        self.name = name
        self.w = {}
        self.rs = {}


class Prog:
    NRING = 32

    def __init__(self, nc):
        self.nc = nc
        self.streams = {e: [] for e in ENGS}
        self.cnt = {e: 0 for e in ENGS}
        self.sems = {e: nc.alloc_semaphore("sem_" + e) for e in ENGS}
        self.ring = [nc.alloc_semaphore("dq%d" % i) for i in range(self.NRING)]
        self.ring_val = [0] * self.NRING
        self.ring_rng = {"sp": (0, self.NRING // 2), "act": (0, self.NRING // 2), "pool": (self.NRING // 2, self.NRING)}
        self.ring_pos = {"sp": 0, "act": 0, "pool": 0}
        self.seen = {e: {} for e in ENGS}
        self.units = {}
        self.fence = []

    def U(self, *key):
        u = self.units.get(key)
        if u is None:
            u = Unit(key)
            self.units[key] = u
        return u

    def _sem(self, key):
        return self.sems[key] if isinstance(key, str) else self.ring[key]

    def _deps(self, eng, reads, writes):
        evs = []
        for u in reads:
            for w in u.w.values():
                if not (w[2] == eng and eng == "pe"):
                    evs.append(w)
            if u.name[0] == "bank":
                for r in u.rs.values():
                    if r[2] != eng:
                        evs.append(r)
        for u in writes:
            if u.name[0] not in self.PERSIST:
                for f in self.fence:
                    if f[2] != eng:
                        evs.append(f)
            for w in u.w.values():
                if w[2] != eng:
                    evs.append(w)
            for r in u.rs.values():
                if r[2] != eng:
                    evs.append(r)
        need = {}
        for (k, v, e) in evs:
            if self.seen[eng].get(k, 0) >= v:
                continue
            if need.get(k, 0) < v:
                need[k] = v
        for k, v in need.items():
            self.seen[eng][k] = v
        return list(need.items())

    def op(self, eng, fn, reads=(), writes=(), mark=True):
        waits = self._deps(eng, reads, writes)
        self.streams[eng].append((waits, fn, [(eng, 1)] if mark else []))
        if mark:
            self.cnt[eng] += 1
            ev = (eng, self.cnt[eng], eng)
            for u in reads:
                u.rs[eng] = ev
            for u in writes:
                u.w = {eng: ev}
                u.rs = {}

    def dma(self, q, out_ap, in_ap, reads=(), writes=(), **kw):
        waits = self._deps(q, reads, writes)
        lo, hi = self.ring_rng[q]
        key_ = "pool" if q == "pool" else "sp"
        i = lo + self.ring_pos[key_] % (hi - lo)
        self.ring_pos[key_] += 1
        prev = self.ring_val[i]
        if prev > 0 and self.seen[q].get(i, 0) < prev:
            waits.append((i, prev))
            self.seen[q][i] = prev
        self.ring_val[i] = prev + 16
        ev = (i, prev + 16, "dma")

        def fn(engine, out_ap=out_ap, in_ap=in_ap, kw=kw):
            return engine.dma_start(out=out_ap, in_=in_ap, **kw)

        self.streams[q].append((waits, fn, [(i, 16)]))
        for u in reads:
            u.rs[("d", i)] = ev
        for u in writes:
            u.w[("d", i)] = ev
            u.rs = {}

    PERSIST = frozenset(("identf", "onesf", "identb", "gT", "gfin", "cvf", "scb", "bmodT", "mod", "A12", "wconvT",
                         "esk", "gmla", "rstd", "eps", "x", "h", "slot", "bank", "sq", "tmp", "pT"))

    def set_fence(self):
        ev = [(o, self.cnt[o], o) for o in ENGS if self.cnt[o] > 0]
        ev += [(i, self.ring_val[i], "dma") for i in range(self.NRING) if self.ring_val[i] > 0]
        self.fence = ev

    def barrier(self):
        for e in ENGS:
            waits = []
            for o in ENGS:
                if o != e and self.cnt[o] > 0 and self.seen[e].get(o, 0) < self.cnt[o]:
                    waits.append((o, self.cnt[o]))
                    self.seen[e][o] = self.cnt[o]
            for i in range(self.NRING):
                v = self.ring_val[i]
                if v > 0 and self.seen[e].get(i, 0) < v:
                    waits.append((i, v))
                    self.seen[e][i] = v
            if waits:
                self.streams[e].append((waits, None, []))

    def finish(self):
        for i in range(self.NRING):
            v = self.ring_val[i]
            if v > 0 and self.seen["sp"].get(i, 0) < v:
                self.streams["sp"].append(([(i, v)], None, []))
                self.seen["sp"][i] = v
        for e in ENGS:
            if e != "sp" and self.cnt[e] > 0:
                self.streams["sp"].append(([(e, self.cnt[e])], None, []))

    def emit(self):
        nc = self.nc
        engobj = {"pe": "tensor", "act": "scalar", "dve": "vector", "pool": "gpsimd", "sp": "sync"}
        with nc.Block() as block:
            for e in ENGS:
                stream = self.streams[e]

                def body(engine, stream=stream):
                    for waits, fn, incs in stream:
                        for k, v in waits:
                            engine.wait_ge(self._sem(k), v)
                        if fn is None:
                            continue
                        ins = fn(engine)
                        for k, n in incs:
                            ins.then_inc(self._sem(k), n)

                getattr(block, engobj[e])(body)


def _mk(name, *a, **k):
    return lambda e: getattr(e, name)(*a, **k)


class Arena:
    def __init__(self, nc, nbytes):
        self.nc = nc
        self.start, self.end = nc.bump_sbuf(nbytes)
        self.cur = self.start
        self.n = 0
        self.on_release = None

    def alloc(self, name, shape, dt):
        nb = int(np.prod(shape[1:])) * (4 if dt == F32 else 2)
        off = (self.cur + 63) // 64 * 64
        assert off + nb <= self.end, ("SBUF arena overflow", name, off + nb - self.end)
        self.cur = off + nb
        self.n += 1
        return self.nc.alloc_sbuf_tensor_at("%s_%d" % (name, self.n), list(shape), dt, offset=off)

    def mark(self):
        return self.cur

    def release(self, m):
        self.cur = m
        if self.on_release is not None:
            self.on_release()


def _consts():
    c = {}
    c["identf"] = np.eye(128, dtype=np.float32)
    c["onesf"] = np.ones((128, 128), np.float32)
    t = np.arange(T)
    rows = (t // 64).astype(np.float32)
    cols = (t % 64).astype(np.float32)
    Cg = np.zeros((128, T), np.float32)
    Sg = np.zeros((128, T), np.float32)
    Pg = np.zeros((128, 128), np.float32)
    for p in range(128):
        d = p % 64
        b = d // 32
        i = d % 32
        inv = 10000.0 ** (-(i % 16) / 16.0)
        pos = rows if b == 0 else cols
        ang = (pos * np.float32(inv)).astype(np.float32)
        Cg[p] = np.cos(ang)
        Sg[p] = np.sin(ang) * (-1.0 if i < 16 else 1.0)
        partner = p + 16 if i < 16 else p - 16
        Pg[partner, p] = 1.0
    c["ropeCg"], c["ropeSg"], c["permg"] = Cg, Sg, Pg
    Cm = np.ones((128, T), np.float32)
    Sm = np.zeros((128, T), np.float32)
    Pm = np.zeros((128, 128), np.float32)
    for p in range(64):
        Pm[p, p] = 1.0
    for p in range(64, 96):
        d = p - 64
        b = d // 16
        i = d % 16
        inv = 10000.0 ** (-(i % 8) / 8.0)
        pos = rows if b == 0 else cols
        ang = (pos * np.float32(inv)).astype(np.float32)
        Cm[p] = np.cos(ang)
        Sm[p] = np.sin(ang) * (-1.0 if i < 8 else 1.0)
        partner = p + 8 if i < 8 else p - 8
        Pm[partner, p] = 1.0
    c["ropeCm"], c["ropeSm"], c["permm"] = Cm, Sm, Pm
    kk = np.arange(128)[:, None]
    qq = np.arange(128)[None, :]
    mprev = np.where(kk >= qq, 0.0, NEG).astype(np.float32)
    mnext = np.where(kk <= qq, 0.0, NEG).astype(np.float32)
    c["mprev"] = np.tile(mprev, (1, 4))
    c["mnext"] = np.tile(mnext, (1, 4))
    k = np.arange(64)[:, None]
    cq = np.arange(64)[None, :]
    cp = 63 - k
    c0 = np.clip(cq - 8, 0, 48)
    ok = (cp >= c0) & (cp < c0 + 16)
    m01 = np.zeros((128, 64), np.float32)
    mneg = np.zeros((128, 64), np.float32)
    m01[:64] = ok
    mneg[:64] = np.where(ok, 0.0, NEG)
    c["na_m01"], c["na_mneg"] = m01, mneg
    J = np.zeros((128, 256), np.float32)
    for kk_ in range(64):
        J[kk_, 63 - kk_] = 1.0
        J[kk_, 128 + 127 - kk_] = 1.0
    c["na_J"] = J
    c["negt"] = np.full((128, 512), NEG, np.float32)
    return c


CONST_SHAPES = {"identf": (128, 128), "onesf": (128, 128), "ropeCg": (128, T), "ropeSg": (128, T),
                "permg": (128, 128), "ropeCm": (128, T), "ropeSm": (128, T), "permm": (128, 128),
                "mprev": (128, 512), "mnext": (128, 512), "na_m01": (128, 64), "na_mneg": (128, 64),
                "na_J": (128, 256), "negt": (128, 512),
                "gmask": (6, 128, 512), "namask": (1024, 1024), "J128": (128, 128), "J2": (128, 128)}

CONST_BF16 = ("namask", "gmask", "J128", "na_J")

IN_SHAPES = {
    "xp": (T, D), "xs": (T, D),
    "cgk": (DEPTH, PAST, 128), "cgv": (DEPTH, PAST, 128), "cnk": (DEPTH, PAST, 256), "cnv": (DEPTH, PAST, 256),
    "cckv": (DEPTH, PAST, 128), "ckr": (DEPTH, PAST, 32), "cvec": (2, D),
    "w_mod": (DEPTH, D, 6 * D), "b_mod": (DEPTH, 6 * D), "g_attn": (DEPTH, D), "g_mlp": (DEPTH, D),
    "w_in": (DEPTH, D, DIN), "w_conv": (DEPTH, 3, 256), "gqa_sink": (DEPTH, 8), "nat": (DEPTH, 4, 31, 128),
    "mla_g_q": (DEPTH, 256), "mla_w_uq": (DEPTH, 256, 384), "mla_g_kv": (DEPTH, 128),
    "mla_w_ukv": (DEPTH, 128, 512), "w_branch_conv": (DEPTH, 256, D), "w_branch_gqa": (DEPTH, 512, D),
    "w_branch_na": (DEPTH, 256, D), "w_branch_mla": (DEPTH, 256, D), "w_o": (DEPTH, D, D),
    "w_ff1": (DEPTH, D, 4 * D), "w_ff2": (DEPTH, 4 * D, D), "g_final": (D,),
}
OUT_SHAPES = {
    "yp": (T, D), "ys": (T, D), "ogk": (4, DEPTH, 256, 128), "ogv": (4, DEPTH, 256, 128),
    "onk": (4, DEPTH, 256, 256), "onv": (4, DEPTH, 256, 256), "ockv": (4, DEPTH, 256, 128),
    "okr": (4, DEPTH, 256, 32),
}


class _Stop(Exception):
    pass


PREF = 1


def build(debug=None, nlayers=DEPTH, stop=None, plan=None, plan_out=None):
    nc = bass.Bass("TRN2", target_bir_lowering=False)
    P = Prog(nc)
    U = P.U
    I = {n: nc.dram_tensor(n, list(s), F32, kind="ExternalInput").ap() for n, s in IN_SHAPES.items()}
    for n, s in CONST_SHAPES.items():
        I[n] = nc.dram_tensor("c_" + n, list(s), BF16 if n in CONST_BF16 else F32, kind="ExternalInput").ap()
    O = {n: nc.dram_tensor(n, list(s), F32, kind="ExternalOutput").ap() for n, s in OUT_SHAPES.items()}
    DBG = {}
    if debug:
        for n, s in debug.items():
            DBG[n] = nc.dram_tensor("dbg_" + n, list(s), F32, kind="ExternalOutput").ap()

    A = Arena(nc, min(212000, nc.sbuf_bytes_remaining - 512))
    A.on_release = P.set_fence
    banks = [nc.alloc_psum_tensor("bank%d" % i, [128, 512], F32) for i in range(8)]
    bank_i = [0]

    def bank():
        i = bank_i[0]
        bank_i[0] = (i + 1) % 5
        return banks[i], U("bank", i)

    xT = [A.alloc("xT%d" % s, [128, KC, T], F32) for s in range(2)]
    hT = A.alloc("hT", [128, KC, T], BF16)
    NSLOT = 3
    slots = [A.alloc("slot%d" % i, [128, 4096], BF16) for i in range(NSLOT)]
    slot_i = [0]
    identf = A.alloc("identf", [128, 128], F32)
    onesf = A.alloc("onesf", [128, 128], F32)
    identb = A.alloc("identb", [128, 128], BF16)
    gT = A.alloc("gT", [128, 2, DEPTH, KC], F32)
    gfin = A.alloc("gfin", [128, KC], F32)
    cvf = A.alloc("cvf", [128, KC, 2], F32)
    scb = A.alloc("scb", [128, KC, 2], BF16)
    bmodL = [A.alloc("bmodT%d" % i, [128, 48], F32) for i in range(2)]
    modL = [A.alloc("mod%d" % i, [128, 48, 2], F32) for i in range(2)]
    A1L = [A.alloc("A1_%d" % i, [128, KC, 2], F32) for i in range(2)]
    A2L = [A.alloc("A2_%d" % i, [128, KC, 2], F32) for i in range(2)]
    wconvT = A.alloc("wconvT", [128, 2, 3], F32)
    esk = A.alloc("esk", [128, 8], F32)
    gq_mla = A.alloc("gq_mla", [128, 2], F32)
    gkv_col = A.alloc("gkv_col", [128, 1], F32)
    gkv_bc = A.alloc("gkv_bc", [128, 128], F32)
    tmpb = [A.alloc("tmpb%d" % i, [128, BL], F32) for i in range(4)]
    rstd = A.alloc("rstd", [128, BL], F32)
    eps_t = A.alloc("eps_t", [128, 1], F32)
    P.op("dve", _mk("memset", eps_t[:], EPS), writes=[U("eps")])
    pTb = [A.alloc("pT%d" % i, [128, BL], BF16) for i in range(4)]
    rot = {"sq": 0, "tmp": 0, "pT": 0}

    def nxt(kind, lst):
        i = rot[kind]
        rot[kind] = (i + 1) % len(lst)
        return lst[i], U(kind, i)

    def slot():
        i = slot_i[0]
        slot_i[0] = (i + 1) % NSLOT
        return slots[i], U("slot", i)

    def wload(dst_ap, src_ap, u):
        P.dma("pool", dst_ap, src_ap, writes=[u])

    req_n = [0]
    issued = [0]

    def _issue(n):
        name, l_, nrows, c0, ncols = plan[n] if plan is not None else plan_out[n]
        i = n % NSLOT
        sl, su = slots[i], U("slot", i)
        kcn = nrows // 128
        slv = sl[:, 0:kcn * ncols].rearrange("p (kc n) -> p kc n", kc=kcn)
        for k0 in range(0, kcn, 8):
            k1 = min(kcn, k0 + 8)
            wload(slv[:, k0:k1, :], I[name][l_][k0 * 128:k1 * 128, c0:c0 + ncols].rearrange("(kc p) n -> p kc n", p=128), su)

    def req_slot(name, l_, nrows, c0, ncols):
        n = req_n[0]
        req_n[0] += 1
        if plan is None:
            plan_out.append((name, l_, nrows, c0, ncols))
        else:
            assert plan[n] == (name, l_, nrows, c0, ncols), (n, plan[n], name, l_, nrows, c0, ncols)
        while issued[0] <= min(n + PREF, (len(plan) - 1) if plan is not None else n):
            _issue(issued[0])
            issued[0] += 1
        i = n % NSLOT
        kcn = nrows // 128
        return slots[i][:, 0:kcn * ncols].rearrange("p (kc n) -> p kc n", kc=kcn), U("slot", i)

    def small_load(dst_ap, src_ap, u, q="sp"):
        P.dma(q, dst_ap, src_ap, writes=[u], allow_slow_non_contiguous=True)

    small_load(identf[:], I["identf"], U("identf"))
    small_load(onesf[:], I["onesf"], U("onesf"))
    P.op("dve", _mk("tensor_copy", out=identb[:], in_=identf[:]), reads=[U("identf")], writes=[U("identb")])
    for j, nm in enumerate(("g_attn", "g_mlp")):
        for l in range(DEPTH):
            small_load(gT[:, j, l, :], I[nm][l].rearrange("(kc p) -> p kc", p=128), U("gT"))
    small_load(gfin[:], I["g_final"].rearrange("(kc p) -> p kc", p=128), U("gfin"))
    for s in range(2):
        small_load(cvf[:, :, s], I["cvec"][s].rearrange("(kc p) -> p kc", p=128), U("cvf"))
    P.op("act", _mk("activation", out=scb[:], in_=cvf[:], func=AF.Silu), reads=[U("cvf")], writes=[U("scb")])

    def mm_group(mms, reads, writes):
        n = len(mms)
        for i, (o, l, r, st, sp) in enumerate(mms):
            f = (_mk("matmul", o, l, r, start=st, stop=sp))
            if i == n - 1:
                P.op("pe", f, reads=reads, writes=writes, mark=True)
            elif i == 0:
                P.op("pe", f, reads=reads, writes=writes, mark=False)
            else:
                P.op("pe", f, mark=False)

    def mm_first_wait(reads, writes):
        pass

    def load_x(s, src):
        m = A.mark()
        stage = [A.alloc("xstage%d" % i, [128, D], F32) for i in range(2)]
        for tt in range(8):
            st, su = stage[tt % 2], U("xstage", tt % 2)
            P.dma("sp", st[:], src[tt * 128:(tt + 1) * 128, :], writes=[su])
            for half in range(2):
                bk, bu = bank()
                for q in range(4):
                    kc = half * 4 + q
                    P.op("pe", _mk("transpose",
                        bk[:, q * 128:(q + 1) * 128], st[:, kc * 128:(kc + 1) * 128], identf[:]),
                        reads=[su, U("identf")], writes=[bu], mark=(q == 3))
                eng = "act" if half == 0 else "dve"
                dst = xT[s][:, half * 4:half * 4 + 4, tt * 128:(tt + 1) * 128]
                srcp = bk[:].rearrange("p (a b) -> p a b", a=4)
                if eng == "act":
                    P.op("act", _mk("activation", out=dst, in_=srcp, func=AF.Copy),
                         reads=[bu], writes=[U("x", s, tt // 4)])
                else:
                    P.op("dve", _mk("tensor_copy", out=dst, in_=srcp),
                         reads=[bu], writes=[U("x", s, tt // 4)])
        A.release(m)

    load_x(0, I["xp"])

    def adaln_piece(l, j):
        lp = l % 2
        mod, bmodT = modL[lp], bmodL[lp]
        if j == 0:
            for j6 in range(6):
                small_load(bmodT[:, j6 * 8:(j6 + 1) * 8], I["b_mod"][l][j6 * 1024:(j6 + 1) * 1024].rearrange("(c p) -> p c", p=128),
                           U("bmodT", lp))
        slv, su = req_slot("w_mod", l, D, j * 512, 512)
        bk, bu = bank()
        for n in range(4):
            mm_group([(bk[:, n * 2:n * 2 + 2], slv[:, kc, n * 128:(n + 1) * 128], scb[:, kc, :], kc == 0, kc == KC - 1)
                      for kc in range(KC)], reads=[su, U("scb")], writes=[bu])
        P.op("dve", _mk("tensor_tensor", out=mod[:, j * 4:j * 4 + 4, :], in0=bk[:, 0:8].rearrange("p (c s) -> p c s", s=2),
                        in1=bmodT[:, j * 4:j * 4 + 4].unsqueeze(2).broadcast_to([128, 4, 2]), op=ALU.add),
             reads=[bu, U("bmodT", lp)], writes=[U("mod", lp)])
        for (Ax, jj, c0, jdone) in ((A1L[lp], 0, 8, 3), (A2L[lp], 1, 32, 9)):
            if j == jdone:
                P.op("dve", _mk("scalar_tensor_tensor",
                    out=Ax[:], in0=mod[:, c0:c0 + 8, :], scalar=1.0,
                    in1=gT[:, jj, l, :].unsqueeze(2).broadcast_to([128, KC, 2]), op0=ALU.add, op1=ALU.mult),
                    reads=[U("mod", lp), U("gT")], writes=[U("A12", lp)])

    ncall = [0]

    def norm_stats(src_fn, nchunks, blk, src_units, scale):
        bk, bu = bank()
        for kc in range(nchunks):
            sq, squ = nxt("tmp", tmpb)
            src = src_fn(kc)
            P.op("act", _mk("activation", out=sq[:], in_=src, func=AF.Square),
                 reads=src_units(kc), writes=[squ])
            P.op("pe", _mk("matmul", bk[:], onesf[:], sq[:], start=(kc == 0), stop=(kc == nchunks - 1)),
                 reads=[squ, U("onesf")], writes=[bu], mark=True)
        P.op("act", _mk("activation", out=rstd[:], in_=bk[:], func=AF.Ln, bias=eps_t[:, 0:1], scale=scale),
             reads=[bu, U("eps")], writes=[U("rstd")])
        P.op("act", _mk("activation", out=rstd[:], in_=rstd[:], func=AF.Exp, scale=-0.5), reads=[U("rstd")], writes=[U("rstd")])
        if debug and ("rs%d" % ncall[0]) in DBG:
            P.dma("sp", DBG["rs%d" % ncall[0]], rstd[:], reads=[U("rstd")])
        ncall[0] += 1

    def norm_mod(s, Ax, bcol0, mod, lp):
        for blk in range(2):
            norm_stats(lambda kc: xT[s][:, kc, blk * BL:(blk + 1) * BL], KC, blk, lambda kc: [U("x", s, blk)], 1.0 / D)
            for kc in range(KC):
                tm, tu = nxt("tmp", tmpb)
                P.op("dve", _mk("scalar_tensor_tensor",
                    out=tm[:], in0=xT[s][:, kc, blk * BL:(blk + 1) * BL], scalar=Ax[:, kc, s:s + 1], in1=rstd[:],
                    op0=ALU.mult, op1=ALU.mult), reads=[U("x", s, blk), U("A12", lp), U("rstd")], writes=[tu])
                P.op("act", _mk("activation",
                    out=hT[:, kc, blk * BL:(blk + 1) * BL], in_=tm[:], func=AF.Identity,
                    bias=mod[:, bcol0 + kc, s:s + 1], scale=1.0), reads=[tu, U("mod", lp)], writes=[U("h", blk)])

    def proj(slv, c0, ncols, blk, su, out_rows=None):
        bk, bu = bank()
        mm_group([(bk[0:ncols, :], slv[:, kc, c0:c0 + ncols], hT[:, kc, blk * BL:(blk + 1) * BL], kc == 0, kc == KC - 1)
                  for kc in range(KC)], reads=[su, U("h", blk)], writes=[bu])
        return bk, bu

    def load_win(l, c0, ncols):
        return req_slot("w_in", l, D, c0, ncols)

    def rope(bk, bu, rows, dst, dst_units, blk, Ct, St, Pt, pre_scale):
        xs, xsu = nxt("tmp", tmpb)
        P.op("act", _mk("activation", out=xs[0:rows, :], in_=bk[0:rows, :], func=AF.Identity, scale=pre_scale),
             reads=[bu], writes=[xsu])
        b2, b2u = bank()
        P.op("pe", _mk("matmul", b2[0:rows, :], Pt[0:rows, 0:rows], xs[0:rows, :], start=True, stop=True),
             reads=[xsu, U("ropeP")], writes=[b2u])
        t2, t2u = nxt("tmp", tmpb)
        P.op("dve", _mk("tensor_tensor", out=t2[0:rows, :], in0=b2[0:rows, :], in1=St[0:rows, blk * BL:(blk + 1) * BL],
                                              op=ALU.mult), reads=[b2u, U("ropeT")], writes=[t2u])
        P.op("dve", _mk("tensor_tensor", out=xs[0:rows, :], in0=xs[0:rows, :], in1=Ct[0:rows, blk * BL:(blk + 1) * BL],
                                               op=ALU.mult), reads=[xsu, U("ropeT")], writes=[xsu])
        P.op("dve", _mk("tensor_tensor", out=dst, in0=xs[0:rows, :], in1=t2[0:rows, :], op=ALU.add),
             reads=[xsu, t2u], writes=dst_units)

    obank_i = [0]
    pend_fin = [None]

    def flush_att():
        if pend_fin[0] is not None:
            pend_fin[0]()
            pend_fin[0] = None
    att_dbg = [0]

    def obank():
        i = 5 + obank_i[0]
        obank_i[0] = (obank_i[0] + 1) % 3
        return banks[i], U("bank", i)

    def attend(q_rhs, N, ktiles, dst, dst_u, scale, rds, sink=None, defer=False):
        ob, obu = obank()
        n = len(ktiles)
        pend = []

        def pv(p):
            i, pt, ptu, v, c0, c1 = p
            P.op("pe", _mk("matmul", ob[:, c0:c1], v, pt[:, c0:c1], start=(i == 0), stop=(i == n - 1)),
                 reads=[ptu] + rds, writes=[obu], mark=True)

        for i, kt in enumerate(ktiles):
            bk, bu = bank()
            bias = kt.get("bias", [])
            c0, c1 = kt.get("cols", (0, N))
            mms = [(bk[:, c0:c1], kt["k"], q_rhs[:, c0:c1], True, len(bias) == 0)]
            r2 = list(rds)
            for bi, (sel, bt, btu) in enumerate(bias):
                mms.append((bk[:, c0:c1], sel, bt[:, c0:c1], False, bi == len(bias) - 1))
                r2.append(btu)
            mm_group(mms, reads=r2, writes=[bu])
            pt, ptu = nxt("pT", pTb)
            P.op("act", _mk("activation", out=pt[:, c0:c1], in_=bk[:, c0:c1], func=AF.Exp, scale=scale),
                 reads=[bu], writes=[ptu])
            if debug and "att_pt" in DBG and att_dbg[0] == 1 and i < 6:
                tmd, tmdu = nxt("tmp", tmpb)
                P.op("dve", _mk("tensor_copy", out=tmd[:], in_=pt[:]), reads=[ptu], writes=[tmdu])
                P.dma("sp", DBG["att_pt"][:, i, :], tmd[:], reads=[tmdu])
                tmd, tmdu = nxt("tmp", tmpb)
                P.op("act", _mk("activation", out=tmd[:], in_=bk[:], func=AF.Copy), reads=[bu], writes=[tmdu])
                P.dma("sp", DBG["att_st"][:, i, :], tmd[:], reads=[tmdu])
            pend.append((i, pt, ptu, kt["v"], c0, c1))
            if len(pend) > 2 and not defer:
                pv(pend.pop(0))
        if defer:
            return lambda: _attend_finish(pend, pv, ob, obu, N, dst, dst_u, sink)
        _attend_finish(pend, pv, ob, obu, N, dst, dst_u, sink)

    def _attend_finish(pend, pv, ob, obu, N, dst, dst_u, sink):
        while pend:
            pv(pend.pop(0))
        if debug and "att_ob" in DBG and att_dbg[0] == 1:
            att_dbg[0] = 2
            tmd, tmdu = nxt("tmp", tmpb)
            P.op("act", _mk("activation", out=tmd[:], in_=ob[:], func=AF.Copy), reads=[obu], writes=[tmdu])
            P.dma("sp", DBG["att_ob"], tmd[:], reads=[tmdu])
        rd, rdu = nxt("tmp", tmpb)
        if sink is not None:
            P.op("act", _mk("activation", out=rd[64:128, 0:N], in_=ob[64:128, 0:N], func=AF.Ln, bias=sink, scale=1.0),
                 reads=[obu, U("esk")], writes=[rdu])
        else:
            P.op("act", _mk("activation", out=rd[64:128, 0:N], in_=ob[64:128, 0:N], func=AF.Ln), reads=[obu], writes=[rdu])
        P.op("act", _mk("activation", out=rd[64:128, 0:N], in_=rd[64:128, 0:N], func=AF.Exp, scale=-1.0), reads=[rdu], writes=[rdu])
        P.op("dve", _mk("tensor_tensor", out=dst, in0=ob[0:64, 0:N], in1=rd[64:128, 0:N], op=ALU.mult),
             reads=[obu, rdu], writes=[dst_u])

    def tok_proj(parts, tt, hu):
        bk, bu = bank()
        off = 0
        for (sv, c0, ncol, su) in parts:
            mm_group([(bk[:, off:off + ncol], hT[:, kc, tt * 128:(tt + 1) * 128], sv[:, kc, c0:c0 + ncol], kc == 0, kc == KC - 1)
                      for kc in range(KC)], reads=[su, hu], writes=[bu])
            off += ncol
        return bk, bu

    def transpose_cache(src_dram, width, dst_fn, stu_name):
        kst = A.alloc("kst", [128, 4, width], F32)
        P.dma("sp", kst[:], src_dram.rearrange("(t p) f -> p t f", p=128), writes=[U(stu_name)])
        for c in range((width + 127) // 128):
            w = min(128, width - c * 128)
            bk, bu = bank()
            for ct in range(4):
                P.op("pe", _mk("transpose", bk[0:w, ct * 128:(ct + 1) * 128], kst[:, ct, c * 128:c * 128 + w], identf[:]),
                     reads=[U(stu_name), U("identf")], writes=[bu], mark=(ct == 3))
            dst_fn(bk, bu, c)

    def dbg_dump(dst, tile, nch, units, c0=0):
        for c in range(nch):
            for blk in range(2):
                tm, tu = nxt("tmp", tmpb)
                P.op("dve", _mk("tensor_copy", out=tm[:], in_=tile[:, c, blk * BL:(blk + 1) * BL]), reads=units, writes=[tu])
                P.dma("sp", dst[:, c0 + c, blk * BL:(blk + 1) * BL], tm[:], reads=[tu])

    def layer_pass(l, s):
        lat = (s == 1)
        NSEQ = 1 if lat else 4
        SL = T // NSEQ
        m_pass = A.mark()
        ycv = A.alloc("ycv", [128, 2, T], BF16)
        ygq = A.alloc("ygq", [128, 4, T], BF16)
        yna = A.alloc("yna", [128, 2, T], BF16)
        yml = A.alloc("yml", [128, 2, T], BF16)
        lp = l % 2
        mod, A1, A2 = modL[lp], A1L[lp], A2L[lp]
        if not lat and l == 0:
            for j_ in range(4):
                adaln_piece(0, j_)
        if debug and "mod" in DBG and l == 0 and s == 0:
            P.dma("sp", DBG["mod"], mod[:], reads=[U("mod", lp)])
        if stop == ("adaln", l, s):
            raise _Stop()
        norm_mod(s, A1, 0, mod, lp)

        if debug and ("h_%d_%d" % (l, s)) in DBG:
            dbg_dump(DBG["h_%d_%d" % (l, s)], hT, KC, [U("h", 0), U("h", 1)])
        if debug and "rstd" in DBG and l == 0 and s == 0:
            P.dma("sp", DBG["rstd"], rstd[:], reads=[U("rstd")])
            P.dma("sp", DBG["A1"], A1[:], reads=[U("A12")])
        if stop == ("norm", l, s):
            raise _Stop()
        for c_ in range(2):
            small_load(wconvT[:, c_, :], I["w_conv"][l][:, c_ * 128:(c_ + 1) * 128].rearrange("k p -> p k"), U("wconvT"))
        s0v, s0u = load_win(l, 0, 512)
        s1v, s1u = load_win(l, 512, 512)
        m = A.mark()
        uc = A.alloc("uc", [128, T], F32)
        yc = A.alloc("yc", [128, T], F32)
        for c in range(2):
            for blk in range(2):
                bcc, bccu = proj(s0v, 256 + c * 128, 128, blk, s0u)
                tm, tu = nxt("tmp", tmpb)
                P.op("act", _mk("activation", out=tm[:], in_=bcc[:], func=AF.Copy), reads=[bccu], writes=[tu])
                bcv, bcvu = proj(s1v, c * 128, 128, blk, s1u)
                P.op("dve", _mk("tensor_tensor",
                    out=uc[:, blk * BL:(blk + 1) * BL], in0=bcv[:], in1=tm[:], op=ALU.mult),
                    reads=[bcvu, tu], writes=[U("uc")])
            ucv = uc[:].rearrange("p (q t) -> p q t", q=NSEQ)
            ycw = yc[:].rearrange("p (q t) -> p q t", q=NSEQ)
            P.op("dve", _mk("tensor_scalar", out=yc[:], in0=uc[:], scalar1=wconvT[:, c, 1:2], scalar2=None, op0=ALU.mult),
                 reads=[U("uc"), U("wconvT")], writes=[U("yc")])
            P.op("dve", _mk("scalar_tensor_tensor",
                out=ycw[:, :, 1:SL], in0=ucv[:, :, 0:SL - 1], scalar=wconvT[:, c, 0:1], in1=ycw[:, :, 1:SL],
                op0=ALU.mult, op1=ALU.add), reads=[U("uc"), U("yc")], writes=[U("yc")])
            P.op("dve", _mk("scalar_tensor_tensor",
                out=ycw[:, :, 0:SL - 1], in0=ucv[:, :, 1:SL], scalar=wconvT[:, c, 2:3], in1=ycw[:, :, 0:SL - 1],
                op0=ALU.mult, op1=ALU.add), reads=[U("uc"), U("yc")], writes=[U("yc")])
            for blk in range(2):
                bcb, bcbu = proj(s0v, c * 128, 128, blk, s0u)
                P.op("dve", _mk("tensor_tensor",
                    out=ycv[:, c, blk * BL:(blk + 1) * BL], in0=bcb[:], in1=yc[:, blk * BL:(blk + 1) * BL], op=ALU.mult),
                    reads=[bcbu, U("yc")], writes=[U("ycv")])
        flush_att()
        A.release(m)

        if stop == ("conv", l, s):
            raise _Stop()
        m = A.mark()
        s2v, s2u = load_win(l, 1024, 512)
        qg = A.alloc("qg", [128, 4, T], BF16)
        kA = A.alloc("kA", [128, T], BF16)
        kB = A.alloc("kB", [128, T], BF16)
        kz = {}
        for key_ in ((0, 0), (1, 1), (1, 0), (0, 1)):
            kz[key_] = A.alloc("kz%d%d" % key_, [128, T], BF16)
            P.op("dve", _mk("memset", kz[key_][:], 0.0), writes=[U("kz")])
        vtg = A.alloc("vtg", [128, 8, 2, 128], BF16)
        P.op("dve", _mk("memset", vtg[:].rearrange("p a b c -> p (a b) c")[:, :, 64:128], 1.0), writes=[U("vtg")])
        if lat:
            ropeC = A.alloc("ropeC", [128, T], F32)
            ropeS = A.alloc("ropeS", [128, T], F32)
            ropeP = A.alloc("ropeP", [128, 128], F32)
            P.dma("sp", ropeC[:], I["ropeCg"], writes=[U("ropeT")])
            P.dma("sp", ropeS[:], I["ropeSg"], writes=[U("ropeT")])
            P.dma("sp", ropeP[:], I["permg"], writes=[U("ropeP")])
            gmk = A.alloc("gmk", [128, 6, 512], BF16)
            P.dma("sp", gmk[:], I["gmask"].rearrange("d p q -> p d q"), writes=[U("gmask")])
            kcz = {}
            for key_ in ((0, 0), (1, 1), (1, 0), (0, 1)):
                kcz[key_] = A.alloc("kcz%d%d" % key_, [128, PAST], BF16)
                P.op("dve", _mk("memset", kcz[key_][:], 0.0), writes=[U("kc")])
            vtgc = A.alloc("vtgc", [128, 4, 2, 128], BF16)
            P.op("dve", _mk("memset", vtgc[:].rearrange("p a b c -> p (a b) c")[:, :, 64:128], 1.0), writes=[U("vtgc")])
            for ct in range(4):
                P.dma("pool", vtgc[:, ct, :, 0:64], I["cgv"][l][ct * 128:(ct + 1) * 128, :].rearrange("p (h d) -> p h d", h=2), writes=[U("vtgc")])

            def kdst(bk, bu, c):
                P.op("act", _mk("activation", out=kcz[(0, 0)][0:64, :], in_=bk[0:64, :], func=AF.Copy), reads=[bu], writes=[U("kc")])
                P.op("act", _mk("activation", out=kcz[(1, 1)][64:128, :], in_=bk[64:128, :], func=AF.Copy), reads=[bu], writes=[U("kc")])
                P.op("dve", _mk("tensor_copy", out=kcz[(1, 0)][0:64, :], in_=bk[64:128, :]), reads=[bu], writes=[U("kc")])
                P.op("dve", _mk("tensor_copy", out=kcz[(0, 1)][64:128, :], in_=bk[0:64, :]), reads=[bu], writes=[U("kc")])
            transpose_cache(I["cgk"][l], 128, kdst, "kst_g")
        small_load(esk[:], I["gqa_sink"][l].partition_broadcast(128), U("esk"))
        P.op("act", _mk("activation", out=esk[:], in_=esk[:], func=AF.Exp), reads=[U("esk")], writes=[U("esk")])
        for blk in range(2):
            for ch in range(4):
                sv, su, c0 = (s1v, s1u, 256 + ch * 128) if ch < 2 else (s2v, s2u, (ch - 2) * 128)
                bk, bu = proj(sv, c0, 128, blk, su)
                dst = qg[:, ch, blk * BL:(blk + 1) * BL]
                if lat:
                    rope(bk, bu, 128, dst, [U("qg")], blk, ropeC, ropeS, ropeP, 0.125)
                else:
                    P.op("act", _mk("activation", out=dst, in_=bk[:], func=AF.Identity, scale=0.125),
                         reads=[bu], writes=[U("qg")])
            bk, bu = proj(s2v, 256, 128, blk, s2u)
            bk2, bu2 = bank()
            mm_group([(bk2[0:64, :], s2v[:, kc, 320:384], hT[:, kc, blk * BL:(blk + 1) * BL], kc == 0, kc == KC - 1) for kc in range(KC)],
                     reads=[s2u, U("h", blk)], writes=[bu2])
            mm_group([(bk2[64:128, :], s2v[:, kc, 256:320], hT[:, kc, blk * BL:(blk + 1) * BL], kc == 0, kc == KC - 1) for kc in range(KC)],
                     reads=[s2u, U("h", blk)], writes=[bu2])
            for (b_, bu_, kX) in ((bk, bu, kA), (bk2, bu2, kB)):
                dst = kX[:, blk * BL:(blk + 1) * BL]
                if lat:
                    rope(b_, bu_, 128, dst, [U("kAB")], blk, ropeC, ropeS, ropeP, 1.0)
                else:
                    P.op("act", _mk("activation", out=dst, in_=b_[:], func=AF.Copy), reads=[bu_], writes=[U("kAB")])
        for (key_, src_, r0_) in (((0, 0), kA, 0), ((1, 1), kA, 64), ((1, 0), kB, 0), ((0, 1), kB, 64)):
            P.op("dve", _mk("tensor_copy", out=kz[key_][r0_:r0_ + 64, :], in_=src_[r0_:r0_ + 64, :]), reads=[U("kAB")], writes=[U("kz")])
        if stop == ("gqa_fm", l, s):
            raise _Stop()
        for tt in range(8):
            if lat:
                bk, bu = tok_proj([(s2v, 384, 128, s2u)], tt, U("h", tt // 4))
                voff = 0
            else:
                bk, bu = tok_proj([(s2v, 256, 256, s2u)], tt, U("h", tt // 4))
                voff = 128
            if not os.environ.get("NOVCOPY"):
              P.op("dve", _mk("tensor_copy",
                out=vtg[:, tt, :, 0:64], in_=bk[:, voff:voff + 128].rearrange("p (h d) -> p h d", h=2)),
                reads=[bu], writes=[U("vtg")])
            if not lat and not os.environ.get("NOOUT"):
                st, stu = nxt("tmp", tmpb)
                P.op("act", _mk("activation", out=st[:, 0:256], in_=bk[:, 0:256], func=AF.Copy), reads=[bu], writes=[stu])
                sq_, r0 = tt // 2, (tt % 2) * 128
                P.dma("sp", O["ogk"][sq_, l, r0:r0 + 128, :], st[:, 0:128], reads=[stu])
                P.dma("sp", O["ogv"][sq_, l, r0:r0 + 128, :], st[:, 128:256], reads=[stu])

        if debug and "qg" in DBG and l == 0 and lat:
            dbg_dump(DBG["qg"], qg, 4, [U("qg")])
            dbg_dump(DBG["kAB"], kA[:].rearrange("p (c t) -> p c t", c=1), 1, [U("kAB")], 0)
            dbg_dump(DBG["kAB"], kB[:].rearrange("p (c t) -> p c t", c=1), 1, [U("kAB")], 1)
            pass
        if stop == ("gqa_proj", l, s):
            raise _Stop()

        def kver(kv, hf, cache=False):
            return (kcz if cache else kz)[(kv, hf)]

        rd_g = [U("qg"), U("kz"), U("vtg")]
        for h in range(8):
            kv, hf, ch = h // 4, h % 2, h // 2
            if not lat:
                for sq_ in range(4):
                    q0 = sq_ * 256
                    kts = [dict(k=kver(kv, hf)[:, q0 + kt * 128:q0 + (kt + 1) * 128], v=vtg[:, sq_ * 2 + kt, kv, :]) for kt in range(2)]
                    fin_ = attend(qg[:, ch, q0:q0 + 256], 256, kts, ygq[hf * 64:(hf + 1) * 64, ch, q0:q0 + 256],
                                  U("ygq"), 1.0, rd_g, sink=esk[64:128, h:h + 1], defer=True)
                    if pend_fin[0] is not None:
                        pend_fin[0]()
                    pend_fin[0] = fin_
                    if l == 0 and 4 <= 4 + h < 12 and sq_ == 0:
                        adaln_piece(0, 4 + h)
            else:
                for u in range(2):
                    if False and l == 0 and h == 1 and u == 0 and att_dbg[0] == 0:
                        att_dbg[0] = 1
                    kts = []
                    for d in range(-1, 5):
                        j = 4 * u + d
                        if 0 <= j < 8:
                            kts.append(dict(k=kver(kv, hf)[:, j * 128:(j + 1) * 128], v=vtg[:, j, kv, :],
                                            bias=[(identb[:], gmk[:, d + 1, :], U("gmask"))],
                                            cols=(max(0, 128 * d - 128), min(512, 128 * d + 256))))
                    for ct in range(4):
                        kts.append(dict(k=kver(kv, hf, True)[:, ct * 128:(ct + 1) * 128], v=vtgc[:, ct, kv, :]))
                    attend(qg[:, ch, u * BL:(u + 1) * BL], 512, kts, ygq[hf * 64:(hf + 1) * 64, ch, u * BL:(u + 1) * BL],
                           U("ygq"), 1.0, rd_g + [U("kc"), U("vtgc"), U("identb")], sink=esk[64:128, h:h + 1])
                    if l + 1 < nlayers and (h * 2 + u) < 12:
                        adaln_piece(l + 1, h * 2 + u)
        flush_att()
        A.release(m)

        if stop == ("gqa", l, s):
            raise _Stop()
        m = A.mark()
        nq = A.alloc("nq", [128, 2, T], BF16)
        nk = A.alloc("nk", [128, 4, T], BF16)
        P.op("dve", _mk("memset", nk[:].rearrange("p a b -> p (a b)"), 0.0), writes=[U("nk")])
        vtn = A.alloc("vtn", [128, 8, 4, 128], BF16)
        P.op("dve", _mk("memset", vtn[:].rearrange("p a b c -> p (a b) c")[:, :, 64:128], 1.0), writes=[U("vtn")])
        s3v, s3u = load_win(l, 1536, 512)
        s4v, s4u = load_win(l, 2048, 512)
        if lat:
            nkc = A.alloc("nkc", [128, 4, PAST], BF16)
            P.op("dve", _mk("memset", nkc[:].rearrange("p a b -> p (a b)"), 0.0), writes=[U("nkc")])
            vtnc = A.alloc("vtnc", [128, 4, 4, 128], BF16)
            P.op("dve", _mk("memset", vtnc[:].rearrange("p a b c -> p (a b) c")[:, :, 64:128], 1.0), writes=[U("vtnc")])
            for ct in range(4):
                P.dma("pool", vtnc[:, ct, :, 0:64], I["cnv"][l][ct * 128:(ct + 1) * 128, :].rearrange("p (h d) -> p h d", h=4), writes=[U("vtnc")])
            def nkdst(bk, bu, c):
                P.op("act", _mk("activation", out=nkc[0:64, 2 * c, :], in_=bk[0:64, :], func=AF.Copy), reads=[bu], writes=[U("nkc")])
                P.op("act", _mk("activation", out=nkc[64:128, 2 * c + 1, :], in_=bk[64:128, :], func=AF.Copy), reads=[bu], writes=[U("nkc")])
            transpose_cache(I["cnk"][l], 256, nkdst, "kst_n")
            J128 = A.alloc("J128", [128, 128], BF16)
            P.dma("sp", J128[:], I["J128"], writes=[U("J128")])
            nbm = {}
            for u_ in range(2):
                for j_ in NA_TILES[u_]:
                    t_ = A.alloc("nbm_%d_%d" % (u_, j_), [128, 512], BF16)
                    jr_ = 7 - j_
                    P.dma("sp", t_[:], I["namask"][jr_ * 128:(jr_ + 1) * 128, u_ * 512:(u_ + 1) * 512], writes=[U("nbm")])
                    nbm[(u_, j_)] = t_
            J2t = A.alloc("J2t", [128, 256], BF16)
            P.dma("sp", J2t[:], I["na_J"], writes=[U("J128")])
            strips = [A.alloc("nstrip%d" % i, [128, 31 * 64], BF16) for i in range(2)]
            for i_ in range(2):
                P.op("dve", _mk("memset", strips[i_][:], 0.0), writes=[U("nstrip", i_)])

            def load_strip(h_):
                st_ = strips[h_ % 2]
                for ph in range(1):
                    for (ra, rb) in ((0, 8), (8, 16), (16, 24), (24, 31)):
                        src = AP(I["nat"].tensor, ((l * 4 + h_) * 31 + ra) * 128, [[1, 64], [128, rb - ra], [1, 64]])
                        P.dma("pool", st_[ph * 64:(ph + 1) * 64, ra * 64:rb * 64].rearrange("p (r c) -> p r c", c=64), src,
                              writes=[U("nstrip", h_ % 2)])
        for blk in range(2):
            for c in range(2):
                bk, bu = proj(s3v, c * 128, 128, blk, s3u)
                P.op("act", _mk("activation", out=nq[:, c, blk * BL:(blk + 1) * BL], in_=bk[:], func=AF.Identity, scale=0.125),
                     reads=[bu], writes=[U("nq")])
                bk, bu = proj(s3v, 256 + c * 128, 128, blk, s3u)
                P.op("dve", _mk("tensor_copy", out=nk[0:64, 2 * c, blk * BL:(blk + 1) * BL], in_=bk[0:64, :]),
                     reads=[bu], writes=[U("nk")])
                P.op("dve", _mk("tensor_copy", out=nk[64:128, 2 * c + 1, blk * BL:(blk + 1) * BL], in_=bk[64:128, :]),
                     reads=[bu], writes=[U("nk")])
        for tt in range(8):
            if lat:
                bk, bu = tok_proj([(s4v, 0, 256, s4u)], tt, U("h", tt // 4))
                voff = 0
            else:
                bk, bu = tok_proj([(s3v, 256, 256, s3u), (s4v, 0, 256, s4u)], tt, U("h", tt // 4))
                voff = 256
            P.op("dve", _mk("tensor_copy",
                out=vtn[:, tt, :, 0:64], in_=bk[:, voff:voff + 256].rearrange("p (h d) -> p h d", h=4)),
                reads=[bu], writes=[U("vtn")])
            if not lat:
                st, stu = nxt("tmp", tmpb)
                P.op("act", _mk("activation", out=st[:], in_=bk[:], func=AF.Copy), reads=[bu], writes=[stu])
                sq_, r0 = tt // 2, (tt % 2) * 128
                P.dma("sp", O["onk"][sq_, l, r0:r0 + 128, :], st[:, 0:256], reads=[stu])
                P.dma("sp", O["onv"][sq_, l, r0:r0 + 128, :], st[:, 256:512], reads=[stu])
        rd_n = [U("nq"), U("nk"), U("vtn")]
        if not lat:
            for h in range(4):
                hf, ch = h % 2, h // 2
                for sq_ in range(4):
                    q0 = sq_ * 256
                    kts = [dict(k=nk[:, h, q0 + kt * 128:q0 + (kt + 1) * 128], v=vtn[:, sq_ * 2 + kt, h, :]) for kt in range(2)]
                    fin_ = attend(nq[:, ch, q0:q0 + 256], 256, kts, yna[hf * 64:(hf + 1) * 64, ch, q0:q0 + 256],
                                  U("yna"), 1.0, rd_n, defer=True)
                    if pend_fin[0] is not None:
                        pend_fin[0]()
                    pend_fin[0] = fin_
        else:
            load_strip(0)
            for h in range(4):
                hf, ch = h % 2, h // 2
                if h + 1 < 4:
                    load_strip(h + 1)
                st_, stu_ = strips[h % 2], U("nstrip", h % 2)
                for u in range(2):
                    kts = []
                    for j in NA_TILES[u]:
                        jr = 7 - j
                        r_lo, r_hi = 2 * jr + 1 + 8 * u, 2 * jr + 8 * u
                        kts.append(dict(k=nk[:, h, j * 128:(j + 1) * 128], v=vtn[:, j, h, :],
                                        bias=[(J128[:], nbm[(u, j)][:], U("nbm")),
                                              (J2t[:, 0:128], st_[:, r_lo * 64:r_lo * 64 + 512], stu_),
                                              (J2t[:, 128:256], st_[:, r_hi * 64:r_hi * 64 + 512], stu_)],
                                        cols=NA_COLS[(u, j)]))
                    for ct in range(4):
                        kts.append(dict(k=nkc[:, h, ct * 128:(ct + 1) * 128], v=vtnc[:, ct, h, :]))
                    attend(nq[:, ch, u * BL:(u + 1) * BL], 512, kts, yna[hf * 64:(hf + 1) * 64, ch, u * BL:(u + 1) * BL],
                           U("yna"), 1.0, rd_n + [U("nkc"), U("vtnc"), U("J128")])
        flush_att()
        A.release(m)

        if debug and "nqk" in DBG and l == 0 and lat:
            dbg_dump(DBG["nqk"], nq, 2, [U("nq")], 0)
        if stop == ("na", l, s):
            raise _Stop()
        m = A.mark()
        s5v, s5u = load_win(l, 2560, 160)
        NK = T + (PAST if lat else 0)
        qn = A.alloc("qn", [128, 2, T], BF16)
        qm = A.alloc("qm", [128, 4, T], BF16)
        km = A.alloc("km", [128, 4, NK], BF16)
        vtm = A.alloc("vtm", [128, NK // 128, 4, 128], BF16)
        ckvT = A.alloc("ckvT", [128, T], BF16)
        wuq = A.alloc("wuq", [128, 2, 384], BF16)
        wukv = A.alloc("wukv", [128, 512], BF16)
        mqf = A.alloc("mqf", [128, 2, BL], F32)
        P.op("dve", _mk("memset", vtm[:].rearrange("p a b c -> p (a b) c")[:, :, 64:128], 1.0), writes=[U("vtm")])
        P.dma("pool", wuq[:], I["mla_w_uq"][l].rearrange("(c p) n -> p c n", p=128), writes=[U("wuq")])
        P.dma("pool", wukv[:], I["mla_w_ukv"][l], writes=[U("wukv")])
        small_load(gq_mla[:], I["mla_g_q"][l].rearrange("(c p) -> p c", p=128), U("gmla"))
        small_load(gkv_col[:], I["mla_g_kv"][l].rearrange("(p o) -> p o", o=1), U("gmla"))
        small_load(gkv_bc[:], I["mla_g_kv"][l].partition_broadcast(128), U("gmla"))
        if lat:
            ropeC = A.alloc("ropeCm", [128, T], F32)
            ropeS = A.alloc("ropeSm", [128, T], F32)
            ropeP = A.alloc("ropePm", [128, 128], F32)
            P.dma("sp", ropeC[:], I["ropeCm"], writes=[U("ropeT")])
            P.dma("sp", ropeS[:], I["ropeSm"], writes=[U("ropeT")])
            P.dma("sp", ropeP[:], I["permm"], writes=[U("ropeP")])
            krt = A.alloc("krt", [128, BL], BF16)
            ckvcT = A.alloc("ckvcT", [128, PAST], BF16)
        for blk in range(2):
            tk = slice(blk * BL, (blk + 1) * BL)
            for c in range(2):
                bk, bu = proj(s4v, 256 + c * 128, 128, blk, s4u)
                P.op("act", _mk("activation", out=mqf[:, c, :], in_=bk[:], func=AF.Copy), reads=[bu], writes=[U("mqf")])
            norm_stats(lambda kc: mqf[:, kc, :], 2, blk, lambda kc: [U("mqf")], 1.0 / 256)
            for c in range(2):
                P.op("dve", _mk("scalar_tensor_tensor", out=qn[:, c, tk], in0=mqf[:, c, :], scalar=gq_mla[:, c:c + 1],
                                                                        in1=rstd[:], op0=ALU.mult, op1=ALU.mult),
                     reads=[U("mqf"), U("gmla"), U("rstd")], writes=[U("qn")])
            for h in range(4):
                bk, bu = bank()
                mm_group([(bk[0:96, :], wuq[:, c, h * 96:(h + 1) * 96], qn[:, c, tk], c == 0, c == 1) for c in range(2)],
                         reads=[U("wuq"), U("qn")], writes=[bu])
                if lat:
                    rope(bk, bu, 96, qm[0:96, h, tk], [U("qm")], blk, ropeC, ropeS, ropeP, 1.0)
                else:
                    P.op("act", _mk("activation", out=qm[0:96, h, tk], in_=bk[0:96, :], func=AF.Copy),
                         reads=[bu], writes=[U("qm")])
            bk, bu = proj(s5v, 0, 128, blk, s5u)
            tm, tu = nxt("tmp", tmpb)
            P.op("act", _mk("activation", out=tm[:], in_=bk[:], func=AF.Copy), reads=[bu], writes=[tu])
            norm_stats(lambda kc: tm[:], 1, blk, lambda kc: [tu], 1.0 / 128)
            P.op("dve", _mk("scalar_tensor_tensor", out=ckvT[:, tk], in0=tm[:], scalar=gkv_col[:, 0:1], in1=rstd[:],
                                                                      op0=ALU.mult, op1=ALU.mult),
                 reads=[tu, U("gmla"), U("rstd")], writes=[U("ckvT")])
            bk, bu = proj(s5v, 64, 96, blk, s5u)
            if lat:
                rope(bk, bu, 96, krt[0:96, :], [U("krt")], blk, ropeC, ropeS, ropeP, 1.0)
                for h in range(4):
                    P.op("dve", _mk("tensor_copy", out=km[64:96, h, tk], in_=krt[64:96, :]), reads=[U("krt")], writes=[U("km")])
            else:
                for h in range(4):
                    P.op("act", _mk("activation", out=km[64:96, h, tk], in_=bk[64:96, :], func=AF.Copy),
                         reads=[bu], writes=[U("km")])
            for h in range(4):
                bk, bu = bank()
                mm_group([(bk[0:64, :], wukv[:, h * 128:h * 128 + 64], ckvT[:, tk], True, True)], reads=[U("wukv"), U("ckvT")], writes=[bu])
                P.op("dve", _mk("tensor_copy", out=km[0:64, h, tk], in_=bk[0:64, :]), reads=[bu], writes=[U("km")])
        for tt in range(8):
            bk, bu = bank()
            mm_group([(bk[:], ckvT[:, tt * 128:(tt + 1) * 128], wukv[:], True, True)], reads=[U("wukv"), U("ckvT")], writes=[bu])
            P.op("dve", _mk("tensor_copy", out=vtm[:, tt, :, 0:64],
                                                             in_=bk[:].rearrange("p (h d) -> p h d", h=4)[:, :, 64:128]),
                 reads=[bu], writes=[U("vtm")])
            if not lat:
                bk, bu = tok_proj([(s5v, 0, 160, s5u)], tt, U("h", tt // 4))
                st, stu = nxt("tmp", tmpb)
                P.op("act", _mk("activation", out=st[:, 256:384], in_=bk[:, 0:128], func=AF.Square, accum_out=st[:, 400:401]),
                     reads=[bu], writes=[stu])
                P.op("act", _mk("activation", out=st[:, 401:402], in_=st[:, 400:401], func=AF.Ln, bias=eps_t[:, 0:1], scale=1.0 / 128),
                     reads=[stu, U("eps")], writes=[stu])
                P.op("act", _mk("activation", out=st[:, 402:403], in_=st[:, 401:402], func=AF.Exp, scale=-0.5), reads=[stu], writes=[stu])
                P.op("dve", _mk("scalar_tensor_tensor", out=st[:, 0:128], in0=bk[:, 0:128], scalar=st[:, 402:403], in1=gkv_bc[:],
                                                                          op0=ALU.mult, op1=ALU.mult), reads=[bu, stu, U("gmla")], writes=[stu])
                P.op("act", _mk("activation", out=st[:, 128:160], in_=bk[:, 128:160], func=AF.Copy), reads=[bu, stu], writes=[stu])
                sq_, r0 = tt // 2, (tt % 2) * 128
                P.dma("sp", O["ockv"][sq_, l, r0:r0 + 128, :], st[:, 0:128], reads=[stu])
                P.dma("sp", O["okr"][sq_, l, r0:r0 + 128, :], st[:, 128:160], reads=[stu])
        if lat:
            transpose_cache(I["cckv"][l], 128, lambda bk, bu, c: P.op(
                "act", _mk("activation", out=ckvcT[:], in_=bk[:], func=AF.Copy), reads=[bu], writes=[U("ckvcT")]), "kst_m")
            for h in range(4):
                bk, bu = bank()
                mm_group([(bk[0:64, :], wukv[:, h * 128:h * 128 + 64], ckvcT[:], True, True)], reads=[U("wukv"), U("ckvcT")], writes=[bu])
                P.op("dve", _mk("tensor_copy", out=km[0:64, h, T:T + PAST], in_=bk[0:64, :]), reads=[bu], writes=[U("km")])
            for ct in range(4):
                bk, bu = bank()
                mm_group([(bk[:], ckvcT[:, ct * 128:(ct + 1) * 128], wukv[:], True, True)], reads=[U("wukv"), U("ckvcT")], writes=[bu])
                P.op("dve", _mk("tensor_copy", out=vtm[:, 8 + ct, :, 0:64],
                                                                 in_=bk[:].rearrange("p (h d) -> p h d", h=4)[:, :, 64:128]),
                     reads=[bu], writes=[U("vtm")])
            krs = A.alloc("krs", [128, 4, 96], F32)
            P.op("dve", _mk("memset", krs[:].rearrange("p a b -> p (a b)"), 0.0), writes=[U("krs")])
            P.dma("sp", krs[:, :, 64:96], I["ckr"][l].rearrange("(t p) f -> p t f", p=128), writes=[U("krs")])
            bk, bu = bank()
            for ct in range(4):
                P.op("pe", _mk("transpose", bk[0:96, ct * 128:(ct + 1) * 128], krs[:, ct, :], identf[:]),
                     reads=[U("krs"), U("identf")], writes=[bu], mark=(ct == 3))
            for h in range(4):
                P.op("act", _mk("activation", out=km[64:96, h, T:T + PAST], in_=bk[64:96, :], func=AF.Copy),
                     reads=[bu], writes=[U("km")])
        rd_m = [U("qm"), U("km"), U("vtm")]
        for h in range(4):
            hf, ch = h % 2, h // 2
            if not lat:
                for sq_ in range(4):
                    q0 = sq_ * 256
                    kts = [dict(k=km[0:96, h, q0 + kt * 128:q0 + (kt + 1) * 128], v=vtm[:, sq_ * 2 + kt, h, :]) for kt in range(2)]
                    fin_ = attend(qm[0:96, h, q0:q0 + 256], 256, kts, yml[hf * 64:(hf + 1) * 64, ch, q0:q0 + 256], U("yml"), MLA_SCALE, rd_m,
                                  defer=True)
                    if pend_fin[0] is not None:
                        pend_fin[0]()
                    pend_fin[0] = fin_
            else:
                for u in range(2):
                    kts = [dict(k=km[0:96, h, j * 128:(j + 1) * 128], v=vtm[:, j, h, :]) for j in range(12)]
                    attend(qm[0:96, h, u * BL:(u + 1) * BL], 512, kts, yml[hf * 64:(hf + 1) * 64, ch, u * BL:(u + 1) * BL],
                           U("yml"), MLA_SCALE, rd_m)
        flush_att()
        A.release(m)

        if debug and ("y_%d_%d" % (l, s)) in DBG:
            dd = DBG["y_%d_%d" % (l, s)]
            for i, (yt, nch, un) in enumerate(((ycv, 2, "ycv"), (ygq, 4, "ygq"), (yna, 2, "yna"), (yml, 2, "yml"))):
                c0 = (0, 2, 6, 8)[i]
                dbg_dump(dd, yt, nch, [U(un)], c0)

        if stop == ("mla", l, s):
            raise _Stop()
        m = A.mark()
        acc = A.alloc("acc", [128, KC, T], F32)
        mrg = A.alloc("mrg", [128, KC, T], BF16)
        wbr = [A.alloc("wbr%d" % i, [128, 4096], BF16) for i in range(2)]
        branches = (("w_branch_conv", 2, ycv, "ycv"), ("w_branch_gqa", 4, ygq, "ygq"), ("w_branch_na", 2, yna, "yna"), ("w_branch_mla", 2, yml, "yml"))
        for b, (wn, kb_n, yb, yu) in enumerate(branches):
            wbt, wsu = wbr[b % 2], U("wbr", b % 2)
            wbv = wbt[:, 0:kb_n * D].rearrange("p (kc n) -> p kc n", kc=kb_n)
            wload(wbv, I[wn][l].rearrange("(kc p) n -> p kc n", p=128), wsu)
            for mg in range(2):
                gsv, gsu = load_win(l, 2720 + b * D + mg * 512, 512)
                for mi in range(4):
                    mo = mg * 4 + mi
                    for blk in range(2):
                        tk = slice(blk * BL, (blk + 1) * BL)
                        gb, gbu = proj(gsv, mi * 128, 128, blk, gsu)
                        sg, sgu = nxt("tmp", tmpb)
                        P.op("act", _mk("activation", out=sg[:], in_=gb[:], func=AF.Sigmoid), reads=[gbu], writes=[sgu])
                        bb, bbu = bank()
                        mm_group([(bb[:], wbv[:, kb, mo * 128:(mo + 1) * 128], yb[:, kb, tk], kb == 0, kb == kb_n - 1) for kb in range(kb_n)],
                                 reads=[wsu, U(yu)], writes=[bbu])
                        if b == 0:
                            P.op("dve", _mk("tensor_tensor", out=acc[:, mo, tk], in0=bb[:], in1=sg[:], op=ALU.mult),
                                 reads=[bbu, sgu], writes=[U("acc", mo, blk)])
                        else:
                            P.op("dve", _mk("tensor_tensor", out=sg[:], in0=bb[:], in1=sg[:], op=ALU.mult),
                                 reads=[bbu, sgu], writes=[sgu])
                            if b < 3:
                                P.op("dve", _mk("tensor_tensor", out=acc[:, mo, tk], in0=acc[:, mo, tk], in1=sg[:], op=ALU.add),
                                     reads=[sgu, U("acc", mo, blk)], writes=[U("acc", mo, blk)])
                            else:
                                P.op("dve", _mk("tensor_tensor", out=mrg[:, mo, tk], in0=acc[:, mo, tk], in1=sg[:], op=ALU.add),
                                     reads=[sgu, U("acc", mo, blk)], writes=[U("mrg", blk)])
        for j in range(2):
            slv, su = req_slot("w_o", l, D, j * 512, 512)
            for mi in range(4):
                mo = j * 4 + mi
                for blk in range(2):
                    tk = slice(blk * BL, (blk + 1) * BL)
                    bk, bu = bank()
                    mm_group([(bk[:], slv[:, kc, mi * 128:(mi + 1) * 128], mrg[:, kc, tk], kc == 0, kc == KC - 1) for kc in range(KC)],
                             reads=[su, U("mrg", blk)], writes=[bu])
                    P.op("dve", _mk("scalar_tensor_tensor",
                        out=xT[s][:, mo, tk], in0=bk[:], scalar=mod[:, 16 + mo, s:s + 1], in1=xT[s][:, mo, tk], op0=ALU.mult, op1=ALU.add),
                        reads=[bu, U("mod", lp), U("x", s, blk)], writes=[U("x", s, blk)])
        A.release(m_pass)

        if debug and ("xa_%d_%d" % (l, s)) in DBG:
            P.dma("sp", DBG["xa_%d_%d" % (l, s)], xT[s][:], reads=[U("x", s, 0), U("x", s, 1)])
        if stop == ("attn", l, s):
            raise _Stop()
        if l == 0 and not lat:
            load_x(1, I["xs"])
        norm_mod(s, A2, 24, mod, lp)
        m = A.mark()
        hid = A.alloc("hid", [128, 32, T], BF16)
        for j in range(8):
            slv, su = req_slot("w_ff1", l, D, j * 512, 512)
            for mi in range(4):
                hc = j * 4 + mi
                for blk in range(2):
                    tk = slice(blk * BL, (blk + 1) * BL)
                    bk, bu = proj(slv, mi * 128, 128, blk, su)
                    r_, ru = nxt("tmp", tmpb)
                    P.op("act", _mk("activation", out=r_[:], in_=bk[:], func=AF.Relu), reads=[bu], writes=[ru])
                    eng = "dve"
                    P.op(eng, _mk("tensor_tensor", out=hid[:, hc, tk], in0=r_[:], in1=r_[:], op=ALU.mult),
                         reads=[ru], writes=[U("hid", blk)])
        for mo in range(8):
            slv, su = req_slot("w_ff2", l, 4 * D, mo * 128, 128)
            for blk in range(2):
                tk = slice(blk * BL, (blk + 1) * BL)
                bk, bu = bank()
                mm_group([(bk[:], slv[:, kc, :], hid[:, kc, tk], kc == 0, kc == 31) for kc in range(32)], reads=[su, U("hid", blk)], writes=[bu])
                P.op("dve", _mk("scalar_tensor_tensor",
                    out=xT[s][:, mo, tk], in0=bk[:], scalar=mod[:, 40 + mo, s:s + 1], in1=xT[s][:, mo, tk], op0=ALU.mult, op1=ALU.add),
                    reads=[bu, U("mod", lp), U("x", s, blk)], writes=[U("x", s, blk)])
        flush_att()
        A.release(m)
        if debug and ("x_%d_%d" % (l, s)) in DBG:
            P.dma("sp", DBG["x_%d_%d" % (l, s)], xT[s][:], reads=[U("x", s, 0), U("x", s, 1)])
        if stop == ("end", l, s):
            raise _Stop()

    def final_out(s, dst):
        m = A.mark()
        tf = A.alloc("tfin", [128, KC, BL], F32)
        ost = [A.alloc("ost%d" % i, [128, D], F32) for i in range(2)]
        for blk in range(2):
            norm_stats(lambda kc: xT[s][:, kc, blk * BL:(blk + 1) * BL], KC, blk, lambda kc: [U("x", s, blk)], 1.0 / D)
            for kc in range(KC):
                P.op("dve", _mk("scalar_tensor_tensor", out=tf[:, kc, :], in0=xT[s][:, kc, blk * BL:(blk + 1) * BL],
                                                                   scalar=gfin[:, kc:kc + 1], in1=rstd[:], op0=ALU.mult, op1=ALU.mult),
                     reads=[U("x", s, blk), U("gfin"), U("rstd")], writes=[U("tfin")])
            for ts in range(4):
                tt = blk * 4 + ts
                o_, ou = ost[tt % 2], U("ost", tt % 2)
                for half in range(2):
                    bk, bu = bank()
                    for q in range(4):
                        kc = half * 4 + q
                        P.op("pe", _mk("transpose", bk[:, q * 128:(q + 1) * 128], tf[:, kc, ts * 128:(ts + 1) * 128], identf[:]),
                             reads=[U("tfin"), U("identf")], writes=[bu], mark=(q == 3))
                    if half == 0:
                        P.op("act", _mk("activation", out=o_[:, 0:512], in_=bk[:], func=AF.Copy), reads=[bu], writes=[ou])
                    else:
                        P.op("dve", _mk("tensor_copy", out=o_[:, 512:1024], in_=bk[:]), reads=[bu], writes=[ou])
                P.dma("sp", dst[tt * 128:(tt + 1) * 128, :], o_[:], reads=[ou])
        flush_att()
        A.release(m)

    try:
        if debug and "xin" in DBG:
            P.dma("sp", DBG["xin"], xT[0][:], reads=[U("x", 0, 0), U("x", 0, 1)])
        if stop == ("load", 0, 0):
            raise _Stop()
        for l in range(nlayers):
            layer_pass(l, 0)
            layer_pass(l, 1)
        final_out(0, O["yp"])
        final_out(1, O["ys"])
    except _Stop:
        pass
    P.finish()
    P.emit()
    return nc


def _na_tables():
    r = np.arange(16)
    r0 = np.clip(r - 4, 0, 8)
    c = np.arange(64)
    c0 = np.clip(c - 8, 0, 48)
    rowok = (r[None, :] >= r0[:, None]) & (r[None, :] < r0[:, None] + 8)
    colok = (c[None, :] >= c0[:, None]) & (c[None, :] < c0[:, None] + 16)
    ok = rowok[:, None, :, None] & colok[None, :, None, :]
    ok = ok.reshape(1024, 1024)
    maskT = np.where(ok.T, 0.0, NEG).astype(np.float32)
    mask_rev = np.ascontiguousarray(maskT[::-1, :])
    tiles = []
    cols = {}
    for u in range(2):
        tl = [j for j in range(8) if ok[u * 512:(u + 1) * 512, j * 128:(j + 1) * 128].any()]
        tiles.append(tl)
        for j in tl:
            v = np.where(ok[u * 512:(u + 1) * 512, j * 128:(j + 1) * 128].any(1))[0]
            cols[(u, j)] = (int(v.min()) // 64 * 64, (int(v.max()) // 64 + 1) * 64)
    return mask_rev, tiles, cols


NA_MASK_REV, NA_TILES, NA_COLS = _na_tables()


def _gmask():
    g = np.zeros((6, 128, 512), np.float32)
    kk = np.arange(128)[:, None]
    qq = np.arange(512)[None, :]
    for d in range(-1, 5):
        g[d + 1] = np.where(np.abs(qq - 128 * d - kk) <= 128, 0.0, NEG)
    return g


_PROG = {}


def kernel(x_prompt, x_sample, cache_gqa_k, cache_gqa_v, cache_na_k, cache_na_v, cache_mla_ckv,
           cache_mla_krope, c, c_ctx, w_mod, b_mod, g_attn, g_mlp, w_in, w_conv, gqa_sink, na_rpb,
           mla_g_q, mla_w_uq, mla_g_kv, mla_w_ukv, w_branch_conv, w_branch_gqa, w_branch_na,
           w_branch_mla, w_o, w_ff1, w_ff2, g_final, _debug=None, _nlayers=DEPTH):
    f = lambda a: np.ascontiguousarray(np.asarray(a, dtype=np.float32))
    consts = _consts()
    consts["gmask"] = _gmask()
    consts["namask"] = NA_MASK_REV
    J = np.zeros((128, 128), np.float32)
    J[np.arange(128), 127 - np.arange(128)] = 1.0
    consts["J128"] = J
    J2 = np.zeros((128, 128), np.float32)
    for p_ in range(64):
        J2[p_, 63 - p_] = 1.0
        J2[64 + p_, 127 - p_] = 1.0
    consts["J2"] = J2
    rpb = f(na_rpb)
    nat = np.zeros((DEPTH, 4, 31, 128), np.float32)
    nat[:, :, 8:23, 48:79] = rpb[:, :, ::-1, ::-1]
    shared = {
        "w_mod": f(w_mod), "b_mod": f(b_mod), "g_attn": f(g_attn), "g_mlp": f(g_mlp), "w_in": f(w_in),
        "w_conv": f(w_conv), "gqa_sink": f(gqa_sink), "nat": nat, "mla_g_q": f(mla_g_q),
        "mla_w_uq": f(mla_w_uq), "mla_g_kv": f(mla_g_kv), "mla_w_ukv": f(mla_w_ukv),
        "w_branch_conv": f(w_branch_conv), "w_branch_gqa": f(w_branch_gqa), "w_branch_na": f(w_branch_na),
        "w_branch_mla": f(w_branch_mla), "w_o": f(w_o), "w_ff1": f(w_ff1), "w_ff2": f(w_ff2), "g_final": f(g_final),
    }
    import ml_dtypes
    for k_, v_ in consts.items():
        shared["c_" + k_] = np.ascontiguousarray(v_.astype(ml_dtypes.bfloat16)) if k_ in CONST_BF16 else f(v_)
    xp, xs = f(x_prompt), f(x_sample)
    cg = {"cgk": f(cache_gqa_k).reshape(8, DEPTH, PAST, 128), "cgv": f(cache_gqa_v).reshape(8, DEPTH, PAST, 128),
          "cnk": f(cache_na_k).reshape(8, DEPTH, PAST, 256), "cnv": f(cache_na_v).reshape(8, DEPTH, PAST, 256),
          "cckv": f(cache_mla_ckv), "ckr": f(cache_mla_krope)}
    cc, cx = f(c), f(c_ctx)
    in_maps = []
    for i in range(NCORES):
        d = dict(shared)
        d["xp"] = np.ascontiguousarray(xp[4 * i:4 * i + 4].reshape(T, D))
        d["xs"] = np.ascontiguousarray(xs[i])
        for k_, v_ in cg.items():
            d[k_] = np.ascontiguousarray(v_[i])
        d["cvec"] = np.ascontiguousarray(np.stack([cx, cc[i]]))
        in_maps.append(d)
    if _debug == "prep":
        return in_maps
    key = (repr(_debug), _nlayers)
    if key not in _PROG:
        po = []
        build(debug=_debug, nlayers=_nlayers, plan_out=po)
        _PROG[key] = build(debug=_debug, nlayers=_nlayers, plan=po)
    res = run_bass_kernel_spmd(_PROG[key], in_maps, core_ids=list(range(NCORES)))
    R = res.results
    y_prompt = np.concatenate([R[i]["yp"].reshape(4, 256, D) for i in range(NCORES)], axis=0)
    y_sample = np.stack([R[i]["ys"] for i in range(NCORES)], axis=0)
    cat = lambda n, shp: np.concatenate([R[i][n] for i in range(NCORES)], axis=0).reshape(shp)
    outs = (y_prompt.astype(np.float32), y_sample.astype(np.float32),
            cat("ogk", (32, DEPTH, 256, 2, 64)), cat("ogv", (32, DEPTH, 256, 2, 64)),
            cat("onk", (32, DEPTH, 256, 4, 64)), cat("onv", (32, DEPTH, 256, 4, 64)),
            cat("ockv", (32, DEPTH, 256, 128)), cat("okr", (32, DEPTH, 256, 32)))
    if _debug:
        kernel.dbg = [{n: R[i]["dbg_" + n] for n in _debug} for i in range(NCORES)]
    return outs
```

```python
import os
import numpy as np
import concourse.bass as bass
import concourse.mybir as mybir
from concourse.ap import AP
from concourse.bass_utils import run_bass_kernel_spmd

F32 = mybir.dt.float32
F32R = mybir.dt.float32r
BF16 = mybir.dt.bfloat16
AF = mybir.ActivationFunctionType
ALU = mybir.AluOpType

ENGS = ("pe", "act", "dve", "pool", "sp")
NCORES = 8
D = 1024
KC = 8
T = 1024
BL = 512
DEPTH = 2
PAST = 512
DIN = 6816
EPS = 1e-6
NEG = -30000.0
MLA_SCALE = 96 ** -0.5


class Unit:
    __slots__ = ("name", "w", "rs")

    def __init__(self, name):
        self.name = name
        self.w = {}
        self.rs = {}


class Prog:
    NRING = 32

    def __init__(self, nc):
        self.nc = nc
        self.streams = {e: [] for e in ENGS}
        self.cnt = {e: 0 for e in ENGS}
        self.sems = {e: nc.alloc_semaphore("sem_" + e) for e in ENGS}
        self.ring = [nc.alloc_semaphore("dq%d" % i) for i in range(self.NRING)]
        self.ring_val = [0] * self.NRING
        self.ring_rng = {"sp": (0, self.NRING // 2), "act": (0, self.NRING // 2), "pool": (self.NRING // 2, self.NRING)}
        self.ring_pos = {"sp": 0, "act": 0, "pool": 0}
        self.seen = {e: {} for e in ENGS}
        self.units = {}
        self.fence = []

    def U(self, *key):
        u = self.units.get(key)
        if u is None:
            u = Unit(key)
            self.units[key] = u
        return u

    def _sem(self, key):
        return self.sems[key] if isinstance(key, str) else self.ring[key]

    def _deps(self, eng, reads, writes):
        evs = []
        for u in reads:
            for w in u.w.values():
                if not (w[2] == eng and eng == "pe"):
                    evs.append(w)
            if u.name[0] == "bank":
                for r in u.rs.values():
                    if r[2] != eng:
                        evs.append(r)
        for u in writes:
            if u.name[0] not in self.PERSIST:
                for f in self.fence:
                    if f[2] != eng:
                        evs.append(f)
            for w in u.w.values():
                if w[2] != eng:
                    evs.append(w)
            for r in u.rs.values():
                if r[2] != eng:
                    evs.append(r)
        need = {}
        for (k, v, e) in evs:
            if self.seen[eng].get(k, 0) >= v:
                continue
            if need.get(k, 0) < v:
                need[k] = v
        for k, v in need.items():
            self.seen[eng][k] = v
        return list(need.items())

    def op(self, eng, fn, reads=(), writes=(), mark=True):
        waits = self._deps(eng, reads, writes)
        self.streams[eng].append((waits, fn, [(eng, 1)] if mark else []))
        if mark:
            self.cnt[eng] += 1
            ev = (eng, self.cnt[eng], eng)
            for u in reads:
                u.rs[eng] = ev
            for u in writes:
                u.w = {eng: ev}
                u.rs = {}

    def dma(self, q, out_ap, in_ap, reads=(), writes=(), **kw):
        waits = self._deps(q, reads, writes)
        lo, hi = self.ring_rng[q]
        key_ = "pool" if q == "pool" else "sp"
        i = lo + self.ring_pos[key_] % (hi - lo)
        self.ring_pos[key_] += 1
        prev = self.ring_val[i]
        if prev > 0 and self.seen[q].get(i, 0) < prev:
            waits.append((i, prev))
            self.seen[q][i] = prev
        self.ring_val[i] = prev + 16
        ev = (i, prev + 16, "dma")

        def fn(engine, out_ap=out_ap, in_ap=in_ap, kw=kw):
            return engine.dma_start(out=out_ap, in_=in_ap, **kw)

        self.streams[q].append((waits, fn, [(i, 16)]))
        for u in reads:
            u.rs[("d", i)] = ev
        for u in writes:
            u.w[("d", i)] = ev
            u.rs = {}

    PERSIST = frozenset(("identf", "onesf", "identb", "gT", "gfin", "cvf", "scb", "bmodT", "mod", "A12", "wconvT",
                         "esk", "gmla", "rstd", "eps", "x", "h", "slot", "bank", "sq", "tmp", "pT"))

    def set_fence(self):
        ev = [(o, self.cnt[o], o) for o in ENGS if self.cnt[o] > 0]
        ev += [(i, self.ring_val[i], "dma") for i in range(self.NRING) if self.ring_val[i] > 0]
        self.fence = ev

    def barrier(self):
        for e in ENGS:
            waits = []
            for o in ENGS:
                if o != e and self.cnt[o] > 0 and self.seen[e].get(o, 0) < self.cnt[o]:
                    waits.append((o, self.cnt[o]))
                    self.seen[e][o] = self.cnt[o]
            for i in range(self.NRING):
                v = self.ring_val[i]
                if v > 0 and self.seen[e].get(i, 0) < v:
                    waits.append((i, v))
                    self.seen[e][i] = v
            if waits:
                self.streams[e].append((waits, None, []))

    def finish(self):
        for i in range(self.NRING):
            v = self.ring_val[i]
            if v > 0 and self.seen["sp"].get(i, 0) < v:
                self.streams["sp"].append(([(i, v)], None, []))
                self.seen["sp"][i] = v
        for e in ENGS:
            if e != "sp" and self.cnt[e] > 0:
                self.streams["sp"].append(([(e, self.cnt[e])], None, []))

    def emit(self):
        nc = self.nc
        engobj = {"pe": "tensor", "act": "scalar", "dve": "vector", "pool": "gpsimd", "sp": "sync"}
        with nc.Block() as block:
            for e in ENGS:
                stream = self.streams[e]

                def body(engine, stream=stream):
                    for waits, fn, incs in stream:
                        for k, v in waits:
                            engine.wait_ge(self._sem(k), v)
                        if fn is None:
                            continue
                        ins = fn(engine)
                        for k, n in incs:
                            ins.then_inc(self._sem(k), n)

                getattr(block, engobj[e])(body)


def _mk(name, *a, **k):
    return lambda e: getattr(e, name)(*a, **k)


class Arena:
    def __init__(self, nc, nbytes):
        self.nc = nc
        self.start, self.end = nc.bump_sbuf(nbytes)
        self.cur = self.start
        self.n = 0
        self.on_release = None

    def alloc(self, name, shape, dt):
        nb = int(np.prod(shape[1:])) * (4 if dt == F32 else 2)
        off = (self.cur + 63) // 64 * 64
        assert off + nb <= self.end, ("SBUF arena overflow", name, off + nb - self.end)
        self.cur = off + nb
        self.n += 1
        return self.nc.alloc_sbuf_tensor_at("%s_%d" % (name, self.n), list(shape), dt, offset=off)

    def mark(self):
        return self.cur

    def release(self, m):
        self.cur = m
        if self.on_release is not None:
            self.on_release()


def _consts():
    c = {}
    c["identf"] = np.eye(128, dtype=np.float32)
    c["onesf"] = np.ones((128, 128), np.float32)
    t = np.arange(T)
    rows = (t // 64).astype(np.float32)
    cols = (t % 64).astype(np.float32)
    Cg = np.zeros((128, T), np.float32)
    Sg = np.zeros((128, T), np.float32)
    Pg = np.zeros((128, 128), np.float32)
    for p in range(128):
        d = p % 64
        b = d // 32
        i = d % 32
        inv = 10000.0 ** (-(i % 16) / 16.0)
        pos = rows if b == 0 else cols
        ang = (pos * np.float32(inv)).astype(np.float32)
        Cg[p] = np.cos(ang)
        Sg[p] = np.sin(ang) * (-1.0 if i < 16 else 1.0)
        partner = p + 16 if i < 16 else p - 16
        Pg[partner, p] = 1.0
    c["ropeCg"], c["ropeSg"], c["permg"] = Cg, Sg, Pg
    Cm = np.ones((128, T), np.float32)
    Sm = np.zeros((128, T), np.float32)
    Pm = np.zeros((128, 128), np.float32)
    for p in range(64):
        Pm[p, p] = 1.0
    for p in range(64, 96):
        d = p - 64
        b = d // 16
        i = d % 16
        inv = 10000.0 ** (-(i % 8) / 8.0)
        pos = rows if b == 0 else cols
        ang = (pos * np.float32(inv)).astype(np.float32)
        Cm[p] = np.cos(ang)
        Sm[p] = np.sin(ang) * (-1.0 if i < 8 else 1.0)
        partner = p + 8 if i < 8 else p - 8
        Pm[partner, p] = 1.0
    c["ropeCm"], c["ropeSm"], c["permm"] = Cm, Sm, Pm
    kk = np.arange(128)[:, None]
    qq = np.arange(128)[None, :]
    mprev = np.where(kk >= qq, 0.0, NEG).astype(np.float32)
    mnext = np.where(kk <= qq, 0.0, NEG).astype(np.float32)
    c["mprev"] = np.tile(mprev, (1, 4))
    c["mnext"] = np.tile(mnext, (1, 4))
    k = np.arange(64)[:, None]
    cq = np.arange(64)[None, :]
    cp = 63 - k
    c0 = np.clip(cq - 8, 0, 48)
    ok = (cp >= c0) & (cp < c0 + 16)
    m01 = np.zeros((128, 64), np.float32)
    mneg = np.zeros((128, 64), np.float32)
    m01[:64] = ok
    mneg[:64] = np.where(ok, 0.0, NEG)
    c["na_m01"], c["na_mneg"] = m01, mneg
    J = np.zeros((128, 256), np.float32)
    for kk_ in range(64):
        J[kk_, 63 - kk_] = 1.0
        J[kk_, 128 + 127 - kk_] = 1.0
    c["na_J"] = J
    c["negt"] = np.full((128, 512), NEG, np.float32)
    return c


CONST_SHAPES = {"identf": (128, 128), "onesf": (128, 128), "ropeCg": (128, T), "ropeSg": (128, T),
                "permg": (128, 128), "ropeCm": (128, T), "ropeSm": (128, T), "permm": (128, 128),
                "mprev": (128, 512), "mnext": (128, 512), "na_m01": (128, 64), "na_mneg": (128, 64),
                "na_J": (128, 256), "negt": (128, 512),
                "gmask": (6, 128, 512), "namask": (1024, 1024), "J128": (128, 128), "J2": (128, 128)}

CONST_BF16 = ("namask", "gmask", "J128", "na_J")

IN_SHAPES = {
    "xp": (T, D), "xs": (T, D),
    "cgk": (DEPTH, PAST, 128), "cgv": (DEPTH, PAST, 128), "cnk": (DEPTH, PAST, 256), "cnv": (DEPTH, PAST, 256),
    "cckv": (DEPTH, PAST, 128), "ckr": (DEPTH, PAST, 32), "cvec": (2, D),
    "w_mod": (DEPTH, D, 6 * D), "b_mod": (DEPTH, 6 * D), "g_attn": (DEPTH, D), "g_mlp": (DEPTH, D),
    "w_in": (DEPTH, D, DIN), "w_conv": (DEPTH, 3, 256), "gqa_sink": (DEPTH, 8), "nat": (DEPTH, 4, 31, 128),
    "mla_g_q": (DEPTH, 256), "mla_w_uq": (DEPTH, 256, 384), "mla_g_kv": (DEPTH, 128),
    "mla_w_ukv": (DEPTH, 128, 512), "w_branch_conv": (DEPTH, 256, D), "w_branch_gqa": (DEPTH, 512, D),
    "w_branch_na": (DEPTH, 256, D), "w_branch_mla": (DEPTH, 256, D), "w_o": (DEPTH, D, D),
    "w_ff1": (DEPTH, D, 4 * D), "w_ff2": (DEPTH, 4 * D, D), "g_final": (D,),
}
OUT_SHAPES = {
    "yp": (T, D), "ys": (T, D), "ogk": (4, DEPTH, 256, 128), "ogv": (4, DEPTH, 256, 128),
    "onk": (4, DEPTH, 256, 256), "onv": (4, DEPTH, 256, 256), "ockv": (4, DEPTH, 256, 128),
    "okr": (4, DEPTH, 256, 32),
}


class _Stop(Exception):
    pass


PREF = 1


def build(debug=None, nlayers=DEPTH, stop=None, plan=None, plan_out=None):
    nc = bass.Bass("TRN2", target_bir_lowering=False)
    P = Prog(nc)
    U = P.U
    I = {n: nc.dram_tensor(n, list(s), F32, kind="ExternalInput").ap() for n, s in IN_SHAPES.items()}
    for n, s in CONST_SHAPES.items():
        I[n] = nc.dram_tensor("c_" + n, list(s), BF16 if n in CONST_BF16 else F32, kind="ExternalInput").ap()
    O = {n: nc.dram_tensor(n, list(s), F32, kind="ExternalOutput").ap() for n, s in OUT_SHAPES.items()}
    DBG = {}
    if debug:
        for n, s in debug.items():
            DBG[n] = nc.dram_tensor("dbg_" + n, list(s), F32, kind="ExternalOutput").ap()

    A = Arena(nc, min(212000, nc.sbuf_bytes_remaining - 512))
    A.on_release = P.set_fence
    banks = [nc.alloc_psum_tensor("bank%d" % i, [128, 512], F32) for i in range(8)]
    bank_i = [0]

    def bank():
        i = bank_i[0]
        bank_i[0] = (i + 1) % 5
        return banks[i], U("bank", i)

    xT = [A.alloc("xT%d" % s, [128, KC, T], F32) for s in range(2)]
    hT = A.alloc("hT", [128, KC, T], BF16)
    NSLOT = 3
    slots = [A.alloc("slot%d" % i, [128, 4096], BF16) for i in range(NSLOT)]
    slot_i = [0]
    identf = A.alloc("identf", [128, 128], F32)
    onesf = A.alloc("onesf", [128, 128], F32)
    identb = A.alloc("identb", [128, 128], BF16)
    onesb = A.alloc("onesb", [128, 128], BF16)
    gT = A.alloc("gT", [128, 2, DEPTH, KC], F32)
    gfin = A.alloc("gfin", [128, KC], F32)
    cvf = A.alloc("cvf", [128, KC, 2], F32)
    scb = A.alloc("scb", [128, KC, 2], BF16)
    bmodL = [A.alloc("bmodT%d" % i, [128, 48], F32) for i in range(2)]
    modL = [A.alloc("mod%d" % i, [128, 48, 2], F32) for i in range(2)]
    A1L = [A.alloc("A1_%d" % i, [128, KC, 2], F32) for i in range(2)]
    A2L = [A.alloc("A2_%d" % i, [128, KC, 2], F32) for i in range(2)]
    wconvT = A.alloc("wconvT", [128, 2, 3], F32)
    esk = A.alloc("esk", [128, 8], F32)
    gq_mla = A.alloc("gq_mla", [128, 2], F32)
    gkv_col = A.alloc("gkv_col", [128, 1], F32)
    gkv_bc = A.alloc("gkv_bc", [128, 128], F32)
    tmpb = [A.alloc("tmpb%d" % i, [128, BL], F32) for i in range(4)]
    rstd = A.alloc("rstd", [128, BL], F32)
    eps_t = A.alloc("eps_t", [128, 1], F32)
    P.op("dve", _mk("memset", eps_t[:], EPS), writes=[U("eps")])
    pTb = [A.alloc("pT%d" % i, [128, BL], BF16) for i in range(4)]
    rot = {"sq": 0, "tmp": 0, "pT": 0}

    def nxt(kind, lst):
        i = rot[kind]
        rot[kind] = (i + 1) % len(lst)
        return lst[i], U(kind, i)

    def slot():
        i = slot_i[0]
        slot_i[0] = (i + 1) % NSLOT
        return slots[i], U("slot", i)

    def wload(dst_ap, src_ap, u):
        P.dma("pool", dst_ap, src_ap, writes=[u])

    req_n = [0]
    issued = [0]

    def _issue(n):
        name, l_, nrows, c0, ncols = plan[n] if plan is not None else plan_out[n]
        i = n % NSLOT
        sl, su = slots[i], U("slot", i)
        kcn = nrows // 128
        slv = sl[:, 0:kcn * ncols].rearrange("p (kc n) -> p kc n", kc=kcn)
        for k0 in range(0, kcn, 8):
            k1 = min(kcn, k0 + 8)
            wload(slv[:, k0:k1, :], I[name][l_][k0 * 128:k1 * 128, c0:c0 + ncols].rearrange("(kc p) n -> p kc n", p=128), su)

    def req_slot(name, l_, nrows, c0, ncols):
        n = req_n[0]
        req_n[0] += 1
        if plan is None:
            plan_out.append((name, l_, nrows, c0, ncols))
        else:
            assert plan[n] == (name, l_, nrows, c0, ncols), (n, plan[n], name, l_, nrows, c0, ncols)
        while issued[0] <= min(n + PREF, (len(plan) - 1) if plan is not None else n):
            _issue(issued[0])
            issued[0] += 1
        i = n % NSLOT
        kcn = nrows // 128
        return slots[i][:, 0:kcn * ncols].rearrange("p (kc n) -> p kc n", kc=kcn), U("slot", i)

    def small_load(dst_ap, src_ap, u, q="sp"):
        P.dma(q, dst_ap, src_ap, writes=[u], allow_slow_non_contiguous=True)

    small_load(identf[:], I["identf"], U("identf"))
    small_load(onesf[:], I["onesf"], U("onesf"))
    P.op("dve", _mk("tensor_copy", out=identb[:], in_=identf[:]), reads=[U("identf")], writes=[U("identb")])
    P.op("dve", _mk("tensor_copy", out=onesb[:], in_=onesf[:]), reads=[U("onesf")], writes=[U("identb")])
    for j, nm in enumerate(("g_attn", "g_mlp")):
        for l in range(DEPTH):
            small_load(gT[:, j, l, :], I[nm][l].rearrange("(kc p) -> p kc", p=128), U("gT"))
    small_load(gfin[:], I["g_final"].rearrange("(kc p) -> p kc", p=128), U("gfin"))
    for s in range(2):
        small_load(cvf[:, :, s], I["cvec"][s].rearrange("(kc p) -> p kc", p=128), U("cvf"))
    P.op("act", _mk("activation", out=scb[:], in_=cvf[:], func=AF.Silu), reads=[U("cvf")], writes=[U("scb")])

    def mm_group(mms, reads, writes):
        n = len(mms)
        for i, (o, l, r, st, sp) in enumerate(mms):
            f = (_mk("matmul", o, l, r, start=st, stop=sp))
            if i == n - 1:
                P.op("pe", f, reads=reads, writes=writes, mark=True)
            elif i == 0:
                P.op("pe", f, reads=reads, writes=writes, mark=False)
            else:
                P.op("pe", f, mark=False)

    def mm_first_wait(reads, writes):
        pass

    def load_x(s, src):
        m = A.mark()
        stage = [A.alloc("xstage%d" % i, [128, D], F32) for i in range(2)]
        for tt in range(8):
            st, su = stage[tt % 2], U("xstage", tt % 2)
            P.dma("sp", st[:], src[tt * 128:(tt + 1) * 128, :], writes=[su])
            for half in range(2):
                bk, bu = bank()
                for q in range(4):
                    kc = half * 4 + q
                    P.op("pe", _mk("transpose",
                        bk[:, q * 128:(q + 1) * 128], st[:, kc * 128:(kc + 1) * 128], identf[:]),
                        reads=[su, U("identf")], writes=[bu], mark=(q == 3))
                eng = "act" if half == 0 else "dve"
                dst = xT[s][:, half * 4:half * 4 + 4, tt * 128:(tt + 1) * 128]
                srcp = bk[:].rearrange("p (a b) -> p a b", a=4)
                if eng == "act":
                    P.op("act", _mk("activation", out=dst, in_=srcp, func=AF.Copy),
                         reads=[bu], writes=[U("x", s, tt // 4)])
                else:
                    P.op("dve", _mk("tensor_copy", out=dst, in_=srcp),
                         reads=[bu], writes=[U("x", s, tt // 4)])
        A.release(m)

    load_x(0, I["xp"])

    def adaln_piece(l, j):
        lp = l % 2
        mod, bmodT = modL[lp], bmodL[lp]
        if j == 0:
            for j6 in range(6):
                small_load(bmodT[:, j6 * 8:(j6 + 1) * 8], I["b_mod"][l][j6 * 1024:(j6 + 1) * 1024].rearrange("(c p) -> p c", p=128),
                           U("bmodT", lp))
        slv, su = req_slot("w_mod", l, D, j * 512, 512)
        bk, bu = bank()
        for n in range(4):
            mm_group([(bk[:, n * 2:n * 2 + 2], slv[:, kc, n * 128:(n + 1) * 128], scb[:, kc, :], kc == 0, kc == KC - 1)
                      for kc in range(KC)], reads=[su, U("scb")], writes=[bu])
        P.op("dve", _mk("tensor_tensor", out=mod[:, j * 4:j * 4 + 4, :], in0=bk[:, 0:8].rearrange("p (c s) -> p c s", s=2),
                        in1=bmodT[:, j * 4:j * 4 + 4].unsqueeze(2).broadcast_to([128, 4, 2]), op=ALU.add),
             reads=[bu, U("bmodT", lp)], writes=[U("mod", lp)])
        for (Ax, jj, c0, jdone) in ((A1L[lp], 0, 8, 3), (A2L[lp], 1, 32, 9)):
            if j == jdone:
                P.op("dve", _mk("scalar_tensor_tensor",
                    out=Ax[:], in0=mod[:, c0:c0 + 8, :], scalar=1.0,
                    in1=gT[:, jj, l, :].unsqueeze(2).broadcast_to([128, KC, 2]), op0=ALU.add, op1=ALU.mult),
                    reads=[U("mod", lp), U("gT")], writes=[U("A12", lp)])

    ncall = [0]

    def norm_stats(src_fn, nchunks, blk, src_units, scale):
        bk, bu = bank()
        for kc in range(nchunks):
            sq, squ = nxt("pT", pTb)
            src = src_fn(kc)
            P.op("act", _mk("activation", out=sq[:], in_=src, func=AF.Square),
                 reads=src_units(kc), writes=[squ])
            P.op("pe", _mk("matmul", bk[:], onesb[:], sq[:], start=(kc == 0), stop=(kc == nchunks - 1)),
                 reads=[squ, U("identb")], writes=[bu], mark=True)
        P.op("act", _mk("activation", out=rstd[:], in_=bk[:], func=AF.Ln, bias=eps_t[:, 0:1], scale=scale),
             reads=[bu, U("eps")], writes=[U("rstd")])
        P.op("act", _mk("activation", out=rstd[:], in_=rstd[:], func=AF.Exp, scale=-0.5), reads=[U("rstd")], writes=[U("rstd")])
        if debug and ("rs%d" % ncall[0]) in DBG:
            P.dma("sp", DBG["rs%d" % ncall[0]], rstd[:], reads=[U("rstd")])
        ncall[0] += 1

    def norm_mod(s, Ax, bcol0, mod, lp):
        for blk in range(2):
            norm_stats(lambda kc: xT[s][:, kc, blk * BL:(blk + 1) * BL], KC, blk, lambda kc: [U("x", s, blk)], 1.0 / D)
            for kc in range(KC):
                tm, tu = nxt("tmp", tmpb)
                P.op("dve", _mk("scalar_tensor_tensor",
                    out=tm[:], in0=xT[s][:, kc, blk * BL:(blk + 1) * BL], scalar=Ax[:, kc, s:s + 1], in1=rstd[:],
                    op0=ALU.mult, op1=ALU.mult), reads=[U("x", s, blk), U("A12", lp), U("rstd")], writes=[tu])
                P.op("act", _mk("activation",
                    out=hT[:, kc, blk * BL:(blk + 1) * BL], in_=tm[:], func=AF.Identity,
                    bias=mod[:, bcol0 + kc, s:s + 1], scale=1.0), reads=[tu, U("mod", lp)], writes=[U("h", blk)])

    def proj(slv, c0, ncols, blk, su, out_rows=None):
        bk, bu = bank()
        mm_group([(bk[0:ncols, :], slv[:, kc, c0:c0 + ncols], hT[:, kc, blk * BL:(blk + 1) * BL], kc == 0, kc == KC - 1)
                  for kc in range(KC)], reads=[su, U("h", blk)], writes=[bu])
        return bk, bu

    def load_win(l, c0, ncols):
        return req_slot("w_in", l, D, c0, ncols)

    def rope(bk, bu, rows, dst, dst_units, blk, Ct, St, Pt, pre_scale):
        xs, xsu = nxt("tmp", tmpb)
        P.op("act", _mk("activation", out=xs[0:rows, :], in_=bk[0:rows, :], func=AF.Identity, scale=pre_scale),
             reads=[bu], writes=[xsu])
        b2, b2u = bank()
        P.op("pe", _mk("matmul", b2[0:rows, :], Pt[0:rows, 0:rows], xs[0:rows, :], start=True, stop=True),
             reads=[xsu, U("ropeP")], writes=[b2u])
        t2, t2u = nxt("tmp", tmpb)
        P.op("dve", _mk("tensor_tensor", out=t2[0:rows, :], in0=b2[0:rows, :], in1=St[0:rows, blk * BL:(blk + 1) * BL],
                                              op=ALU.mult), reads=[b2u, U("ropeT")], writes=[t2u])
        P.op("dve", _mk("tensor_tensor", out=xs[0:rows, :], in0=xs[0:rows, :], in1=Ct[0:rows, blk * BL:(blk + 1) * BL],
                                               op=ALU.mult), reads=[xsu, U("ropeT")], writes=[xsu])
        P.op("dve", _mk("tensor_tensor", out=dst, in0=xs[0:rows, :], in1=t2[0:rows, :], op=ALU.add),
             reads=[xsu, t2u], writes=dst_units)

    obank_i = [0]
    pend_fin = [None]

    def flush_att():
        if pend_fin[0] is not None:
            pend_fin[0]()
            pend_fin[0] = None
    att_dbg = [0]

    def obank():
        i = 5 + obank_i[0]
        obank_i[0] = (obank_i[0] + 1) % 3
        return banks[i], U("bank", i)

    def attend(q_rhs, N, ktiles, dst, dst_u, scale, rds, sink=None, defer=False):
        ob, obu = obank()
        n = len(ktiles)
        pend = []

        def pv(p):
            i, pt, ptu, v, c0, c1 = p
            P.op("pe", _mk("matmul", ob[:, c0:c1], v, pt[:, c0:c1], start=(i == 0), stop=(i == n - 1)),
                 reads=[ptu] + rds, writes=[obu], mark=True)

        for i, kt in enumerate(ktiles):
            bk, bu = bank()
            bias = kt.get("bias", [])
            c0, c1 = kt.get("cols", (0, N))
            mms = [(bk[:, c0:c1], kt["k"], q_rhs[:, c0:c1], True, len(bias) == 0)]
            r2 = list(rds)
            for bi, (sel, bt, btu) in enumerate(bias):
                mms.append((bk[:, c0:c1], sel, bt[:, c0:c1], False, bi == len(bias) - 1))
                r2.append(btu)
            mm_group(mms, reads=r2, writes=[bu])
            pt, ptu = nxt("pT", pTb)
            P.op("act", _mk("activation", out=pt[:, c0:c1], in_=bk[:, c0:c1], func=AF.Exp, scale=scale),
                 reads=[bu], writes=[ptu])
            if debug and "att_pt" in DBG and att_dbg[0] == 1 and i < 6:
                tmd, tmdu = nxt("tmp", tmpb)
                P.op("dve", _mk("tensor_copy", out=tmd[:], in_=pt[:]), reads=[ptu], writes=[tmdu])
                P.dma("sp", DBG["att_pt"][:, i, :], tmd[:], reads=[tmdu])
                tmd, tmdu = nxt("tmp", tmpb)
                P.op("act", _mk("activation", out=tmd[:], in_=bk[:], func=AF.Copy), reads=[bu], writes=[tmdu])
                P.dma("sp", DBG["att_st"][:, i, :], tmd[:], reads=[tmdu])
            pend.append((i, pt, ptu, kt["v"], c0, c1))
            if len(pend) > 2 and not defer:
                pv(pend.pop(0))
        if defer:
            return lambda: _attend_finish(pend, pv, ob, obu, N, dst, dst_u, sink)
        _attend_finish(pend, pv, ob, obu, N, dst, dst_u, sink)

    def _attend_finish(pend, pv, ob, obu, N, dst, dst_u, sink):
        while pend:
            pv(pend.pop(0))
        if debug and "att_ob" in DBG and att_dbg[0] == 1:
            att_dbg[0] = 2
            tmd, tmdu = nxt("tmp", tmpb)
            P.op("act", _mk("activation", out=tmd[:], in_=ob[:], func=AF.Copy), reads=[obu], writes=[tmdu])
            P.dma("sp", DBG["att_ob"], tmd[:], reads=[tmdu])
        rd, rdu = nxt("tmp", tmpb)
        if sink is not None:
            P.op("act", _mk("activation", out=rd[64:128, 0:N], in_=ob[64:128, 0:N], func=AF.Ln, bias=sink, scale=1.0),
                 reads=[obu, U("esk")], writes=[rdu])
        else:
            P.op("act", _mk("activation", out=rd[64:128, 0:N], in_=ob[64:128, 0:N], func=AF.Ln), reads=[obu], writes=[rdu])
        P.op("act", _mk("activation", out=rd[64:128, 0:N], in_=rd[64:128, 0:N], func=AF.Exp, scale=-1.0), reads=[rdu], writes=[rdu])
        P.op("dve", _mk("tensor_tensor", out=dst, in0=ob[0:64, 0:N], in1=rd[64:128, 0:N], op=ALU.mult),
             reads=[obu, rdu], writes=[dst_u])

    def tok_proj(parts, tt, hu):
        bk, bu = bank()
        off = 0
        for (sv, c0, ncol, su) in parts:
            mm_group([(bk[:, off:off + ncol], hT[:, kc, tt * 128:(tt + 1) * 128], sv[:, kc, c0:c0 + ncol], kc == 0, kc == KC - 1)
                      for kc in range(KC)], reads=[su, hu], writes=[bu])
            off += ncol
        return bk, bu

    def transpose_cache(src_dram, width, dst_fn, stu_name):
        kst = A.alloc("kst", [128, 4, width], F32)
        P.dma("sp", kst[:], src_dram.rearrange("(t p) f -> p t f", p=128), writes=[U(stu_name)])
        for c in range((width + 127) // 128):
            w = min(128, width - c * 128)
            bk, bu = bank()
            for ct in range(4):
                P.op("pe", _mk("transpose", bk[0:w, ct * 128:(ct + 1) * 128], kst[:, ct, c * 128:c * 128 + w], identf[:]),
                     reads=[U(stu_name), U("identf")], writes=[bu], mark=(ct == 3))
            dst_fn(bk, bu, c)

    def dbg_dump(dst, tile, nch, units, c0=0):
        for c in range(nch):
            for blk in range(2):
                tm, tu = nxt("tmp", tmpb)
                P.op("dve", _mk("tensor_copy", out=tm[:], in_=tile[:, c, blk * BL:(blk + 1) * BL]), reads=units, writes=[tu])
                P.dma("sp", dst[:, c0 + c, blk * BL:(blk + 1) * BL], tm[:], reads=[tu])

    def layer_pass(l, s):
        lat = (s == 1)
        NSEQ = 1 if lat else 4
        SL = T // NSEQ
        m_pass = A.mark()
        ycv = A.alloc("ycv", [128, 2, T], BF16)
        ygq = A.alloc("ygq", [128, 4, T], BF16)
        yna = A.alloc("yna", [128, 2, T], BF16)
        yml = A.alloc("yml", [128, 2, T], BF16)
        lp = l % 2
        mod, A1, A2 = modL[lp], A1L[lp], A2L[lp]
        if not lat and l == 0:
            for j_ in range(4):
                adaln_piece(0, j_)
        if debug and "mod" in DBG and l == 0 and s == 0:
            P.dma("sp", DBG["mod"], mod[:], reads=[U("mod", lp)])
        if stop == ("adaln", l, s):
            raise _Stop()
        norm_mod(s, A1, 0, mod, lp)

        if debug and ("h_%d_%d" % (l, s)) in DBG:
            dbg_dump(DBG["h_%d_%d" % (l, s)], hT, KC, [U("h", 0), U("h", 1)])
        if debug and "rstd" in DBG and l == 0 and s == 0:
            P.dma("sp", DBG["rstd"], rstd[:], reads=[U("rstd")])
            P.dma("sp", DBG["A1"], A1[:], reads=[U("A12")])
        if stop == ("norm", l, s):
            raise _Stop()
        for c_ in range(2):
            small_load(wconvT[:, c_, :], I["w_conv"][l][:, c_ * 128:(c_ + 1) * 128].rearrange("k p -> p k"), U("wconvT"))
        s0v, s0u = load_win(l, 0, 512)
        s1v, s1u = load_win(l, 512, 512)
        m = A.mark()
        uc = A.alloc("uc", [128, T], F32)
        yc = A.alloc("yc", [128, T], F32)
        for c in range(2):
            for blk in range(2):
                bcc, bccu = proj(s0v, 256 + c * 128, 128, blk, s0u)
                tm, tu = nxt("tmp", tmpb)
                P.op("act", _mk("activation", out=tm[:], in_=bcc[:], func=AF.Copy), reads=[bccu], writes=[tu])
                bcv, bcvu = proj(s1v, c * 128, 128, blk, s1u)
                P.op("dve", _mk("tensor_tensor",
                    out=uc[:, blk * BL:(blk + 1) * BL], in0=bcv[:], in1=tm[:], op=ALU.mult),
                    reads=[bcvu, tu], writes=[U("uc")])
            ucv = uc[:].rearrange("p (q t) -> p q t", q=NSEQ)
            ycw = yc[:].rearrange("p (q t) -> p q t", q=NSEQ)
            P.op("dve", _mk("tensor_scalar", out=yc[:], in0=uc[:], scalar1=wconvT[:, c, 1:2], scalar2=None, op0=ALU.mult),
                 reads=[U("uc"), U("wconvT")], writes=[U("yc")])
            P.op("dve", _mk("scalar_tensor_tensor",
                out=ycw[:, :, 1:SL], in0=ucv[:, :, 0:SL - 1], scalar=wconvT[:, c, 0:1], in1=ycw[:, :, 1:SL],
                op0=ALU.mult, op1=ALU.add), reads=[U("uc"), U("yc")], writes=[U("yc")])
            P.op("dve", _mk("scalar_tensor_tensor",
                out=ycw[:, :, 0:SL - 1], in0=ucv[:, :, 1:SL], scalar=wconvT[:, c, 2:3], in1=ycw[:, :, 0:SL - 1],
                op0=ALU.mult, op1=ALU.add), reads=[U("uc"), U("yc")], writes=[U("yc")])
            for blk in range(2):
                bcb, bcbu = proj(s0v, c * 128, 128, blk, s0u)
                P.op("dve", _mk("tensor_tensor",
                    out=ycv[:, c, blk * BL:(blk + 1) * BL], in0=bcb[:], in1=yc[:, blk * BL:(blk + 1) * BL], op=ALU.mult),
                    reads=[bcbu, U("yc")], writes=[U("ycv")])
        flush_att()
        A.release(m)

        if stop == ("conv", l, s):
            raise _Stop()
        m = A.mark()
        s2v, s2u = load_win(l, 1024, 512)
        qg = A.alloc("qg", [128, 4, T], BF16)
        kA = A.alloc("kA", [128, T], BF16)
        kB = A.alloc("kB", [128, T], BF16)
        kz = {}
        for key_ in ((0, 0), (1, 1), (1, 0), (0, 1)):
            kz[key_] = A.alloc("kz%d%d" % key_, [128, T], BF16)
            P.op("dve", _mk("memset", kz[key_][:], 0.0), writes=[U("kz")])
        vtg = A.alloc("vtg", [128, 8, 2, 128], BF16)
        P.op("dve", _mk("memset", vtg[:].rearrange("p a b c -> p (a b) c")[:, :, 64:128], 1.0), writes=[U("vtg")])
        if lat:
            ropeC = A.alloc("ropeC", [128, T], F32)
            ropeS = A.alloc("ropeS", [128, T], F32)
            ropeP = A.alloc("ropeP", [128, 128], F32)
            P.dma("sp", ropeC[:], I["ropeCg"], writes=[U("ropeT")])
            P.dma("sp", ropeS[:], I["ropeSg"], writes=[U("ropeT")])
            P.dma("sp", ropeP[:], I["permg"], writes=[U("ropeP")])
            gmk = A.alloc("gmk", [128, 6, 512], BF16)
            P.dma("sp", gmk[:], I["gmask"].rearrange("d p q -> p d q"), writes=[U("gmask")])
            kcz = {}
            for key_ in ((0, 0), (1, 1), (1, 0), (0, 1)):
                kcz[key_] = A.alloc("kcz%d%d" % key_, [128, PAST], BF16)
                P.op("dve", _mk("memset", kcz[key_][:], 0.0), writes=[U("kc")])
            vtgc = A.alloc("vtgc", [128, 4, 2, 128], BF16)
            P.op("dve", _mk("memset", vtgc[:].rearrange("p a b c -> p (a b) c")[:, :, 64:128], 1.0), writes=[U("vtgc")])
            for ct in range(4):
                P.dma("pool", vtgc[:, ct, :, 0:64], I["cgv"][l][ct * 128:(ct + 1) * 128, :].rearrange("p (h d) -> p h d", h=2), writes=[U("vtgc")])

            def kdst(bk, bu, c):
                P.op("act", _mk("activation", out=kcz[(0, 0)][0:64, :], in_=bk[0:64, :], func=AF.Copy), reads=[bu], writes=[U("kc")])
                P.op("act", _mk("activation", out=kcz[(1, 1)][64:128, :], in_=bk[64:128, :], func=AF.Copy), reads=[bu], writes=[U("kc")])
                P.op("dve", _mk("tensor_copy", out=kcz[(1, 0)][0:64, :], in_=bk[64:128, :]), reads=[bu], writes=[U("kc")])
                P.op("dve", _mk("tensor_copy", out=kcz[(0, 1)][64:128, :], in_=bk[0:64, :]), reads=[bu], writes=[U("kc")])
            transpose_cache(I["cgk"][l], 128, kdst, "kst_g")
        small_load(esk[:], I["gqa_sink"][l].partition_broadcast(128), U("esk"))
        P.op("act", _mk("activation", out=esk[:], in_=esk[:], func=AF.Exp), reads=[U("esk")], writes=[U("esk")])
        for blk in range(2):
            for ch in range(4):
                sv, su, c0 = (s1v, s1u, 256 + ch * 128) if ch < 2 else (s2v, s2u, (ch - 2) * 128)
                bk, bu = proj(sv, c0, 128, blk, su)
                dst = qg[:, ch, blk * BL:(blk + 1) * BL]
                if lat:
                    rope(bk, bu, 128, dst, [U("qg")], blk, ropeC, ropeS, ropeP, 0.125)
                else:
                    P.op("act", _mk("activation", out=dst, in_=bk[:], func=AF.Identity, scale=0.125),
                         reads=[bu], writes=[U("qg")])
            bk, bu = proj(s2v, 256, 128, blk, s2u)
            bk2, bu2 = bank()
            mm_group([(bk2[0:64, :], s2v[:, kc, 320:384], hT[:, kc, blk * BL:(blk + 1) * BL], kc == 0, kc == KC - 1) for kc in range(KC)],
                     reads=[s2u, U("h", blk)], writes=[bu2])
            mm_group([(bk2[64:128, :], s2v[:, kc, 256:320], hT[:, kc, blk * BL:(blk + 1) * BL], kc == 0, kc == KC - 1) for kc in range(KC)],
                     reads=[s2u, U("h", blk)], writes=[bu2])
            for (b_, bu_, kX) in ((bk, bu, kA), (bk2, bu2, kB)):
                dst = kX[:, blk * BL:(blk + 1) * BL]
                if lat:
                    rope(b_, bu_, 128, dst, [U("kAB")], blk, ropeC, ropeS, ropeP, 1.0)
                else:
                    P.op("act", _mk("activation", out=dst, in_=b_[:], func=AF.Copy), reads=[bu_], writes=[U("kAB")])
        for (key_, src_, r0_) in (((0, 0), kA, 0), ((1, 1), kA, 64), ((1, 0), kB, 0), ((0, 1), kB, 64)):
            P.op("dve", _mk("tensor_copy", out=kz[key_][r0_:r0_ + 64, :], in_=src_[r0_:r0_ + 64, :]), reads=[U("kAB")], writes=[U("kz")])
        if stop == ("gqa_fm", l, s):
            raise _Stop()
        for tt in range(8):
            if lat:
                bk, bu = tok_proj([(s2v, 384, 128, s2u)], tt, U("h", tt // 4))
                voff = 0
            else:
                bk, bu = tok_proj([(s2v, 256, 256, s2u)], tt, U("h", tt // 4))
                voff = 128
            if not os.environ.get("NOVCOPY"):
              P.op("dve", _mk("tensor_copy",
                out=vtg[:, tt, :, 0:64], in_=bk[:, voff:voff + 128].rearrange("p (h d) -> p h d", h=2)),
                reads=[bu], writes=[U("vtg")])
            if not lat and not os.environ.get("NOOUT"):
                st, stu = nxt("tmp", tmpb)
                P.op("act", _mk("activation", out=st[:, 0:256], in_=bk[:, 0:256], func=AF.Copy), reads=[bu], writes=[stu])
                sq_, r0 = tt // 2, (tt % 2) * 128
                P.dma("sp", O["ogk"][sq_, l, r0:r0 + 128, :], st[:, 0:128], reads=[stu])
                P.dma("sp", O["ogv"][sq_, l, r0:r0 + 128, :], st[:, 128:256], reads=[stu])

        if debug and "qg" in DBG and l == 0 and lat:
            dbg_dump(DBG["qg"], qg, 4, [U("qg")])
            dbg_dump(DBG["kAB"], kA[:].rearrange("p (c t) -> p c t", c=1), 1, [U("kAB")], 0)
            dbg_dump(DBG["kAB"], kB[:].rearrange("p (c t) -> p c t", c=1), 1, [U("kAB")], 1)
            pass
        if stop == ("gqa_proj", l, s):
            raise _Stop()

        def kver(kv, hf, cache=False):
            return (kcz if cache else kz)[(kv, hf)]

        rd_g = [U("qg"), U("kz"), U("vtg")]
        for h in range(8):
            kv, hf, ch = h // 4, h % 2, h // 2
            if not lat:
                for sq_ in range(4):
                    q0 = sq_ * 256
                    kts = [dict(k=kver(kv, hf)[:, q0 + kt * 128:q0 + (kt + 1) * 128], v=vtg[:, sq_ * 2 + kt, kv, :]) for kt in range(2)]
                    fin_ = attend(qg[:, ch, q0:q0 + 256], 256, kts, ygq[hf * 64:(hf + 1) * 64, ch, q0:q0 + 256],
                                  U("ygq"), 1.0, rd_g, sink=esk[64:128, h:h + 1], defer=True)
                    if pend_fin[0] is not None:
                        pend_fin[0]()
                    pend_fin[0] = fin_
                    if l == 0 and 4 <= 4 + h < 12 and sq_ == 0:
                        adaln_piece(0, 4 + h)
            else:
                for u in range(2):
                    if False and l == 0 and h == 1 and u == 0 and att_dbg[0] == 0:
                        att_dbg[0] = 1
                    kts = []
                    for d in range(-1, 5):
                        j = 4 * u + d
                        if 0 <= j < 8:
                            kts.append(dict(k=kver(kv, hf)[:, j * 128:(j + 1) * 128], v=vtg[:, j, kv, :],
                                            bias=[(identb[:], gmk[:, d + 1, :], U("gmask"))],
                                            cols=(max(0, 128 * d - 128), min(512, 128 * d + 256))))
                    for ct in range(4):
                        kts.append(dict(k=kver(kv, hf, True)[:, ct * 128:(ct + 1) * 128], v=vtgc[:, ct, kv, :]))
                    attend(qg[:, ch, u * BL:(u + 1) * BL], 512, kts, ygq[hf * 64:(hf + 1) * 64, ch, u * BL:(u + 1) * BL],
                           U("ygq"), 1.0, rd_g + [U("kc"), U("vtgc"), U("identb")], sink=esk[64:128, h:h + 1])
                    if l + 1 < nlayers and (h * 2 + u) < 12:
                        adaln_piece(l + 1, h * 2 + u)
        flush_att()
        A.release(m)

        if stop == ("gqa", l, s):
            raise _Stop()
        m = A.mark()
        nq = A.alloc("nq", [128, 2, T], BF16)
        nk = A.alloc("nk", [128, 4, T], BF16)
        P.op("dve", _mk("memset", nk[:].rearrange("p a b -> p (a b)"), 0.0), writes=[U("nk")])
        vtn = A.alloc("vtn", [128, 8, 4, 128], BF16)
        P.op("dve", _mk("memset", vtn[:].rearrange("p a b c -> p (a b) c")[:, :, 64:128], 1.0), writes=[U("vtn")])
        s3v, s3u = load_win(l, 1536, 512)
        s4v, s4u = load_win(l, 2048, 512)
        if lat:
            nkc = A.alloc("nkc", [128, 4, PAST], BF16)
            P.op("dve", _mk("memset", nkc[:].rearrange("p a b -> p (a b)"), 0.0), writes=[U("nkc")])
            vtnc = A.alloc("vtnc", [128, 4, 4, 128], BF16)
            P.op("dve", _mk("memset", vtnc[:].rearrange("p a b c -> p (a b) c")[:, :, 64:128], 1.0), writes=[U("vtnc")])
            for ct in range(4):
                P.dma("pool", vtnc[:, ct, :, 0:64], I["cnv"][l][ct * 128:(ct + 1) * 128, :].rearrange("p (h d) -> p h d", h=4), writes=[U("vtnc")])
            def nkdst(bk, bu, c):
                P.op("act", _mk("activation", out=nkc[0:64, 2 * c, :], in_=bk[0:64, :], func=AF.Copy), reads=[bu], writes=[U("nkc")])
                P.op("act", _mk("activation", out=nkc[64:128, 2 * c + 1, :], in_=bk[64:128, :], func=AF.Copy), reads=[bu], writes=[U("nkc")])
            transpose_cache(I["cnk"][l], 256, nkdst, "kst_n")
            J128 = A.alloc("J128", [128, 128], BF16)
            P.dma("sp", J128[:], I["J128"], writes=[U("J128")])
            nbm = {}
            for u_ in range(2):
                for j_ in NA_TILES[u_]:
                    t_ = A.alloc("nbm_%d_%d" % (u_, j_), [128, 512], BF16)
                    jr_ = 7 - j_
                    P.dma("sp", t_[:], I["namask"][jr_ * 128:(jr_ + 1) * 128, u_ * 512:(u_ + 1) * 512], writes=[U("nbm")])
                    nbm[(u_, j_)] = t_
            J2t = A.alloc("J2t", [128, 256], BF16)
            P.dma("sp", J2t[:], I["na_J"], writes=[U("J128")])
            strips = [A.alloc("nstrip%d" % i, [128, 31 * 64], BF16) for i in range(2)]
            for i_ in range(2):
                P.op("dve", _mk("memset", strips[i_][:], 0.0), writes=[U("nstrip", i_)])

            def load_strip(h_):
                st_ = strips[h_ % 2]
                for ph in range(1):
                    for (ra, rb) in ((0, 8), (8, 16), (16, 24), (24, 31)):
                        src = AP(I["nat"].tensor, ((l * 4 + h_) * 31 + ra) * 128, [[1, 64], [128, rb - ra], [1, 64]])
                        P.dma("pool", st_[ph * 64:(ph + 1) * 64, ra * 64:rb * 64].rearrange("p (r c) -> p r c", c=64), src,
                              writes=[U("nstrip", h_ % 2)])
        for blk in range(2):
            for c in range(2):
                bk, bu = proj(s3v, c * 128, 128, blk, s3u)
                P.op("act", _mk("activation", out=nq[:, c, blk * BL:(blk + 1) * BL], in_=bk[:], func=AF.Identity, scale=0.125),
                     reads=[bu], writes=[U("nq")])
                bk, bu = proj(s3v, 256 + c * 128, 128, blk, s3u)
                P.op("dve", _mk("tensor_copy", out=nk[0:64, 2 * c, blk * BL:(blk + 1) * BL], in_=bk[0:64, :]),
                     reads=[bu], writes=[U("nk")])
                P.op("dve", _mk("tensor_copy", out=nk[64:128, 2 * c + 1, blk * BL:(blk + 1) * BL], in_=bk[64:128, :]),
                     reads=[bu], writes=[U("nk")])
        for tt in range(8):
            if lat:
                bk, bu = tok_proj([(s4v, 0, 256, s4u)], tt, U("h", tt // 4))
                voff = 0
            else:
                bk, bu = tok_proj([(s3v, 256, 256, s3u), (s4v, 0, 256, s4u)], tt, U("h", tt // 4))
                voff = 256
            P.op("dve", _mk("tensor_copy",
                out=vtn[:, tt, :, 0:64], in_=bk[:, voff:voff + 256].rearrange("p (h d) -> p h d", h=4)),
                reads=[bu], writes=[U("vtn")])
            if not lat:
                st, stu = nxt("tmp", tmpb)
                P.op("act", _mk("activation", out=st[:], in_=bk[:], func=AF.Copy), reads=[bu], writes=[stu])
                sq_, r0 = tt // 2, (tt % 2) * 128
                P.dma("sp", O["onk"][sq_, l, r0:r0 + 128, :], st[:, 0:256], reads=[stu])
                P.dma("sp", O["onv"][sq_, l, r0:r0 + 128, :], st[:, 256:512], reads=[stu])
        rd_n = [U("nq"), U("nk"), U("vtn")]
        if not lat:
            for h in range(4):
                hf, ch = h % 2, h // 2
                for sq_ in range(4):
                    q0 = sq_ * 256
                    kts = [dict(k=nk[:, h, q0 + kt * 128:q0 + (kt + 1) * 128], v=vtn[:, sq_ * 2 + kt, h, :]) for kt in range(2)]
                    fin_ = attend(nq[:, ch, q0:q0 + 256], 256, kts, yna[hf * 64:(hf + 1) * 64, ch, q0:q0 + 256],
                                  U("yna"), 1.0, rd_n, defer=True)
                    if pend_fin[0] is not None:
                        pend_fin[0]()
                    pend_fin[0] = fin_
        else:
            load_strip(0)
            for h in range(4):
                hf, ch = h % 2, h // 2
                if h + 1 < 4:
                    load_strip(h + 1)
                st_, stu_ = strips[h % 2], U("nstrip", h % 2)
                for u in range(2):
                    kts = []
                    for j in NA_TILES[u]:
                        jr = 7 - j
                        r_lo, r_hi = 2 * jr + 1 + 8 * u, 2 * jr + 8 * u
                        kts.append(dict(k=nk[:, h, j * 128:(j + 1) * 128], v=vtn[:, j, h, :],
                                        bias=[(J128[:], nbm[(u, j)][:], U("nbm")),
                                              (J2t[:, 0:128], st_[:, r_lo * 64:r_lo * 64 + 512], stu_),
                                              (J2t[:, 128:256], st_[:, r_hi * 64:r_hi * 64 + 512], stu_)],
                                        cols=NA_COLS[(u, j)]))
                    for ct in range(4):
                        kts.append(dict(k=nkc[:, h, ct * 128:(ct + 1) * 128], v=vtnc[:, ct, h, :]))
                    attend(nq[:, ch, u * BL:(u + 1) * BL], 512, kts, yna[hf * 64:(hf + 1) * 64, ch, u * BL:(u + 1) * BL],
                           U("yna"), 1.0, rd_n + [U("nkc"), U("vtnc"), U("J128")])
        flush_att()
        A.release(m)

        if debug and "nqk" in DBG and l == 0 and lat:
            dbg_dump(DBG["nqk"], nq, 2, [U("nq")], 0)
        if stop == ("na", l, s):
            raise _Stop()
        m = A.mark()
        s5v, s5u = load_win(l, 2560, 160)
        NK = T + (PAST if lat else 0)
        qn = A.alloc("qn", [128, 2, T], BF16)
        qm = A.alloc("qm", [128, 4, T], BF16)
        km = A.alloc("km", [128, 4, NK], BF16)
        vtm = A.alloc("vtm", [128, NK // 128, 4, 128], BF16)
        ckvT = A.alloc("ckvT", [128, T], BF16)
        wuq = A.alloc("wuq", [128, 2, 384], BF16)
        wukv = A.alloc("wukv", [128, 512], BF16)
        mqf = A.alloc("mqf", [128, 2, BL], F32)
        P.op("dve", _mk("memset", vtm[:].rearrange("p a b c -> p (a b) c")[:, :, 64:128], 1.0), writes=[U("vtm")])
        P.dma("pool", wuq[:], I["mla_w_uq"][l].rearrange("(c p) n -> p c n", p=128), writes=[U("wuq")])
        P.dma("pool", wukv[:], I["mla_w_ukv"][l], writes=[U("wukv")])
        small_load(gq_mla[:], I["mla_g_q"][l].rearrange("(c p) -> p c", p=128), U("gmla"))
        small_load(gkv_col[:], I["mla_g_kv"][l].rearrange("(p o) -> p o", o=1), U("gmla"))
        small_load(gkv_bc[:], I["mla_g_kv"][l].partition_broadcast(128), U("gmla"))
        if lat:
            ropeC = A.alloc("ropeCm", [128, T], F32)
            ropeS = A.alloc("ropeSm", [128, T], F32)
            ropeP = A.alloc("ropePm", [128, 128], F32)
            P.dma("sp", ropeC[:], I["ropeCm"], writes=[U("ropeT")])
            P.dma("sp", ropeS[:], I["ropeSm"], writes=[U("ropeT")])
            P.dma("sp", ropeP[:], I["permm"], writes=[U("ropeP")])
            krt = A.alloc("krt", [128, BL], BF16)
            ckvcT = A.alloc("ckvcT", [128, PAST], BF16)
        for blk in range(2):
            tk = slice(blk * BL, (blk + 1) * BL)
            for c in range(2):
                bk, bu = proj(s4v, 256 + c * 128, 128, blk, s4u)
                P.op("act", _mk("activation", out=mqf[:, c, :], in_=bk[:], func=AF.Copy), reads=[bu], writes=[U("mqf")])
            norm_stats(lambda kc: mqf[:, kc, :], 2, blk, lambda kc: [U("mqf")], 1.0 / 256)
            for c in range(2):
                P.op("dve", _mk("scalar_tensor_tensor", out=qn[:, c, tk], in0=mqf[:, c, :], scalar=gq_mla[:, c:c + 1],
                                                                        in1=rstd[:], op0=ALU.mult, op1=ALU.mult),
                     reads=[U("mqf"), U("gmla"), U("rstd")], writes=[U("qn")])
            for h in range(4):
                bk, bu = bank()
                mm_group([(bk[0:96, :], wuq[:, c, h * 96:(h + 1) * 96], qn[:, c, tk], c == 0, c == 1) for c in range(2)],
                         reads=[U("wuq"), U("qn")], writes=[bu])
                if lat:
                    rope(bk, bu, 96, qm[0:96, h, tk], [U("qm")], blk, ropeC, ropeS, ropeP, 1.0)
                else:
                    P.op("act", _mk("activation", out=qm[0:96, h, tk], in_=bk[0:96, :], func=AF.Copy),
                         reads=[bu], writes=[U("qm")])
            bk, bu = proj(s5v, 0, 128, blk, s5u)
            tm, tu = nxt("tmp", tmpb)
            P.op("act", _mk("activation", out=tm[:], in_=bk[:], func=AF.Copy), reads=[bu], writes=[tu])
            norm_stats(lambda kc: tm[:], 1, blk, lambda kc: [tu], 1.0 / 128)
            P.op("dve", _mk("scalar_tensor_tensor", out=ckvT[:, tk], in0=tm[:], scalar=gkv_col[:, 0:1], in1=rstd[:],
                                                                      op0=ALU.mult, op1=ALU.mult),
                 reads=[tu, U("gmla"), U("rstd")], writes=[U("ckvT")])
            bk, bu = proj(s5v, 64, 96, blk, s5u)
            if lat:
                rope(bk, bu, 96, krt[0:96, :], [U("krt")], blk, ropeC, ropeS, ropeP, 1.0)
                for h in range(4):
                    P.op("dve", _mk("tensor_copy", out=km[64:96, h, tk], in_=krt[64:96, :]), reads=[U("krt")], writes=[U("km")])
            else:
                for h in range(4):
                    P.op("act", _mk("activation", out=km[64:96, h, tk], in_=bk[64:96, :], func=AF.Copy),
                         reads=[bu], writes=[U("km")])
            for h in range(4):
                bk, bu = bank()
                mm_group([(bk[0:64, :], wukv[:, h * 128:h * 128 + 64], ckvT[:, tk], True, True)], reads=[U("wukv"), U("ckvT")], writes=[bu])
                P.op("dve", _mk("tensor_copy", out=km[0:64, h, tk], in_=bk[0:64, :]), reads=[bu], writes=[U("km")])
        for tt in range(8):
            bk, bu = bank()
            mm_group([(bk[:], ckvT[:, tt * 128:(tt + 1) * 128], wukv[:], True, True)], reads=[U("wukv"), U("ckvT")], writes=[bu])
            P.op("dve", _mk("tensor_copy", out=vtm[:, tt, :, 0:64],
                                                             in_=bk[:].rearrange("p (h d) -> p h d", h=4)[:, :, 64:128]),
                 reads=[bu], writes=[U("vtm")])
            if not lat:
                bk, bu = tok_proj([(s5v, 0, 160, s5u)], tt, U("h", tt // 4))
                st, stu = nxt("tmp", tmpb)
                P.op("act", _mk("activation", out=st[:, 256:384], in_=bk[:, 0:128], func=AF.Square, accum_out=st[:, 400:401]),
                     reads=[bu], writes=[stu])
                P.op("act", _mk("activation", out=st[:, 401:402], in_=st[:, 400:401], func=AF.Ln, bias=eps_t[:, 0:1], scale=1.0 / 128),
                     reads=[stu, U("eps")], writes=[stu])
                P.op("act", _mk("activation", out=st[:, 402:403], in_=st[:, 401:402], func=AF.Exp, scale=-0.5), reads=[stu], writes=[stu])
                P.op("dve", _mk("scalar_tensor_tensor", out=st[:, 0:128], in0=bk[:, 0:128], scalar=st[:, 402:403], in1=gkv_bc[:],
                                                                          op0=ALU.mult, op1=ALU.mult), reads=[bu, stu, U("gmla")], writes=[stu])
                P.op("act", _mk("activation", out=st[:, 128:160], in_=bk[:, 128:160], func=AF.Copy), reads=[bu, stu], writes=[stu])
                sq_, r0 = tt // 2, (tt % 2) * 128
                P.dma("sp", O["ockv"][sq_, l, r0:r0 + 128, :], st[:, 0:128], reads=[stu])
                P.dma("sp", O["okr"][sq_, l, r0:r0 + 128, :], st[:, 128:160], reads=[stu])
        if lat:
            transpose_cache(I["cckv"][l], 128, lambda bk, bu, c: P.op(
                "act", _mk("activation", out=ckvcT[:], in_=bk[:], func=AF.Copy), reads=[bu], writes=[U("ckvcT")]), "kst_m")
            for h in range(4):
                bk, bu = bank()
                mm_group([(bk[0:64, :], wukv[:, h * 128:h * 128 + 64], ckvcT[:], True, True)], reads=[U("wukv"), U("ckvcT")], writes=[bu])
                P.op("dve", _mk("tensor_copy", out=km[0:64, h, T:T + PAST], in_=bk[0:64, :]), reads=[bu], writes=[U("km")])
            for ct in range(4):
                bk, bu = bank()
                mm_group([(bk[:], ckvcT[:, ct * 128:(ct + 1) * 128], wukv[:], True, True)], reads=[U("wukv"), U("ckvcT")], writes=[bu])
                P.op("dve", _mk("tensor_copy", out=vtm[:, 8 + ct, :, 0:64],
                                                                 in_=bk[:].rearrange("p (h d) -> p h d", h=4)[:, :, 64:128]),
                     reads=[bu], writes=[U("vtm")])
            krs = A.alloc("krs", [128, 4, 96], F32)
            P.op("dve", _mk("memset", krs[:].rearrange("p a b -> p (a b)"), 0.0), writes=[U("krs")])
            P.dma("sp", krs[:, :, 64:96], I["ckr"][l].rearrange("(t p) f -> p t f", p=128), writes=[U("krs")])
            bk, bu = bank()
            for ct in range(4):
                P.op("pe", _mk("transpose", bk[0:96, ct * 128:(ct + 1) * 128], krs[:, ct, :], identf[:]),
                     reads=[U("krs"), U("identf")], writes=[bu], mark=(ct == 3))
            for h in range(4):
                P.op("act", _mk("activation", out=km[64:96, h, T:T + PAST], in_=bk[64:96, :], func=AF.Copy),
                     reads=[bu], writes=[U("km")])
        rd_m = [U("qm"), U("km"), U("vtm")]
        for h in range(4):
            hf, ch = h % 2, h // 2
            if not lat:
                for sq_ in range(4):
                    q0 = sq_ * 256
                    kts = [dict(k=km[0:96, h, q0 + kt * 128:q0 + (kt + 1) * 128], v=vtm[:, sq_ * 2 + kt, h, :]) for kt in range(2)]
                    fin_ = attend(qm[0:96, h, q0:q0 + 256], 256, kts, yml[hf * 64:(hf + 1) * 64, ch, q0:q0 + 256], U("yml"), MLA_SCALE, rd_m,
                                  defer=True)
                    if pend_fin[0] is not None:
                        pend_fin[0]()
                    pend_fin[0] = fin_
            else:
                for u in range(2):
                    kts = [dict(k=km[0:96, h, j * 128:(j + 1) * 128], v=vtm[:, j, h, :]) for j in range(12)]
                    attend(qm[0:96, h, u * BL:(u + 1) * BL], 512, kts, yml[hf * 64:(hf + 1) * 64, ch, u * BL:(u + 1) * BL],
                           U("yml"), MLA_SCALE, rd_m)
        flush_att()
        A.release(m)

        if debug and ("y_%d_%d" % (l, s)) in DBG:
            dd = DBG["y_%d_%d" % (l, s)]
            for i, (yt, nch, un) in enumerate(((ycv, 2, "ycv"), (ygq, 4, "ygq"), (yna, 2, "yna"), (yml, 2, "yml"))):
                c0 = (0, 2, 6, 8)[i]
                dbg_dump(dd, yt, nch, [U(un)], c0)

        if stop == ("mla", l, s):
            raise _Stop()
        m = A.mark()
        acc = A.alloc("acc", [128, KC, T], F32)
        mrg = A.alloc("mrg", [128, KC, T], BF16)
        wbr = [A.alloc("wbr%d" % i, [128, 4096], BF16) for i in range(2)]
        branches = (("w_branch_conv", 2, ycv, "ycv"), ("w_branch_gqa", 4, ygq, "ygq"), ("w_branch_na", 2, yna, "yna"), ("w_branch_mla", 2, yml, "yml"))
        for b, (wn, kb_n, yb, yu) in enumerate(branches):
            wbt, wsu = wbr[b % 2], U("wbr", b % 2)
            wbv = wbt[:, 0:kb_n * D].rearrange("p (kc n) -> p kc n", kc=kb_n)
            wload(wbv, I[wn][l].rearrange("(kc p) n -> p kc n", p=128), wsu)
            for mg in range(2):
                gsv, gsu = load_win(l, 2720 + b * D + mg * 512, 512)
                for mi in range(4):
                    mo = mg * 4 + mi
                    for blk in range(2):
                        tk = slice(blk * BL, (blk + 1) * BL)
                        gb, gbu = proj(gsv, mi * 128, 128, blk, gsu)
                        sg, sgu = nxt("tmp", tmpb)
                        P.op("act", _mk("activation", out=sg[:], in_=gb[:], func=AF.Sigmoid), reads=[gbu], writes=[sgu])
                        bb, bbu = bank()
                        mm_group([(bb[:], wbv[:, kb, mo * 128:(mo + 1) * 128], yb[:, kb, tk], kb == 0, kb == kb_n - 1) for kb in range(kb_n)],
                                 reads=[wsu, U(yu)], writes=[bbu])
                        if b == 0:
                            P.op("dve", _mk("tensor_tensor", out=acc[:, mo, tk], in0=bb[:], in1=sg[:], op=ALU.mult),
                                 reads=[bbu, sgu], writes=[U("acc", mo, blk)])
                        else:
                            P.op("dve", _mk("tensor_tensor", out=sg[:], in0=bb[:], in1=sg[:], op=ALU.mult),
                                 reads=[bbu, sgu], writes=[sgu])
                            if b < 3:
                                P.op("dve", _mk("tensor_tensor", out=acc[:, mo, tk], in0=acc[:, mo, tk], in1=sg[:], op=ALU.add),
                                     reads=[sgu, U("acc", mo, blk)], writes=[U("acc", mo, blk)])
                            else:
                                P.op("dve", _mk("tensor_tensor", out=mrg[:, mo, tk], in0=acc[:, mo, tk], in1=sg[:], op=ALU.add),
                                     reads=[sgu, U("acc", mo, blk)], writes=[U("mrg", blk)])
        for j in range(2):
            slv, su = req_slot("w_o", l, D, j * 512, 512)
            for mi in range(4):
                mo = j * 4 + mi
                for blk in range(2):
                    tk = slice(blk * BL, (blk + 1) * BL)
                    bk, bu = bank()
                    mm_group([(bk[:], slv[:, kc, mi * 128:(mi + 1) * 128], mrg[:, kc, tk], kc == 0, kc == KC - 1) for kc in range(KC)],
                             reads=[su, U("mrg", blk)], writes=[bu])
                    P.op("dve", _mk("scalar_tensor_tensor",
                        out=xT[s][:, mo, tk], in0=bk[:], scalar=mod[:, 16 + mo, s:s + 1], in1=xT[s][:, mo, tk], op0=ALU.mult, op1=ALU.add),
                        reads=[bu, U("mod", lp), U("x", s, blk)], writes=[U("x", s, blk)])
        A.release(m_pass)

        if debug and ("xa_%d_%d" % (l, s)) in DBG:
            P.dma("sp", DBG["xa_%d_%d" % (l, s)], xT[s][:], reads=[U("x", s, 0), U("x", s, 1)])
        if stop == ("attn", l, s):
            raise _Stop()
        if l == 0 and not lat:
            load_x(1, I["xs"])
        norm_mod(s, A2, 24, mod, lp)
        m = A.mark()
        hid = A.alloc("hid", [128, 32, T], BF16)
        for j in range(8):
            slv, su = req_slot("w_ff1", l, D, j * 512, 512)
            for mi in range(4):
                hc = j * 4 + mi
                for blk in range(2):
                    tk = slice(blk * BL, (blk + 1) * BL)
                    bk, bu = proj(slv, mi * 128, 128, blk, su)
                    r_, ru = nxt("tmp", tmpb)
                    P.op("act", _mk("activation", out=r_[:], in_=bk[:], func=AF.Relu), reads=[bu], writes=[ru])
                    eng = "dve"
                    P.op(eng, _mk("tensor_tensor", out=hid[:, hc, tk], in0=r_[:], in1=r_[:], op=ALU.mult),
                         reads=[ru], writes=[U("hid", blk)])
        for mo in range(8):
            slv, su = req_slot("w_ff2", l, 4 * D, mo * 128, 128)
            for blk in range(2):
                tk = slice(blk * BL, (blk + 1) * BL)
                bk, bu = bank()
                mm_group([(bk[:], slv[:, kc, :], hid[:, kc, tk], kc == 0, kc == 31) for kc in range(32)], reads=[su, U("hid", blk)], writes=[bu])
                P.op("dve", _mk("scalar_tensor_tensor",
                    out=xT[s][:, mo, tk], in0=bk[:], scalar=mod[:, 40 + mo, s:s + 1], in1=xT[s][:, mo, tk], op0=ALU.mult, op1=ALU.add),
                    reads=[bu, U("mod", lp), U("x", s, blk)], writes=[U("x", s, blk)])
        flush_att()
        A.release(m)
        if debug and ("x_%d_%d" % (l, s)) in DBG:
            P.dma("sp", DBG["x_%d_%d" % (l, s)], xT[s][:], reads=[U("x", s, 0), U("x", s, 1)])
        if stop == ("end", l, s):
            raise _Stop()

    def final_out(s, dst):
        m = A.mark()
        tf = A.alloc("tfin", [128, KC, BL], F32)
        ost = [A.alloc("ost%d" % i, [128, D], F32) for i in range(2)]
        for blk in range(2):
            norm_stats(lambda kc: xT[s][:, kc, blk * BL:(blk + 1) * BL], KC, blk, lambda kc: [U("x", s, blk)], 1.0 / D)
            for kc in range(KC):
                P.op("dve", _mk("scalar_tensor_tensor", out=tf[:, kc, :], in0=xT[s][:, kc, blk * BL:(blk + 1) * BL],
                                                                   scalar=gfin[:, kc:kc + 1], in1=rstd[:], op0=ALU.mult, op1=ALU.mult),
                     reads=[U("x", s, blk), U("gfin"), U("rstd")], writes=[U("tfin")])
            for ts in range(4):
                tt = blk * 4 + ts
                o_, ou = ost[tt % 2], U("ost", tt % 2)
                for half in range(2):
                    bk, bu = bank()
                    for q in range(4):
                        kc = half * 4 + q
                        P.op("pe", _mk("transpose", bk[:, q * 128:(q + 1) * 128], tf[:, kc, ts * 128:(ts + 1) * 128], identf[:]),
                             reads=[U("tfin"), U("identf")], writes=[bu], mark=(q == 3))
                    if half == 0:
                        P.op("act", _mk("activation", out=o_[:, 0:512], in_=bk[:], func=AF.Copy), reads=[bu], writes=[ou])
                    else:
                        P.op("dve", _mk("tensor_copy", out=o_[:, 512:1024], in_=bk[:]), reads=[bu], writes=[ou])
                P.dma("sp", dst[tt * 128:(tt + 1) * 128, :], o_[:], reads=[ou])
        flush_att()
        A.release(m)

    try:
        if debug and "xin" in DBG:
            P.dma("sp", DBG["xin"], xT[0][:], reads=[U("x", 0, 0), U("x", 0, 1)])
        if stop == ("load", 0, 0):
            raise _Stop()
        for l in range(nlayers):
            layer_pass(l, 0)
            layer_pass(l, 1)
        final_out(0, O["yp"])
        final_out(1, O["ys"])
    except _Stop:
        pass
    P.finish()
    P.emit()
    return nc


def _na_tables():
    r = np.arange(16)
    r0 = np.clip(r - 4, 0, 8)
    c = np.arange(64)
    c0 = np.clip(c - 8, 0, 48)
    rowok = (r[None, :] >= r0[:, None]) & (r[None, :] < r0[:, None] + 8)
    colok = (c[None, :] >= c0[:, None]) & (c[None, :] < c0[:, None] + 16)
    ok = rowok[:, None, :, None] & colok[None, :, None, :]
    ok = ok.reshape(1024, 1024)
    maskT = np.where(ok.T, 0.0, NEG).astype(np.float32)
    mask_rev = np.ascontiguousarray(maskT[::-1, :])
    tiles = []
    cols = {}
    for u in range(2):
        tl = [j for j in range(8) if ok[u * 512:(u + 1) * 512, j * 128:(j + 1) * 128].any()]
        tiles.append(tl)
        for j in tl:
            v = np.where(ok[u * 512:(u + 1) * 512, j * 128:(j + 1) * 128].any(1))[0]
            cols[(u, j)] = (int(v.min()) // 64 * 64, (int(v.max()) // 64 + 1) * 64)
    return mask_rev, tiles, cols


NA_MASK_REV, NA_TILES, NA_COLS = _na_tables()


def _gmask():
    g = np.zeros((6, 128, 512), np.float32)
    kk = np.arange(128)[:, None]
    qq = np.arange(512)[None, :]
    for d in range(-1, 5):
        g[d + 1] = np.where(np.abs(qq - 128 * d - kk) <= 128, 0.0, NEG)
    return g


_PROG = {}


def kernel(x_prompt, x_sample, cache_gqa_k, cache_gqa_v, cache_na_k, cache_na_v, cache_mla_ckv,
           cache_mla_krope, c, c_ctx, w_mod, b_mod, g_attn, g_mlp, w_in, w_conv, gqa_sink, na_rpb,
           mla_g_q, mla_w_uq, mla_g_kv, mla_w_ukv, w_branch_conv, w_branch_gqa, w_branch_na,
           w_branch_mla, w_o, w_ff1, w_ff2, g_final, _debug=None, _nlayers=DEPTH):
    f = lambda a: np.ascontiguousarray(np.asarray(a, dtype=np.float32))
    consts = _consts()
    consts["gmask"] = _gmask()
    consts["namask"] = NA_MASK_REV
    J = np.zeros((128, 128), np.float32)
    J[np.arange(128), 127 - np.arange(128)] = 1.0
    consts["J128"] = J
    J2 = np.zeros((128, 128), np.float32)
    for p_ in range(64):
        J2[p_, 63 - p_] = 1.0
        J2[64 + p_, 127 - p_] = 1.0
    consts["J2"] = J2
    rpb = f(na_rpb)
    nat = np.zeros((DEPTH, 4, 31, 128), np.float32)
    nat[:, :, 8:23, 48:79] = rpb[:, :, ::-1, ::-1]
    shared = {
        "w_mod": f(w_mod), "b_mod": f(b_mod), "g_attn": f(g_attn), "g_mlp": f(g_mlp), "w_in": f(w_in),
        "w_conv": f(w_conv), "gqa_sink": f(gqa_sink), "nat": nat, "mla_g_q": f(mla_g_q),
        "mla_w_uq": f(mla_w_uq), "mla_g_kv": f(mla_g_kv), "mla_w_ukv": f(mla_w_ukv),
        "w_branch_conv": f(w_branch_conv), "w_branch_gqa": f(w_branch_gqa), "w_branch_na": f(w_branch_na),
        "w_branch_mla": f(w_branch_mla), "w_o": f(w_o), "w_ff1": f(w_ff1), "w_ff2": f(w_ff2), "g_final": f(g_final),
    }
    import ml_dtypes
    for k_, v_ in consts.items():
        shared["c_" + k_] = np.ascontiguousarray(v_.astype(ml_dtypes.bfloat16)) if k_ in CONST_BF16 else f(v_)
    xp, xs = f(x_prompt), f(x_sample)
    cg = {"cgk": f(cache_gqa_k).reshape(8, DEPTH, PAST, 128), "cgv": f(cache_gqa_v).reshape(8, DEPTH, PAST, 128),
          "cnk": f(cache_na_k).reshape(8, DEPTH, PAST, 256), "cnv": f(cache_na_v).reshape(8, DEPTH, PAST, 256),
          "cckv": f(cache_mla_ckv), "ckr": f(cache_mla_krope)}
    cc, cx = f(c), f(c_ctx)
    in_maps = []
    for i in range(NCORES):
        d = dict(shared)
        d["xp"] = np.ascontiguousarray(xp[4 * i:4 * i + 4].reshape(T, D))
        d["xs"] = np.ascontiguousarray(xs[i])
        for k_, v_ in cg.items():
            d[k_] = np.ascontiguousarray(v_[i])
        d["cvec"] = np.ascontiguousarray(np.stack([cx, cc[i]]))
        in_maps.append(d)
    if _debug == "prep":
        return in_maps
    key = (repr(_debug), _nlayers)
    if key not in _PROG:
        po = []
        build(debug=_debug, nlayers=_nlayers, plan_out=po)
        _PROG[key] = build(debug=_debug, nlayers=_nlayers, plan=po)
    res = run_bass_kernel_spmd(_PROG[key], in_maps, core_ids=list(range(NCORES)))
    R = res.results
    y_prompt = np.concatenate([R[i]["yp"].reshape(4, 256, D) for i in range(NCORES)], axis=0)
    y_sample = np.stack([R[i]["ys"] for i in range(NCORES)], axis=0)
    cat = lambda n, shp: np.concatenate([R[i][n] for i in range(NCORES)], axis=0).reshape(shp)
    outs = (y_prompt.astype(np.float32), y_sample.astype(np.float32),
            cat("ogk", (32, DEPTH, 256, 2, 64)), cat("ogv", (32, DEPTH, 256, 2, 64)),
            cat("onk", (32, DEPTH, 256, 4, 64)), cat("onv", (32, DEPTH, 256, 4, 64)),
            cat("ockv", (32, DEPTH, 256, 128)), cat("okr", (32, DEPTH, 256, 32)))
    if _debug:
        kernel.dbg = [{n: R[i]["dbg_" + n] for n in _debug} for i in range(NCORES)]
    return outs
```

```python
import os
import numpy as np
import concourse.bass as bass
import concourse.mybir as mybir
from concourse.ap import AP
from concourse.bass_utils import run_bass_kernel_spmd

F32 = mybir.dt.float32
F32R = mybir.dt.float32r
BF16 = mybir.dt.bfloat16
AF = mybir.ActivationFunctionType
ALU = mybir.AluOpType

ENGS = ("pe", "act", "dve", "pool", "sp")
NCORES = 8
D = 1024
KC = 8
T = 1024
BL = 512
DEPTH = 2
PAST = 512
DIN = 6816
EPS = 1e-6
NEG = -30000.0
MLA_SCALE = 96 ** -0.5


class Unit:
    __slots__ = ("name", "w", "rs")

    def __init__(self, name):
        self.name = name
        self.w = {}
        self.rs = {}


class Prog:
    NRING = 32

    def __init__(self, nc):
        self.nc = nc
        self.streams = {e: [] for e in ENGS}
        self.cnt = {e: 0 for e in ENGS}
        self.sems = {e: nc.alloc_semaphore("sem_" + e) for e in ENGS}
        self.ring = [nc.alloc_semaphore("dq%d" % i) for i in range(self.NRING)]
        self.ring_val = [0] * self.NRING
        self.ring_rng = {"sp": (0, self.NRING // 2), "act": (0, self.NRING // 2), "pool": (self.NRING // 2, self.NRING)}
        self.ring_pos = {"sp": 0, "act": 0, "pool": 0}
        self.seen = {e: {} for e in ENGS}
        self.units = {}
        self.fence = []

    def U(self, *key):
        u = self.units.get(key)
        if u is None:
            u = Unit(key)
            self.units[key] = u
        return u

    def _sem(self, key):
        return self.sems[key] if isinstance(key, str) else self.ring[key]

    def _deps(self, eng, reads, writes):
        evs = []
        for u in reads:
            for w in u.w.values():
                if not (w[2] == eng and eng == "pe"):
                    evs.append(w)
            if u.name[0] == "bank":
                for r in u.rs.values():
                    if r[2] != eng:
                        evs.append(r)
        for u in writes:
            if u.name[0] not in self.PERSIST:
                for f in self.fence:
                    if f[2] != eng:
                        evs.append(f)
            for w in u.w.values():
                if w[2] != eng:
                    evs.append(w)
            for r in u.rs.values():
                if r[2] != eng:
                    evs.append(r)
        need = {}
        for (k, v, e) in evs:
            if self.seen[eng].get(k, 0) >= v:
                continue
            if need.get(k, 0) < v:
                need[k] = v
        for k, v in need.items():
            self.seen[eng][k] = v
        return list(need.items())

    def op(self, eng, fn, reads=(), writes=(), mark=True):
        waits = self._deps(eng, reads, writes)
        self.streams[eng].append((waits, fn, [(eng, 1)] if mark else []))
        if mark:
            self.cnt[eng] += 1
            ev = (eng, self.cnt[eng], eng)
            for u in reads:
                u.rs[eng] = ev
            for u in writes:
                u.w = {eng: ev}
                u.rs = {}

    def dma(self, q, out_ap, in_ap, reads=(), writes=(), **kw):
        waits = self._deps(q, reads, writes)
        lo, hi = self.ring_rng[q]
        key_ = "pool" if q == "pool" else "sp"
        i = lo + self.ring_pos[key_] % (hi - lo)
        self.ring_pos[key_] += 1
        prev = self.ring_val[i]
        if prev > 0 and self.seen[q].get(i, 0) < prev:
            waits.append((i, prev))
            self.seen[q][i] = prev
        self.ring_val[i] = prev + 16
        ev = (i, prev + 16, "dma")

        def fn(engine, out_ap=out_ap, in_ap=in_ap, kw=kw):
            return engine.dma_start(out=out_ap, in_=in_ap, **kw)

        self.streams[q].append((waits, fn, [(i, 16)]))
        for u in reads:
            u.rs[("d", i)] = ev
        for u in writes:
            u.w[("d", i)] = ev
            u.rs = {}

    PERSIST = frozenset(("identf", "onesf", "identb", "gT", "gfin", "cvf", "scb", "bmodT", "mod", "A12", "wconvT",
                         "esk", "gmla", "rstd", "eps", "x", "h", "slot", "bank", "sq", "tmp", "pT"))

    def set_fence(self):
        ev = [(o, self.cnt[o], o) for o in ENGS if self.cnt[o] > 0]
        ev += [(i, self.ring_val[i], "dma") for i in range(self.NRING) if self.ring_val[i] > 0]
        self.fence = ev

    def barrier(self):
        for e in ENGS:
            waits = []
            for o in ENGS:
                if o != e and self.cnt[o] > 0 and self.seen[e].get(o, 0) < self.cnt[o]:
                    waits.append((o, self.cnt[o]))
                    self.seen[e][o] = self.cnt[o]
            for i in range(self.NRING):
                v = self.ring_val[i]
                if v > 0 and self.seen[e].get(i, 0) < v:
                    waits.append((i, v))
                    self.seen[e][i] = v
            if waits:
                self.streams[e].append((waits, None, []))

    def finish(self):
        for i in range(self.NRING):
            v = self.ring_val[i]
            if v > 0 and self.seen["sp"].get(i, 0) < v:
                self.streams["sp"].append(([(i, v)], None, []))
                self.seen["sp"][i] = v
        for e in ENGS:
            if e != "sp" and self.cnt[e] > 0:
                self.streams["sp"].append(([(e, self.cnt[e])], None, []))

    def emit(self):
        nc = self.nc
        engobj = {"pe": "tensor", "act": "scalar", "dve": "vector", "pool": "gpsimd", "sp": "sync"}
        with nc.Block() as block:
            for e in ENGS:
                stream = self.streams[e]

                def body(engine, stream=stream):
                    for waits, fn, incs in stream:
                        for k, v in waits:
                            engine.wait_ge(self._sem(k), v)
                        if fn is None:
                            continue
                        ins = fn(engine)
                        for k, n in incs:
                            ins.then_inc(self._sem(k), n)

                getattr(block, engobj[e])(body)


def _mk(name, *a, **k):
    return lambda e: getattr(e, name)(*a, **k)


class Arena:
    def __init__(self, nc, nbytes):
        self.nc = nc
        self.start, self.end = nc.bump_sbuf(nbytes)
        self.cur = self.start
        self.n = 0
        self.on_release = None

    def alloc(self, name, shape, dt):
        nb = int(np.prod(shape[1:])) * (4 if dt == F32 else 2)
        off = (self.cur + 63) // 64 * 64
        assert off + nb <= self.end, ("SBUF arena overflow", name, off + nb - self.end)
        self.cur = off + nb
        self.n += 1
        return self.nc.alloc_sbuf_tensor_at("%s_%d" % (name, self.n), list(shape), dt, offset=off)

    def mark(self):
        return self.cur

    def release(self, m):
        self.cur = m
        if self.on_release is not None:
            self.on_release()


def _consts():
    c = {}
    c["identf"] = np.eye(128, dtype=np.float32)
    c["onesf"] = np.ones((128, 128), np.float32)
    t = np.arange(T)
    rows = (t // 64).astype(np.float32)
    cols = (t % 64).astype(np.float32)
    Cg = np.zeros((128, T), np.float32)
    Sg = np.zeros((128, T), np.float32)
    Pg = np.zeros((128, 128), np.float32)
    for p in range(128):
        d = p % 64
        b = d // 32
        i = d % 32
        inv = 10000.0 ** (-(i % 16) / 16.0)
        pos = rows if b == 0 else cols
        ang = (pos * np.float32(inv)).astype(np.float32)
        Cg[p] = np.cos(ang)
        Sg[p] = np.sin(ang) * (-1.0 if i < 16 else 1.0)
        partner = p + 16 if i < 16 else p - 16
        Pg[partner, p] = 1.0
    c["ropeCg"], c["ropeSg"], c["permg"] = Cg, Sg, Pg
    Cm = np.ones((128, T), np.float32)
    Sm = np.zeros((128, T), np.float32)
    Pm = np.zeros((128, 128), np.float32)
    for p in range(64):
        Pm[p, p] = 1.0
    for p in range(64, 96):
        d = p - 64
        b = d // 16
        i = d % 16
        inv = 10000.0 ** (-(i % 8) / 8.0)
        pos = rows if b == 0 else cols
        ang = (pos * np.float32(inv)).astype(np.float32)
        Cm[p] = np.cos(ang)
        Sm[p] = np.sin(ang) * (-1.0 if i < 8 else 1.0)
        partner = p + 8 if i < 8 else p - 8
        Pm[partner, p] = 1.0
    c["ropeCm"], c["ropeSm"], c["permm"] = Cm, Sm, Pm
    kk = np.arange(128)[:, None]
    qq = np.arange(128)[None, :]
    mprev = np.where(kk >= qq, 0.0, NEG).astype(np.float32)
    mnext = np.where(kk <= qq, 0.0, NEG).astype(np.float32)
    c["mprev"] = np.tile(mprev, (1, 4))
    c["mnext"] = np.tile(mnext, (1, 4))
    k = np.arange(64)[:, None]
    cq = np.arange(64)[None, :]
    cp = 63 - k
    c0 = np.clip(cq - 8, 0, 48)
    ok = (cp >= c0) & (cp < c0 + 16)
    m01 = np.zeros((128, 64), np.float32)
    mneg = np.zeros((128, 64), np.float32)
    m01[:64] = ok
    mneg[:64] = np.where(ok, 0.0, NEG)
    c["na_m01"], c["na_mneg"] = m01, mneg
    J = np.zeros((128, 256), np.float32)
    for kk_ in range(64):
        J[kk_, 63 - kk_] = 1.0
        J[kk_, 128 + 127 - kk_] = 1.0
    c["na_J"] = J
    c["negt"] = np.full((128, 512), NEG, np.float32)
    return c


CONST_SHAPES = {"identf": (128, 128), "onesf": (128, 128), "ropeCg": (128, T), "ropeSg": (128, T),
                "permg": (128, 128), "ropeCm": (128, T), "ropeSm": (128, T), "permm": (128, 128),
                "mprev": (128, 512), "mnext": (128, 512), "na_m01": (128, 64), "na_mneg": (128, 64),
                "na_J": (128, 256), "negt": (128, 512),
                "gmask": (6, 128, 512), "namask": (1024, 1024), "J128": (128, 128), "J2": (128, 128)}

CONST_BF16 = ("namask", "gmask", "J128", "na_J")

IN_SHAPES = {
    "xp": (T, D), "xs": (T, D),
    "cgk": (DEPTH, PAST, 128), "cgv": (DEPTH, PAST, 128), "cnk": (DEPTH, PAST, 256), "cnv": (DEPTH, PAST, 256),
    "cckv": (DEPTH, PAST, 128), "ckr": (DEPTH, PAST, 32), "cvec": (2, D),
    "w_mod": (DEPTH, D, 6 * D), "b_mod": (DEPTH, 6 * D), "g_attn": (DEPTH, D), "g_mlp": (DEPTH, D),
    "w_in": (DEPTH, D, DIN), "w_conv": (DEPTH, 3, 256), "gqa_sink": (DEPTH, 8), "nat": (DEPTH, 4, 31, 128),
    "mla_g_q": (DEPTH, 256), "mla_w_uq": (DEPTH, 256, 384), "mla_g_kv": (DEPTH, 128),
    "mla_w_ukv": (DEPTH, 128, 512), "w_branch_conv": (DEPTH, 256, D), "w_branch_gqa": (DEPTH, 512, D),
    "w_branch_na": (DEPTH, 256, D), "w_branch_mla": (DEPTH, 256, D), "w_o": (DEPTH, D, D),
    "w_ff1": (DEPTH, D, 4 * D), "w_ff2": (DEPTH, 4 * D, D), "g_final": (D,),
}
OUT_SHAPES = {
    "yp": (T, D), "ys": (T, D), "ogk": (4, DEPTH, 256, 128), "ogv": (4, DEPTH, 256, 128),
    "onk": (4, DEPTH, 256, 256), "onv": (4, DEPTH, 256, 256), "ockv": (4, DEPTH, 256, 128),
    "okr": (4, DEPTH, 256, 32),
}


class _Stop(Exception):
    pass


PREF = 1


def build(debug=None, nlayers=DEPTH, stop=None, plan=None, plan_out=None):
    nc = bass.Bass("TRN2", target_bir_lowering=False)
    P = Prog(nc)
    U = P.U
    I = {n: nc.dram_tensor(n, list(s), F32, kind="ExternalInput").ap() for n, s in IN_SHAPES.items()}
    for n, s in CONST_SHAPES.items():
        I[n] = nc.dram_tensor("c_" + n, list(s), BF16 if n in CONST_BF16 else F32, kind="ExternalInput").ap()
    O = {n: nc.dram_tensor(n, list(s), F32, kind="ExternalOutput").ap() for n, s in OUT_SHAPES.items()}
    DBG = {}
    if debug:
        for n, s in debug.items():
            DBG[n] = nc.dram_tensor("dbg_" + n, list(s), F32, kind="ExternalOutput").ap()

    A = Arena(nc, min(212000, nc.sbuf_bytes_remaining - 512))
    A.on_release = P.set_fence
    banks = [nc.alloc_psum_tensor("bank%d" % i, [128, 512], F32) for i in range(8)]
    bank_i = [0]

    def bank():
        i = bank_i[0]
        bank_i[0] = (i + 1) % 5
        return banks[i], U("bank", i)

    xT = [A.alloc("xT%d" % s, [128, KC, T], F32) for s in range(2)]
    hT = A.alloc("hT", [128, KC, T], BF16)
    NSLOT = 3
    slots = [A.alloc("slot%d" % i, [128, 4096], BF16) for i in range(NSLOT)]
    slot_i = [0]
    identf = A.alloc("identf", [128, 128], F32)
    onesf = A.alloc("onesf", [128, 128], F32)
    identb = A.alloc("identb", [128, 128], BF16)
    onesb = A.alloc("onesb", [128, 128], BF16)
    gT = A.alloc("gT", [128, 2, DEPTH, KC], F32)
    gfin = A.alloc("gfin", [128, KC], F32)
    cvf = A.alloc("cvf", [128, KC, 2], F32)
    scb = A.alloc("scb", [128, KC, 2], BF16)
    bmodL = [A.alloc("bmodT%d" % i, [128, 48], F32) for i in range(2)]
    modL = [A.alloc("mod%d" % i, [128, 48, 2], F32) for i in range(2)]
    A1L = [A.alloc("A1_%d" % i, [128, KC, 2], F32) for i in range(2)]
    A2L = [A.alloc("A2_%d" % i, [128, KC, 2], F32) for i in range(2)]
    wconvT = A.alloc("wconvT", [128, 2, 3], F32)
    esk = A.alloc("esk", [128, 8], F32)
    gq_mla = A.alloc("gq_mla", [128, 2], F32)
    gkv_col = A.alloc("gkv_col", [128, 1], F32)
    gkv_bc = A.alloc("gkv_bc", [128, 128], F32)
    tmpb = [A.alloc("tmpb%d" % i, [128, BL], F32) for i in range(3)]
    rstd_t = [A.alloc("rstd%d" % i, [128, BL], F32) for i in range(2)]
    rstd = rstd_t[0]
    eps_t = A.alloc("eps_t", [128, 1], F32)
    P.op("dve", _mk("memset", eps_t[:], EPS), writes=[U("eps")])
    pTb = [A.alloc("pT%d" % i, [128, BL], BF16) for i in range(4)]
    rot = {"sq": 0, "tmp": 0, "pT": 0}

    def nxt(kind, lst):
        i = rot[kind]
        rot[kind] = (i + 1) % len(lst)
        return lst[i], U(kind, i)

    def slot():
        i = slot_i[0]
        slot_i[0] = (i + 1) % NSLOT
        return slots[i], U("slot", i)

    def wload(dst_ap, src_ap, u):
        P.dma("pool", dst_ap, src_ap, writes=[u])

    req_n = [0]
    issued = [0]

    def _issue(n):
        name, l_, nrows, c0, ncols = plan[n] if plan is not None else plan_out[n]
        i = n % NSLOT
        sl, su = slots[i], U("slot", i)
        kcn = nrows // 128
        slv = sl[:, 0:kcn * ncols].rearrange("p (kc n) -> p kc n", kc=kcn)
        for k0 in range(0, kcn, 8):
            k1 = min(kcn, k0 + 8)
            wload(slv[:, k0:k1, :], I[name][l_][k0 * 128:k1 * 128, c0:c0 + ncols].rearrange("(kc p) n -> p kc n", p=128), su)

    def req_slot(name, l_, nrows, c0, ncols):
        n = req_n[0]
        req_n[0] += 1
        if plan is None:
            plan_out.append((name, l_, nrows, c0, ncols))
        else:
            assert plan[n] == (name, l_, nrows, c0, ncols), (n, plan[n], name, l_, nrows, c0, ncols)
        while issued[0] <= min(n + PREF, (len(plan) - 1) if plan is not None else n):
            _issue(issued[0])
            issued[0] += 1
        i = n % NSLOT
        kcn = nrows // 128
        return slots[i][:, 0:kcn * ncols].rearrange("p (kc n) -> p kc n", kc=kcn), U("slot", i)

    def small_load(dst_ap, src_ap, u, q="sp"):
        P.dma(q, dst_ap, src_ap, writes=[u], allow_slow_non_contiguous=True)

    small_load(identf[:], I["identf"], U("identf"))
    small_load(onesf[:], I["onesf"], U("onesf"))
    P.op("dve", _mk("tensor_copy", out=identb[:], in_=identf[:]), reads=[U("identf")], writes=[U("identb")])
    P.op("dve", _mk("tensor_copy", out=onesb[:], in_=onesf[:]), reads=[U("onesf")], writes=[U("identb")])
    for j, nm in enumerate(("g_attn", "g_mlp")):
        for l in range(DEPTH):
            small_load(gT[:, j, l, :], I[nm][l].rearrange("(kc p) -> p kc", p=128), U("gT"))
    small_load(gfin[:], I["g_final"].rearrange("(kc p) -> p kc", p=128), U("gfin"))
    for s in range(2):
        small_load(cvf[:, :, s], I["cvec"][s].rearrange("(kc p) -> p kc", p=128), U("cvf"))
    P.op("act", _mk("activation", out=scb[:], in_=cvf[:], func=AF.Silu), reads=[U("cvf")], writes=[U("scb")])

    def mm_group(mms, reads, writes):
        n = len(mms)
        for i, (o, l, r, st, sp) in enumerate(mms):
            f = (_mk("matmul", o, l, r, start=st, stop=sp))
            if i == n - 1:
                P.op("pe", f, reads=reads, writes=writes, mark=True)
            elif i == 0:
                P.op("pe", f, reads=reads, writes=writes, mark=False)
            else:
                P.op("pe", f, mark=False)

    def mm_first_wait(reads, writes):
        pass

    def load_x(s, src):
        m = A.mark()
        stage = [A.alloc("xstage%d" % i, [128, D], F32) for i in range(2)]
        for tt in range(8):
            st, su = stage[tt % 2], U("xstage", tt % 2)
            P.dma("sp", st[:], src[tt * 128:(tt + 1) * 128, :], writes=[su])
            for half in range(2):
                bk, bu = bank()
                for q in range(4):
                    kc = half * 4 + q
                    P.op("pe", _mk("transpose",
                        bk[:, q * 128:(q + 1) * 128], st[:, kc * 128:(kc + 1) * 128], identf[:]),
                        reads=[su, U("identf")], writes=[bu], mark=(q == 3))
                eng = "act" if half == 0 else "dve"
                dst = xT[s][:, half * 4:half * 4 + 4, tt * 128:(tt + 1) * 128]
                srcp = bk[:].rearrange("p (a b) -> p a b", a=4)
                if eng == "act":
                    P.op("act", _mk("activation", out=dst, in_=srcp, func=AF.Copy),
                         reads=[bu], writes=[U("x", s, tt // 4)])
                else:
                    P.op("dve", _mk("tensor_copy", out=dst, in_=srcp),
                         reads=[bu], writes=[U("x", s, tt // 4)])
        A.release(m)

    load_x(0, I["xp"])

    def adaln_piece(l, j):
        lp = l % 2
        mod, bmodT = modL[lp], bmodL[lp]
        if j == 0:
            for j6 in range(6):
                small_load(bmodT[:, j6 * 8:(j6 + 1) * 8], I["b_mod"][l][j6 * 1024:(j6 + 1) * 1024].rearrange("(c p) -> p c", p=128),
                           U("bmodT", lp))
        slv, su = req_slot("w_mod", l, D, j * 512, 512)
        bk, bu = bank()
        for n in range(4):
            mm_group([(bk[:, n * 2:n * 2 + 2], slv[:, kc, n * 128:(n + 1) * 128], scb[:, kc, :], kc == 0, kc == KC - 1)
                      for kc in range(KC)], reads=[su, U("scb")], writes=[bu])
        P.op("dve", _mk("tensor_tensor", out=mod[:, j * 4:j * 4 + 4, :], in0=bk[:, 0:8].rearrange("p (c s) -> p c s", s=2),
                        in1=bmodT[:, j * 4:j * 4 + 4].unsqueeze(2).broadcast_to([128, 4, 2]), op=ALU.add),
             reads=[bu, U("bmodT", lp)], writes=[U("mod", lp)])
        for (Ax, jj, c0, jdone) in ((A1L[lp], 0, 8, 3), (A2L[lp], 1, 32, 9)):
            if j == jdone:
                P.op("dve", _mk("scalar_tensor_tensor",
                    out=Ax[:], in0=mod[:, c0:c0 + 8, :], scalar=1.0,
                    in1=gT[:, jj, l, :].unsqueeze(2).broadcast_to([128, KC, 2]), op0=ALU.add, op1=ALU.mult),
                    reads=[U("mod", lp), U("gT")], writes=[U("A12", lp)])

    ncall = [0]

    def norm_stats(src_fn, nchunks, blk, src_units, scale, rstd=None, ru=None):
        if rstd is None:
            rstd, ru = rstd_t[0], U("rstd", 0)
        bk, bu = bank()
        for kc in range(nchunks):
            sq, squ = nxt("pT", pTb)
            src = src_fn(kc)
            P.op("act", _mk("activation", out=sq[:], in_=src, func=AF.Square),
                 reads=src_units(kc), writes=[squ])
            P.op("pe", _mk("matmul", bk[:], onesb[:], sq[:], start=(kc == 0), stop=(kc == nchunks - 1)),
                 reads=[squ, U("identb")], writes=[bu], mark=True)
        P.op("act", _mk("activation", out=rstd[:], in_=bk[:], func=AF.Ln, bias=eps_t[:, 0:1], scale=scale),
             reads=[bu, U("eps")], writes=[ru])
        P.op("act", _mk("activation", out=rstd[:], in_=rstd[:], func=AF.Exp, scale=-0.5), reads=[ru], writes=[ru])
        if debug and ("rs%d" % ncall[0]) in DBG:
            P.dma("sp", DBG["rs%d" % ncall[0]], rstd[:], reads=[ru])
        ncall[0] += 1

    def norm_mod(s, Ax, bcol0, mod, lp):
        for blk in range(2):
            norm_stats(lambda kc: xT[s][:, kc, blk * BL:(blk + 1) * BL], KC, blk, lambda kc: [U("x", s, blk)], 1.0 / D,
                       rstd_t[blk], U("rstd", blk))
        for blk in range(2):
            for kc in range(KC):
                tm, tu = nxt("tmp", tmpb)
                P.op("dve", _mk("scalar_tensor_tensor",
                    out=tm[:], in0=xT[s][:, kc, blk * BL:(blk + 1) * BL], scalar=Ax[:, kc, s:s + 1], in1=rstd_t[blk][:],
                    op0=ALU.mult, op1=ALU.mult), reads=[U("x", s, blk), U("A12", lp), U("rstd", blk)], writes=[tu])
                P.op("act", _mk("activation",
                    out=hT[:, kc, blk * BL:(blk + 1) * BL], in_=tm[:], func=AF.Identity,
                    bias=mod[:, bcol0 + kc, s:s + 1], scale=1.0), reads=[tu, U("mod", lp)], writes=[U("h", blk)])

    def proj(slv, c0, ncols, blk, su, out_rows=None):
        bk, bu = bank()
        mm_group([(bk[0:ncols, :], slv[:, kc, c0:c0 + ncols], hT[:, kc, blk * BL:(blk + 1) * BL], kc == 0, kc == KC - 1)
                  for kc in range(KC)], reads=[su, U("h", blk)], writes=[bu])
        return bk, bu

    def load_win(l, c0, ncols):
        return req_slot("w_in", l, D, c0, ncols)

    def rope(bk, bu, rows, dst, dst_units, blk, Ct, St, Pt, pre_scale):
        xs, xsu = nxt("tmp", tmpb)
        P.op("act", _mk("activation", out=xs[0:rows, :], in_=bk[0:rows, :], func=AF.Identity, scale=pre_scale),
             reads=[bu], writes=[xsu])
        b2, b2u = bank()
        P.op("pe", _mk("matmul", b2[0:rows, :], Pt[0:rows, 0:rows], xs[0:rows, :], start=True, stop=True),
             reads=[xsu, U("ropeP")], writes=[b2u])
        t2, t2u = nxt("tmp", tmpb)
        P.op("dve", _mk("tensor_tensor", out=t2[0:rows, :], in0=b2[0:rows, :], in1=St[0:rows, blk * BL:(blk + 1) * BL],
                                              op=ALU.mult), reads=[b2u, U("ropeT")], writes=[t2u])
        P.op("dve", _mk("tensor_tensor", out=xs[0:rows, :], in0=xs[0:rows, :], in1=Ct[0:rows, blk * BL:(blk + 1) * BL],
                                               op=ALU.mult), reads=[xsu, U("ropeT")], writes=[xsu])
        P.op("dve", _mk("tensor_tensor", out=dst, in0=xs[0:rows, :], in1=t2[0:rows, :], op=ALU.add),
             reads=[xsu, t2u], writes=dst_units)

    obank_i = [0]
    pend_fin = [None]

    def flush_att():
        if pend_fin[0] is not None:
            pend_fin[0]()
            pend_fin[0] = None
    att_dbg = [0]

    def obank():
        i = 5 + obank_i[0]
        obank_i[0] = (obank_i[0] + 1) % 3
        return banks[i], U("bank", i)

    def attend(q_rhs, N, ktiles, dst, dst_u, scale, rds, sink=None, defer=False):
        ob, obu = obank()
        n = len(ktiles)
        pend = []

        def pv(p):
            i, pt, ptu, v, c0, c1 = p
            P.op("pe", _mk("matmul", ob[:, c0:c1], v, pt[:, c0:c1], start=(i == 0), stop=(i == n - 1)),
                 reads=[ptu] + rds, writes=[obu], mark=True)

        for i, kt in enumerate(ktiles):
            bk, bu = bank()
            bias = kt.get("bias", [])
            c0, c1 = kt.get("cols", (0, N))
            mms = [(bk[:, c0:c1], kt["k"], q_rhs[:, c0:c1], True, len(bias) == 0)]
            r2 = list(rds)
            for bi, (sel, bt, btu) in enumerate(bias):
                mms.append((bk[:, c0:c1], sel, bt[:, c0:c1], False, bi == len(bias) - 1))
                r2.append(btu)
            mm_group(mms, reads=r2, writes=[bu])
            pt, ptu = nxt("pT", pTb)
            P.op("act", _mk("activation", out=pt[:, c0:c1], in_=bk[:, c0:c1], func=AF.Exp, scale=scale),
                 reads=[bu], writes=[ptu])
            if debug and "att_pt" in DBG and att_dbg[0] == 1 and i < 6:
                tmd, tmdu = nxt("tmp", tmpb)
                P.op("dve", _mk("tensor_copy", out=tmd[:], in_=pt[:]), reads=[ptu], writes=[tmdu])
                P.dma("sp", DBG["att_pt"][:, i, :], tmd[:], reads=[tmdu])
                tmd, tmdu = nxt("tmp", tmpb)
                P.op("act", _mk("activation", out=tmd[:], in_=bk[:], func=AF.Copy), reads=[bu], writes=[tmdu])
                P.dma("sp", DBG["att_st"][:, i, :], tmd[:], reads=[tmdu])
            pend.append((i, pt, ptu, kt["v"], c0, c1))
            if len(pend) > 2 and not defer:
                pv(pend.pop(0))
        if defer:
            return lambda: _attend_finish(pend, pv, ob, obu, N, dst, dst_u, sink)
        _attend_finish(pend, pv, ob, obu, N, dst, dst_u, sink)

    def _attend_finish(pend, pv, ob, obu, N, dst, dst_u, sink):
        while pend:
            pv(pend.pop(0))
        if debug and "att_ob" in DBG and att_dbg[0] == 1:
            att_dbg[0] = 2
            tmd, tmdu = nxt("tmp", tmpb)
            P.op("act", _mk("activation", out=tmd[:], in_=ob[:], func=AF.Copy), reads=[obu], writes=[tmdu])
            P.dma("sp", DBG["att_ob"], tmd[:], reads=[tmdu])
        rd, rdu = nxt("tmp", tmpb)
        if sink is not None:
            P.op("act", _mk("activation", out=rd[64:128, 0:N], in_=ob[64:128, 0:N], func=AF.Ln, bias=sink, scale=1.0),
                 reads=[obu, U("esk")], writes=[rdu])
        else:
            P.op("act", _mk("activation", out=rd[64:128, 0:N], in_=ob[64:128, 0:N], func=AF.Ln), reads=[obu], writes=[rdu])
        P.op("act", _mk("activation", out=rd[64:128, 0:N], in_=rd[64:128, 0:N], func=AF.Exp, scale=-1.0), reads=[rdu], writes=[rdu])
        P.op("dve", _mk("tensor_tensor", out=dst, in0=ob[0:64, 0:N], in1=rd[64:128, 0:N], op=ALU.mult),
             reads=[obu, rdu], writes=[dst_u])

    def tok_proj(parts, tt, hu):
        bk, bu = bank()
        off = 0
        for (sv, c0, ncol, su) in parts:
            mm_group([(bk[:, off:off + ncol], hT[:, kc, tt * 128:(tt + 1) * 128], sv[:, kc, c0:c0 + ncol], kc == 0, kc == KC - 1)
                      for kc in range(KC)], reads=[su, hu], writes=[bu])
            off += ncol
        return bk, bu

    def transpose_cache(src_dram, width, dst_fn, stu_name):
        kst = A.alloc("kst", [128, 4, width], F32)
        P.dma("sp", kst[:], src_dram.rearrange("(t p) f -> p t f", p=128), writes=[U(stu_name)])
        for c in range((width + 127) // 128):
            w = min(128, width - c * 128)
            bk, bu = bank()
            for ct in range(4):
                P.op("pe", _mk("transpose", bk[0:w, ct * 128:(ct + 1) * 128], kst[:, ct, c * 128:c * 128 + w], identf[:]),
                     reads=[U(stu_name), U("identf")], writes=[bu], mark=(ct == 3))
            dst_fn(bk, bu, c)

    def dbg_dump(dst, tile, nch, units, c0=0):
        for c in range(nch):
            for blk in range(2):
                tm, tu = nxt("tmp", tmpb)
                P.op("dve", _mk("tensor_copy", out=tm[:], in_=tile[:, c, blk * BL:(blk + 1) * BL]), reads=units, writes=[tu])
                P.dma("sp", dst[:, c0 + c, blk * BL:(blk + 1) * BL], tm[:], reads=[tu])

    def layer_pass(l, s):
        lat = (s == 1)
        NSEQ = 1 if lat else 4
        SL = T // NSEQ
        m_pass = A.mark()
        ycv = A.alloc("ycv", [128, 2, T], BF16)
        ygq = A.alloc("ygq", [128, 4, T], BF16)
        yna = A.alloc("yna", [128, 2, T], BF16)
        yml = A.alloc("yml", [128, 2, T], BF16)
        lp = l % 2
        mod, A1, A2 = modL[lp], A1L[lp], A2L[lp]
        if not lat and l == 0:
            for j_ in range(4):
                adaln_piece(0, j_)
        if debug and "mod" in DBG and l == 0 and s == 0:
            P.dma("sp", DBG["mod"], mod[:], reads=[U("mod", lp)])
        if stop == ("adaln", l, s):
            raise _Stop()
        norm_mod(s, A1, 0, mod, lp)

        if debug and ("h_%d_%d" % (l, s)) in DBG:
            dbg_dump(DBG["h_%d_%d" % (l, s)], hT, KC, [U("h", 0), U("h", 1)])
        if debug and "rstd" in DBG and l == 0 and s == 0:
            P.dma("sp", DBG["rstd"], rstd[:], reads=[U("rstd", 0)])
            P.dma("sp", DBG["A1"], A1[:], reads=[U("A12")])
        if stop == ("norm", l, s):
            raise _Stop()
        for c_ in range(2):
            small_load(wconvT[:, c_, :], I["w_conv"][l][:, c_ * 128:(c_ + 1) * 128].rearrange("k p -> p k"), U("wconvT"))
        s0v, s0u = load_win(l, 0, 512)
        s1v, s1u = load_win(l, 512, 512)
        m = A.mark()
        uc = A.alloc("uc", [128, T], F32)
        yc = A.alloc("yc", [128, T], F32)
        for c in range(2):
            for blk in range(2):
                bcc, bccu = proj(s0v, 256 + c * 128, 128, blk, s0u)
                tm, tu = nxt("tmp", tmpb)
                P.op("act", _mk("activation", out=tm[:], in_=bcc[:], func=AF.Copy), reads=[bccu], writes=[tu])
                bcv, bcvu = proj(s1v, c * 128, 128, blk, s1u)
                P.op("dve", _mk("tensor_tensor",
                    out=uc[:, blk * BL:(blk + 1) * BL], in0=bcv[:], in1=tm[:], op=ALU.mult),
                    reads=[bcvu, tu], writes=[U("uc")])
            ucv = uc[:].rearrange("p (q t) -> p q t", q=NSEQ)
            ycw = yc[:].rearrange("p (q t) -> p q t", q=NSEQ)
            P.op("dve", _mk("tensor_scalar", out=yc[:], in0=uc[:], scalar1=wconvT[:, c, 1:2], scalar2=None, op0=ALU.mult),
                 reads=[U("uc"), U("wconvT")], writes=[U("yc")])
            P.op("dve", _mk("scalar_tensor_tensor",
                out=ycw[:, :, 1:SL], in0=ucv[:, :, 0:SL - 1], scalar=wconvT[:, c, 0:1], in1=ycw[:, :, 1:SL],
                op0=ALU.mult, op1=ALU.add), reads=[U("uc"), U("yc")], writes=[U("yc")])
            P.op("dve", _mk("scalar_tensor_tensor",
                out=ycw[:, :, 0:SL - 1], in0=ucv[:, :, 1:SL], scalar=wconvT[:, c, 2:3], in1=ycw[:, :, 0:SL - 1],
                op0=ALU.mult, op1=ALU.add), reads=[U("uc"), U("yc")], writes=[U("yc")])
            for blk in range(2):
                bcb, bcbu = proj(s0v, c * 128, 128, blk, s0u)
                P.op("dve", _mk("tensor_tensor",
                    out=ycv[:, c, blk * BL:(blk + 1) * BL], in0=bcb[:], in1=yc[:, blk * BL:(blk + 1) * BL], op=ALU.mult),
                    reads=[bcbu, U("yc")], writes=[U("ycv")])
        flush_att()
        A.release(m)

        if stop == ("conv", l, s):
            raise _Stop()
        m = A.mark()
        s2v, s2u = load_win(l, 1024, 512)
        qg = A.alloc("qg", [128, 4, T], BF16)
        kA = A.alloc("kA", [128, T], BF16)
        kB = A.alloc("kB", [128, T], BF16)
        kz = {}
        for key_ in ((0, 0), (1, 1), (1, 0), (0, 1)):
            kz[key_] = A.alloc("kz%d%d" % key_, [128, T], BF16)
            P.op("dve", _mk("memset", kz[key_][:], 0.0), writes=[U("kz")])
        vtg = A.alloc("vtg", [128, 8, 2, 128], BF16)
        P.op("dve", _mk("memset", vtg[:].rearrange("p a b c -> p (a b) c")[:, :, 64:128], 1.0), writes=[U("vtg")])
        if lat:
            ropeC = A.alloc("ropeC", [128, T], F32)
            ropeS = A.alloc("ropeS", [128, T], F32)
            ropeP = A.alloc("ropeP", [128, 128], F32)
            P.dma("sp", ropeC[:], I["ropeCg"], writes=[U("ropeT")])
            P.dma("sp", ropeS[:], I["ropeSg"], writes=[U("ropeT")])
            P.dma("sp", ropeP[:], I["permg"], writes=[U("ropeP")])
            gmk = A.alloc("gmk", [128, 6, 512], BF16)
            P.dma("sp", gmk[:], I["gmask"].rearrange("d p q -> p d q"), writes=[U("gmask")])
            kcz = {}
            for key_ in ((0, 0), (1, 1), (1, 0), (0, 1)):
                kcz[key_] = A.alloc("kcz%d%d" % key_, [128, PAST], BF16)
                P.op("dve", _mk("memset", kcz[key_][:], 0.0), writes=[U("kc")])
            vtgc = A.alloc("vtgc", [128, 4, 2, 128], BF16)
            P.op("dve", _mk("memset", vtgc[:].rearrange("p a b c -> p (a b) c")[:, :, 64:128], 1.0), writes=[U("vtgc")])
            for ct in range(4):
                P.dma("pool", vtgc[:, ct, :, 0:64], I["cgv"][l][ct * 128:(ct + 1) * 128, :].rearrange("p (h d) -> p h d", h=2), writes=[U("vtgc")])

            def kdst(bk, bu, c):
                P.op("act", _mk("activation", out=kcz[(0, 0)][0:64, :], in_=bk[0:64, :], func=AF.Copy), reads=[bu], writes=[U("kc")])
                P.op("act", _mk("activation", out=kcz[(1, 1)][64:128, :], in_=bk[64:128, :], func=AF.Copy), reads=[bu], writes=[U("kc")])
                P.op("dve", _mk("tensor_copy", out=kcz[(1, 0)][0:64, :], in_=bk[64:128, :]), reads=[bu], writes=[U("kc")])
                P.op("dve", _mk("tensor_copy", out=kcz[(0, 1)][64:128, :], in_=bk[0:64, :]), reads=[bu], writes=[U("kc")])
            transpose_cache(I["cgk"][l], 128, kdst, "kst_g")
        small_load(esk[:], I["gqa_sink"][l].partition_broadcast(128), U("esk"))
        P.op("act", _mk("activation", out=esk[:], in_=esk[:], func=AF.Exp), reads=[U("esk")], writes=[U("esk")])
        for blk in range(2):
            for ch in range(4):
                sv, su, c0 = (s1v, s1u, 256 + ch * 128) if ch < 2 else (s2v, s2u, (ch - 2) * 128)
                bk, bu = proj(sv, c0, 128, blk, su)
                dst = qg[:, ch, blk * BL:(blk + 1) * BL]
                if lat:
                    rope(bk, bu, 128, dst, [U("qg")], blk, ropeC, ropeS, ropeP, 0.125)
                else:
                    P.op("act", _mk("activation", out=dst, in_=bk[:], func=AF.Identity, scale=0.125),
                         reads=[bu], writes=[U("qg")])
            bk, bu = proj(s2v, 256, 128, blk, s2u)
            bk2, bu2 = bank()
            mm_group([(bk2[0:64, :], s2v[:, kc, 320:384], hT[:, kc, blk * BL:(blk + 1) * BL], kc == 0, kc == KC - 1) for kc in range(KC)],
                     reads=[s2u, U("h", blk)], writes=[bu2])
            mm_group([(bk2[64:128, :], s2v[:, kc, 256:320], hT[:, kc, blk * BL:(blk + 1) * BL], kc == 0, kc == KC - 1) for kc in range(KC)],
                     reads=[s2u, U("h", blk)], writes=[bu2])
            for (b_, bu_, kX) in ((bk, bu, kA), (bk2, bu2, kB)):
                dst = kX[:, blk * BL:(blk + 1) * BL]
                if lat:
                    rope(b_, bu_, 128, dst, [U("kAB")], blk, ropeC, ropeS, ropeP, 1.0)
                else:
                    P.op("act", _mk("activation", out=dst, in_=b_[:], func=AF.Copy), reads=[bu_], writes=[U("kAB")])
        for (key_, src_, r0_) in (((0, 0), kA, 0), ((1, 1), kA, 64), ((1, 0), kB, 0), ((0, 1), kB, 64)):
            P.op("dve", _mk("tensor_copy", out=kz[key_][r0_:r0_ + 64, :], in_=src_[r0_:r0_ + 64, :]), reads=[U("kAB")], writes=[U("kz")])
        if stop == ("gqa_fm", l, s):
            raise _Stop()
        for tt in range(8):
            if lat:
                bk, bu = tok_proj([(s2v, 384, 128, s2u)], tt, U("h", tt // 4))
                voff = 0
            else:
                bk, bu = tok_proj([(s2v, 256, 256, s2u)], tt, U("h", tt // 4))
                voff = 128
            if not os.environ.get("NOVCOPY"):
              P.op("dve", _mk("tensor_copy",
                out=vtg[:, tt, :, 0:64], in_=bk[:, voff:voff + 128].rearrange("p (h d) -> p h d", h=2)),
                reads=[bu], writes=[U("vtg")])
            if not lat and not os.environ.get("NOOUT"):
                st, stu = nxt("tmp", tmpb)
                P.op("act", _mk("activation", out=st[:, 0:256], in_=bk[:, 0:256], func=AF.Copy), reads=[bu], writes=[stu])
                sq_, r0 = tt // 2, (tt % 2) * 128
                P.dma("sp", O["ogk"][sq_, l, r0:r0 + 128, :], st[:, 0:128], reads=[stu])
                P.dma("sp", O["ogv"][sq_, l, r0:r0 + 128, :], st[:, 128:256], reads=[stu])

        if debug and "qg" in DBG and l == 0 and lat:
            dbg_dump(DBG["qg"], qg, 4, [U("qg")])
            dbg_dump(DBG["kAB"], kA[:].rearrange("p (c t) -> p c t", c=1), 1, [U("kAB")], 0)
            dbg_dump(DBG["kAB"], kB[:].rearrange("p (c t) -> p c t", c=1), 1, [U("kAB")], 1)
            pass
        if stop == ("gqa_proj", l, s):
            raise _Stop()

        def kver(kv, hf, cache=False):
            return (kcz if cache else kz)[(kv, hf)]

        rd_g = [U("qg"), U("kz"), U("vtg")]
        for h in range(8):
            kv, hf, ch = h // 4, h % 2, h // 2
            if not lat:
                for sq_ in range(4):
                    q0 = sq_ * 256
                    kts = [dict(k=kver(kv, hf)[:, q0 + kt * 128:q0 + (kt + 1) * 128], v=vtg[:, sq_ * 2 + kt, kv, :]) for kt in range(2)]
                    fin_ = attend(qg[:, ch, q0:q0 + 256], 256, kts, ygq[hf * 64:(hf + 1) * 64, ch, q0:q0 + 256],
                                  U("ygq"), 1.0, rd_g, sink=esk[64:128, h:h + 1], defer=True)
                    if pend_fin[0] is not None:
                        pend_fin[0]()
                    pend_fin[0] = fin_
                    if l == 0 and 4 <= 4 + h < 12 and sq_ == 0:
                        adaln_piece(0, 4 + h)
            else:
                for u in range(2):
                    if False and l == 0 and h == 1 and u == 0 and att_dbg[0] == 0:
                        att_dbg[0] = 1
                    kts = []
                    for d in range(-1, 5):
                        j = 4 * u + d
                        if 0 <= j < 8:
                            kts.append(dict(k=kver(kv, hf)[:, j * 128:(j + 1) * 128], v=vtg[:, j, kv, :],
                                            bias=[(identb[:], gmk[:, d + 1, :], U("gmask"))],
                                            cols=(max(0, 128 * d - 128), min(512, 128 * d + 256))))
                    for ct in range(4):
                        kts.append(dict(k=kver(kv, hf, True)[:, ct * 128:(ct + 1) * 128], v=vtgc[:, ct, kv, :]))
                    attend(qg[:, ch, u * BL:(u + 1) * BL], 512, kts, ygq[hf * 64:(hf + 1) * 64, ch, u * BL:(u + 1) * BL],
                           U("ygq"), 1.0, rd_g + [U("kc"), U("vtgc"), U("identb")], sink=esk[64:128, h:h + 1])
                    if l + 1 < nlayers and (h * 2 + u) < 12:
                        adaln_piece(l + 1, h * 2 + u)
        flush_att()
        A.release(m)

        if stop == ("gqa", l, s):
            raise _Stop()
        m = A.mark()
        nq = A.alloc("nq", [128, 2, T], BF16)
        nk = A.alloc("nk", [128, 4, T], BF16)
        P.op("dve", _mk("memset", nk[:].rearrange("p a b -> p (a b)"), 0.0), writes=[U("nk")])
        vtn = A.alloc("vtn", [128, 8, 4, 128], BF16)
        P.op("dve", _mk("memset", vtn[:].rearrange("p a b c -> p (a b) c")[:, :, 64:128], 1.0), writes=[U("vtn")])
        s3v, s3u = load_win(l, 1536, 512)
        s4v, s4u = load_win(l, 2048, 512)
        if lat:
            nkc = A.alloc("nkc", [128, 4, PAST], BF16)
            P.op("dve", _mk("memset", nkc[:].rearrange("p a b -> p (a b)"), 0.0), writes=[U("nkc")])
            vtnc = A.alloc("vtnc", [128, 4, 4, 128], BF16)
            P.op("dve", _mk("memset", vtnc[:].rearrange("p a b c -> p (a b) c")[:, :, 64:128], 1.0), writes=[U("vtnc")])
            for ct in range(4):
                P.dma("pool", vtnc[:, ct, :, 0:64], I["cnv"][l][ct * 128:(ct + 1) * 128, :].rearrange("p (h d) -> p h d", h=4), writes=[U("vtnc")])
            def nkdst(bk, bu, c):
                P.op("act", _mk("activation", out=nkc[0:64, 2 * c, :], in_=bk[0:64, :], func=AF.Copy), reads=[bu], writes=[U("nkc")])
                P.op("act", _mk("activation", out=nkc[64:128, 2 * c + 1, :], in_=bk[64:128, :], func=AF.Copy), reads=[bu], writes=[U("nkc")])
            transpose_cache(I["cnk"][l], 256, nkdst, "kst_n")
            J128 = A.alloc("J128", [128, 128], BF16)
            P.dma("sp", J128[:], I["J128"], writes=[U("J128")])
            nbm = {}
            for u_ in range(2):
                for j_ in NA_TILES[u_]:
                    t_ = A.alloc("nbm_%d_%d" % (u_, j_), [128, 512], BF16)
                    jr_ = 7 - j_
                    P.dma("sp", t_[:], I["namask"][jr_ * 128:(jr_ + 1) * 128, u_ * 512:(u_ + 1) * 512], writes=[U("nbm")])
                    nbm[(u_, j_)] = t_
            J2t = A.alloc("J2t", [128, 256], BF16)
            P.dma("sp", J2t[:], I["na_J"], writes=[U("J128")])
            strips = [A.alloc("nstrip%d" % i, [128, 31 * 64], BF16) for i in range(2)]
            for i_ in range(2):
                P.op("dve", _mk("memset", strips[i_][:], 0.0), writes=[U("nstrip", i_)])

            def load_strip(h_):
                st_ = strips[h_ % 2]
                for ph in range(1):
                    for (ra, rb) in ((0, 8), (8, 16), (16, 24), (24, 31)):
                        src = AP(I["nat"].tensor, ((l * 4 + h_) * 31 + ra) * 128, [[1, 64], [128, rb - ra], [1, 64]])
                        P.dma("pool", st_[ph * 64:(ph + 1) * 64, ra * 64:rb * 64].rearrange("p (r c) -> p r c", c=64), src,
                              writes=[U("nstrip", h_ % 2)])
        for blk in range(2):
            for c in range(2):
                bk, bu = proj(s3v, c * 128, 128, blk, s3u)
                P.op("act", _mk("activation", out=nq[:, c, blk * BL:(blk + 1) * BL], in_=bk[:], func=AF.Identity, scale=0.125),
                     reads=[bu], writes=[U("nq")])
                bk, bu = proj(s3v, 256 + c * 128, 128, blk, s3u)
                P.op("dve", _mk("tensor_copy", out=nk[0:64, 2 * c, blk * BL:(blk + 1) * BL], in_=bk[0:64, :]),
                     reads=[bu], writes=[U("nk")])
                P.op("dve", _mk("tensor_copy", out=nk[64:128, 2 * c + 1, blk * BL:(blk + 1) * BL], in_=bk[64:128, :]),
                     reads=[bu], writes=[U("nk")])
        for tt in range(8):
            if lat:
                bk, bu = tok_proj([(s4v, 0, 256, s4u)], tt, U("h", tt // 4))
                voff = 0
            else:
                bk, bu = tok_proj([(s3v, 256, 256, s3u), (s4v, 0, 256, s4u)], tt, U("h", tt // 4))
                voff = 256
            P.op("dve", _mk("tensor_copy",
                out=vtn[:, tt, :, 0:64], in_=bk[:, voff:voff + 256].rearrange("p (h d) -> p h d", h=4)),
                reads=[bu], writes=[U("vtn")])
            if not lat:
                st, stu = nxt("tmp", tmpb)
                P.op("act", _mk("activation", out=st[:], in_=bk[:], func=AF.Copy), reads=[bu], writes=[stu])
                sq_, r0 = tt // 2, (tt % 2) * 128
                P.dma("sp", O["onk"][sq_, l, r0:r0 + 128, :], st[:, 0:256], reads=[stu])
                P.dma("sp", O["onv"][sq_, l, r0:r0 + 128, :], st[:, 256:512], reads=[stu])
        rd_n = [U("nq"), U("nk"), U("vtn")]
        if not lat:
            for h in range(4):
                hf, ch = h % 2, h // 2
                for sq_ in range(4):
                    q0 = sq_ * 256
                    kts = [dict(k=nk[:, h, q0 + kt * 128:q0 + (kt + 1) * 128], v=vtn[:, sq_ * 2 + kt, h, :]) for kt in range(2)]
                    fin_ = attend(nq[:, ch, q0:q0 + 256], 256, kts, yna[hf * 64:(hf + 1) * 64, ch, q0:q0 + 256],
                                  U("yna"), 1.0, rd_n, defer=True)
                    if pend_fin[0] is not None:
                        pend_fin[0]()
                    pend_fin[0] = fin_
        else:
            load_strip(0)
            for h in range(4):
                hf, ch = h % 2, h // 2
                if h + 1 < 4:
                    load_strip(h + 1)
                st_, stu_ = strips[h % 2], U("nstrip", h % 2)
                for u in range(2):
                    kts = []
                    for j in NA_TILES[u]:
                        jr = 7 - j
                        r_lo, r_hi = 2 * jr + 1 + 8 * u, 2 * jr + 8 * u
                        kts.append(dict(k=nk[:, h, j * 128:(j + 1) * 128], v=vtn[:, j, h, :],
                                        bias=[(J128[:], nbm[(u, j)][:], U("nbm")),
                                              (J2t[:, 0:128], st_[:, r_lo * 64:r_lo * 64 + 512], stu_),
                                              (J2t[:, 128:256], st_[:, r_hi * 64:r_hi * 64 + 512], stu_)],
                                        cols=NA_COLS[(u, j)]))
                    for ct in range(4):
                        kts.append(dict(k=nkc[:, h, ct * 128:(ct + 1) * 128], v=vtnc[:, ct, h, :]))
                    attend(nq[:, ch, u * BL:(u + 1) * BL], 512, kts, yna[hf * 64:(hf + 1) * 64, ch, u * BL:(u + 1) * BL],
                           U("yna"), 1.0, rd_n + [U("nkc"), U("vtnc"), U("J128")])
        flush_att()
        A.release(m)

        if debug and "nqk" in DBG and l == 0 and lat:
            dbg_dump(DBG["nqk"], nq, 2, [U("nq")], 0)
        if stop == ("na", l, s):
            raise _Stop()
        m = A.mark()
        s5v, s5u = load_win(l, 2560, 160)
        NK = T + (PAST if lat else 0)
        qn = A.alloc("qn", [128, 2, T], BF16)
        qm = A.alloc("qm", [128, 4, T], BF16)
        km = A.alloc("km", [128, 4, NK], BF16)
        vtm = A.alloc("vtm", [128, NK // 128, 4, 128], BF16)
        ckvT = A.alloc("ckvT", [128, T], BF16)
        wuq = A.alloc("wuq", [128, 2, 384], BF16)
        wukv = A.alloc("wukv", [128, 512], BF16)
        mqf = A.alloc("mqf", [128, 2, BL], F32)
        P.op("dve", _mk("memset", vtm[:].rearrange("p a b c -> p (a b) c")[:, :, 64:128], 1.0), writes=[U("vtm")])
        P.dma("pool", wuq[:], I["mla_w_uq"][l].rearrange("(c p) n -> p c n", p=128), writes=[U("wuq")])
        P.dma("pool", wukv[:], I["mla_w_ukv"][l], writes=[U("wukv")])
        small_load(gq_mla[:], I["mla_g_q"][l].rearrange("(c p) -> p c", p=128), U("gmla"))
        small_load(gkv_col[:], I["mla_g_kv"][l].rearrange("(p o) -> p o", o=1), U("gmla"))
        small_load(gkv_bc[:], I["mla_g_kv"][l].partition_broadcast(128), U("gmla"))
        if lat:
            ropeC = A.alloc("ropeCm", [128, T], F32)
            ropeS = A.alloc("ropeSm", [128, T], F32)
            ropeP = A.alloc("ropePm", [128, 128], F32)
            P.dma("sp", ropeC[:], I["ropeCm"], writes=[U("ropeT")])
            P.dma("sp", ropeS[:], I["ropeSm"], writes=[U("ropeT")])
            P.dma("sp", ropeP[:], I["permm"], writes=[U("ropeP")])
            krt = A.alloc("krt", [128, BL], BF16)
            ckvcT = A.alloc("ckvcT", [128, PAST], BF16)
        for blk in range(2):
            tk = slice(blk * BL, (blk + 1) * BL)
            for c in range(2):
                bk, bu = proj(s4v, 256 + c * 128, 128, blk, s4u)
                P.op("act", _mk("activation", out=mqf[:, c, :], in_=bk[:], func=AF.Copy), reads=[bu], writes=[U("mqf")])
            norm_stats(lambda kc: mqf[:, kc, :], 2, blk, lambda kc: [U("mqf")], 1.0 / 256)
            for c in range(2):
                P.op("dve", _mk("scalar_tensor_tensor", out=qn[:, c, tk], in0=mqf[:, c, :], scalar=gq_mla[:, c:c + 1],
                                                                        in1=rstd[:], op0=ALU.mult, op1=ALU.mult),
                     reads=[U("mqf"), U("gmla"), U("rstd", 0)], writes=[U("qn")])
            for h in range(4):
                bk, bu = bank()
                mm_group([(bk[0:96, :], wuq[:, c, h * 96:(h + 1) * 96], qn[:, c, tk], c == 0, c == 1) for c in range(2)],
                         reads=[U("wuq"), U("qn")], writes=[bu])
                if lat:
                    rope(bk, bu, 96, qm[0:96, h, tk], [U("qm")], blk, ropeC, ropeS, ropeP, 1.0)
                else:
                    P.op("act", _mk("activation", out=qm[0:96, h, tk], in_=bk[0:96, :], func=AF.Copy),
                         reads=[bu], writes=[U("qm")])
            bk, bu = proj(s5v, 0, 128, blk, s5u)
            tm, tu = nxt("tmp", tmpb)
            P.op("act", _mk("activation", out=tm[:], in_=bk[:], func=AF.Copy), reads=[bu], writes=[tu])
            norm_stats(lambda kc: tm[:], 1, blk, lambda kc: [tu], 1.0 / 128)
            P.op("dve", _mk("scalar_tensor_tensor", out=ckvT[:, tk], in0=tm[:], scalar=gkv_col[:, 0:1], in1=rstd[:],
                                                                      op0=ALU.mult, op1=ALU.mult),
                 reads=[tu, U("gmla"), U("rstd", 0)], writes=[U("ckvT")])
            bk, bu = proj(s5v, 64, 96, blk, s5u)
            if lat:
                rope(bk, bu, 96, krt[0:96, :], [U("krt")], blk, ropeC, ropeS, ropeP, 1.0)
                for h in range(4):
                    P.op("dve", _mk("tensor_copy", out=km[64:96, h, tk], in_=krt[64:96, :]), reads=[U("krt")], writes=[U("km")])
            else:
                for h in range(4):
                    P.op("act", _mk("activation", out=km[64:96, h, tk], in_=bk[64:96, :], func=AF.Copy),
                         reads=[bu], writes=[U("km")])
            for h in range(4):
                bk, bu = bank()
                mm_group([(bk[0:64, :], wukv[:, h * 128:h * 128 + 64], ckvT[:, tk], True, True)], reads=[U("wukv"), U("ckvT")], writes=[bu])
                P.op("dve", _mk("tensor_copy", out=km[0:64, h, tk], in_=bk[0:64, :]), reads=[bu], writes=[U("km")])
        for tt in range(8):
            bk, bu = bank()
            mm_group([(bk[:], ckvT[:, tt * 128:(tt + 1) * 128], wukv[:], True, True)], reads=[U("wukv"), U("ckvT")], writes=[bu])
            P.op("dve", _mk("tensor_copy", out=vtm[:, tt, :, 0:64],
                                                             in_=bk[:].rearrange("p (h d) -> p h d", h=4)[:, :, 64:128]),
                 reads=[bu], writes=[U("vtm")])
            if not lat:
                bk, bu = tok_proj([(s5v, 0, 160, s5u)], tt, U("h", tt // 4))
                st, stu = nxt("tmp", tmpb)
                P.op("act", _mk("activation", out=st[:, 256:384], in_=bk[:, 0:128], func=AF.Square, accum_out=st[:, 400:401]),
                     reads=[bu], writes=[stu])
                P.op("act", _mk("activation", out=st[:, 401:402], in_=st[:, 400:401], func=AF.Ln, bias=eps_t[:, 0:1], scale=1.0 / 128),
                     reads=[stu, U("eps")], writes=[stu])
                P.op("act", _mk("activation", out=st[:, 402:403], in_=st[:, 401:402], func=AF.Exp, scale=-0.5), reads=[stu], writes=[stu])
                P.op("dve", _mk("scalar_tensor_tensor", out=st[:, 0:128], in0=bk[:, 0:128], scalar=st[:, 402:403], in1=gkv_bc[:],
                                                                          op0=ALU.mult, op1=ALU.mult), reads=[bu, stu, U("gmla")], writes=[stu])
                P.op("act", _mk("activation", out=st[:, 128:160], in_=bk[:, 128:160], func=AF.Copy), reads=[bu, stu], writes=[stu])
                sq_, r0 = tt // 2, (tt % 2) * 128
                P.dma("sp", O["ockv"][sq_, l, r0:r0 + 128, :], st[:, 0:128], reads=[stu])
                P.dma("sp", O["okr"][sq_, l, r0:r0 + 128, :], st[:, 128:160], reads=[stu])
        if lat:
            transpose_cache(I["cckv"][l], 128, lambda bk, bu, c: P.op(
                "act", _mk("activation", out=ckvcT[:], in_=bk[:], func=AF.Copy), reads=[bu], writes=[U("ckvcT")]), "kst_m")
            for h in range(4):
                bk, bu = bank()
                mm_group([(bk[0:64, :], wukv[:, h * 128:h * 128 + 64], ckvcT[:], True, True)], reads=[U("wukv"), U("ckvcT")], writes=[bu])
                P.op("dve", _mk("tensor_copy", out=km[0:64, h, T:T + PAST], in_=bk[0:64, :]), reads=[bu], writes=[U("km")])
            for ct in range(4):
                bk, bu = bank()
                mm_group([(bk[:], ckvcT[:, ct * 128:(ct + 1) * 128], wukv[:], True, True)], reads=[U("wukv"), U("ckvcT")], writes=[bu])
                P.op("dve", _mk("tensor_copy", out=vtm[:, 8 + ct, :, 0:64],
                                                                 in_=bk[:].rearrange("p (h d) -> p h d", h=4)[:, :, 64:128]),
                     reads=[bu], writes=[U("vtm")])
            krs = A.alloc("krs", [128, 4, 96], F32)
            P.op("dve", _mk("memset", krs[:].rearrange("p a b -> p (a b)"), 0.0), writes=[U("krs")])
            P.dma("sp", krs[:, :, 64:96], I["ckr"][l].rearrange("(t p) f -> p t f", p=128), writes=[U("krs")])
            bk, bu = bank()
            for ct in range(4):
                P.op("pe", _mk("transpose", bk[0:96, ct * 128:(ct + 1) * 128], krs[:, ct, :], identf[:]),
                     reads=[U("krs"), U("identf")], writes=[bu], mark=(ct == 3))
            for h in range(4):
                P.op("act", _mk("activation", out=km[64:96, h, T:T + PAST], in_=bk[64:96, :], func=AF.Copy),
                     reads=[bu], writes=[U("km")])
        rd_m = [U("qm"), U("km"), U("vtm")]
        for h in range(4):
            hf, ch = h % 2, h // 2
            if not lat:
                for sq_ in range(4):
                    q0 = sq_ * 256
                    kts = [dict(k=km[0:96, h, q0 + kt * 128:q0 + (kt + 1) * 128], v=vtm[:, sq_ * 2 + kt, h, :]) for kt in range(2)]
                    fin_ = attend(qm[0:96, h, q0:q0 + 256], 256, kts, yml[hf * 64:(hf + 1) * 64, ch, q0:q0 + 256], U("yml"), MLA_SCALE, rd_m,
                                  defer=True)
                    if pend_fin[0] is not None:
                        pend_fin[0]()
                    pend_fin[0] = fin_
            else:
                for u in range(2):
                    kts = [dict(k=km[0:96, h, j * 128:(j + 1) * 128], v=vtm[:, j, h, :]) for j in range(12)]
                    attend(qm[0:96, h, u * BL:(u + 1) * BL], 512, kts, yml[hf * 64:(hf + 1) * 64, ch, u * BL:(u + 1) * BL],
                           U("yml"), MLA_SCALE, rd_m)
        flush_att()
        A.release(m)

        if debug and ("y_%d_%d" % (l, s)) in DBG:
            dd = DBG["y_%d_%d" % (l, s)]
            for i, (yt, nch, un) in enumerate(((ycv, 2, "ycv"), (ygq, 4, "ygq"), (yna, 2, "yna"), (yml, 2, "yml"))):
                c0 = (0, 2, 6, 8)[i]
                dbg_dump(dd, yt, nch, [U(un)], c0)

        if stop == ("mla", l, s):
            raise _Stop()
        m = A.mark()
        acc = A.alloc("acc", [128, KC, T], F32)
        mrg = A.alloc("mrg", [128, KC, T], BF16)
        wbr = [A.alloc("wbr%d" % i, [128, 4096], BF16) for i in range(2)]
        branches = (("w_branch_conv", 2, ycv, "ycv"), ("w_branch_gqa", 4, ygq, "ygq"), ("w_branch_na", 2, yna, "yna"), ("w_branch_mla", 2, yml, "yml"))
        for b, (wn, kb_n, yb, yu) in enumerate(branches):
            wbt, wsu = wbr[b % 2], U("wbr", b % 2)
            wbv = wbt[:, 0:kb_n * D].rearrange("p (kc n) -> p kc n", kc=kb_n)
            wload(wbv, I[wn][l].rearrange("(kc p) n -> p kc n", p=128), wsu)
            for mg in range(2):
                gsv, gsu = load_win(l, 2720 + b * D + mg * 512, 512)
                for mi in range(4):
                    mo = mg * 4 + mi
                    for blk in range(2):
                        tk = slice(blk * BL, (blk + 1) * BL)
                        gb, gbu = proj(gsv, mi * 128, 128, blk, gsu)
                        sg, sgu = nxt("tmp", tmpb)
                        P.op("act", _mk("activation", out=sg[:], in_=gb[:], func=AF.Sigmoid), reads=[gbu], writes=[sgu])
                        bb, bbu = bank()
                        mm_group([(bb[:], wbv[:, kb, mo * 128:(mo + 1) * 128], yb[:, kb, tk], kb == 0, kb == kb_n - 1) for kb in range(kb_n)],
                                 reads=[wsu, U(yu)], writes=[bbu])
                        if b == 0:
                            P.op("dve", _mk("tensor_tensor", out=acc[:, mo, tk], in0=bb[:], in1=sg[:], op=ALU.mult),
                                 reads=[bbu, sgu], writes=[U("acc", mo, blk)])
                        else:
                            P.op("dve", _mk("tensor_tensor", out=sg[:], in0=bb[:], in1=sg[:], op=ALU.mult),
                                 reads=[bbu, sgu], writes=[sgu])
                            if b < 3:
                                P.op("dve", _mk("tensor_tensor", out=acc[:, mo, tk], in0=acc[:, mo, tk], in1=sg[:], op=ALU.add),
                                     reads=[sgu, U("acc", mo, blk)], writes=[U("acc", mo, blk)])
                            else:
                                P.op("dve", _mk("tensor_tensor", out=mrg[:, mo, tk], in0=acc[:, mo, tk], in1=sg[:], op=ALU.add),
                                     reads=[sgu, U("acc", mo, blk)], writes=[U("mrg", blk)])
        for j in range(2):
            slv, su = req_slot("w_o", l, D, j * 512, 512)
            for mi in range(4):
                mo = j * 4 + mi
                for blk in range(2):
                    tk = slice(blk * BL, (blk + 1) * BL)
                    bk, bu = bank()
                    mm_group([(bk[:], slv[:, kc, mi * 128:(mi + 1) * 128], mrg[:, kc, tk], kc == 0, kc == KC - 1) for kc in range(KC)],
                             reads=[su, U("mrg", blk)], writes=[bu])
                    P.op("dve", _mk("scalar_tensor_tensor",
                        out=xT[s][:, mo, tk], in0=bk[:], scalar=mod[:, 16 + mo, s:s + 1], in1=xT[s][:, mo, tk], op0=ALU.mult, op1=ALU.add),
                        reads=[bu, U("mod", lp), U("x", s, blk)], writes=[U("x", s, blk)])
        A.release(m_pass)

        if debug and ("xa_%d_%d" % (l, s)) in DBG:
            P.dma("sp", DBG["xa_%d_%d" % (l, s)], xT[s][:], reads=[U("x", s, 0), U("x", s, 1)])
        if stop == ("attn", l, s):
            raise _Stop()
        if l == 0 and not lat:
            load_x(1, I["xs"])
        norm_mod(s, A2, 24, mod, lp)
        m = A.mark()
        hid = A.alloc("hid", [128, 32, T], BF16)
        for j in range(8):
            slv, su = req_slot("w_ff1", l, D, j * 512, 512)
            for mi in range(4):
                hc = j * 4 + mi
                for blk in range(2):
                    tk = slice(blk * BL, (blk + 1) * BL)
                    bk, bu = proj(slv, mi * 128, 128, blk, su)
                    r_, ru = nxt("tmp", tmpb)
                    P.op("act", _mk("activation", out=r_[:], in_=bk[:], func=AF.Relu), reads=[bu], writes=[ru])
                    eng = "dve"
                    P.op(eng, _mk("tensor_tensor", out=hid[:, hc, tk], in0=r_[:], in1=r_[:], op=ALU.mult),
                         reads=[ru], writes=[U("hid", blk)])
        for mo in range(8):
            slv, su = req_slot("w_ff2", l, 4 * D, mo * 128, 128)
            for blk in range(2):
                tk = slice(blk * BL, (blk + 1) * BL)
                bk, bu = bank()
                mm_group([(bk[:], slv[:, kc, :], hid[:, kc, tk], kc == 0, kc == 31) for kc in range(32)], reads=[su, U("hid", blk)], writes=[bu])
                P.op("dve", _mk("scalar_tensor_tensor",
                    out=xT[s][:, mo, tk], in0=bk[:], scalar=mod[:, 40 + mo, s:s + 1], in1=xT[s][:, mo, tk], op0=ALU.mult, op1=ALU.add),
                    reads=[bu, U("mod", lp), U("x", s, blk)], writes=[U("x", s, blk)])
        flush_att()
        A.release(m)
        if debug and ("x_%d_%d" % (l, s)) in DBG:
            P.dma("sp", DBG["x_%d_%d" % (l, s)], xT[s][:], reads=[U("x", s, 0), U("x", s, 1)])
        if stop == ("end", l, s):
            raise _Stop()

    def final_out(s, dst):
        m = A.mark()
        tf = A.alloc("tfin", [128, KC, BL], F32)
        ost = [A.alloc("ost%d" % i, [128, D], F32) for i in range(2)]
        for blk in range(2):
            norm_stats(lambda kc: xT[s][:, kc, blk * BL:(blk + 1) * BL], KC, blk, lambda kc: [U("x", s, blk)], 1.0 / D)
            for kc in range(KC):
                P.op("dve", _mk("scalar_tensor_tensor", out=tf[:, kc, :], in0=xT[s][:, kc, blk * BL:(blk + 1) * BL],
                                                                   scalar=gfin[:, kc:kc + 1], in1=rstd[:], op0=ALU.mult, op1=ALU.mult),
                     reads=[U("x", s, blk), U("gfin"), U("rstd", 0)], writes=[U("tfin")])
            for ts in range(4):
                tt = blk * 4 + ts
                o_, ou = ost[tt % 2], U("ost", tt % 2)
                for half in range(2):
                    bk, bu = bank()
                    for q in range(4):
                        kc = half * 4 + q
                        P.op("pe", _mk("transpose", bk[:, q * 128:(q + 1) * 128], tf[:, kc, ts * 128:(ts + 1) * 128], identf[:]),
                             reads=[U("tfin"), U("identf")], writes=[bu], mark=(q == 3))
                    if half == 0:
                        P.op("act", _mk("activation", out=o_[:, 0:512], in_=bk[:], func=AF.Copy), reads=[bu], writes=[ou])
                    else:
                        P.op("dve", _mk("tensor_copy", out=o_[:, 512:1024], in_=bk[:]), reads=[bu], writes=[ou])
                P.dma("sp", dst[tt * 128:(tt + 1) * 128, :], o_[:], reads=[ou])
        flush_att()
        A.release(m)

    try:
        if debug and "xin" in DBG:
            P.dma("sp", DBG["xin"], xT[0][:], reads=[U("x", 0, 0), U("x", 0, 1)])
        if stop == ("load", 0, 0):
            raise _Stop()
        for l in range(nlayers):
            layer_pass(l, 0)
            layer_pass(l, 1)
        final_out(0, O["yp"])
        final_out(1, O["ys"])
    except _Stop:
        pass
    P.finish()
    P.emit()
    return nc


def _na_tables():
    r = np.arange(16)
    r0 = np.clip(r - 4, 0, 8)
    c = np.arange(64)
    c0 = np.clip(c - 8, 0, 48)
    rowok = (r[None, :] >= r0[:, None]) & (r[None, :] < r0[:, None] + 8)
    colok = (c[None, :] >= c0[:, None]) & (c[None, :] < c0[:, None] + 16)
    ok = rowok[:, None, :, None] & colok[None, :, None, :]
    ok = ok.reshape(1024, 1024)
    maskT = np.where(ok.T, 0.0, NEG).astype(np.float32)
    mask_rev = np.ascontiguousarray(maskT[::-1, :])
    tiles = []
    cols = {}
    for u in range(2):
        tl = [j for j in range(8) if ok[u * 512:(u + 1) * 512, j * 128:(j + 1) * 128].any()]
        tiles.append(tl)
        for j in tl:
            v = np.where(ok[u * 512:(u + 1) * 512, j * 128:(j + 1) * 128].any(1))[0]
            cols[(u, j)] = (int(v.min()) // 64 * 64, (int(v.max()) // 64 + 1) * 64)
    return mask_rev, tiles, cols


NA_MASK_REV, NA_TILES, NA_COLS = _na_tables()


def _gmask():
    g = np.zeros((6, 128, 512), np.float32)
    kk = np.arange(128)[:, None]
    qq = np.arange(512)[None, :]
    for d in range(-1, 5):
        g[d + 1] = np.where(np.abs(qq - 128 * d - kk) <= 128, 0.0, NEG)
    return g


_PROG = {}


def kernel(x_prompt, x_sample, cache_gqa_k, cache_gqa_v, cache_na_k, cache_na_v, cache_mla_ckv,
           cache_mla_krope, c, c_ctx, w_mod, b_mod, g_attn, g_mlp, w_in, w_conv, gqa_sink, na_rpb,
           mla_g_q, mla_w_uq, mla_g_kv, mla_w_ukv, w_branch_conv, w_branch_gqa, w_branch_na,
           w_branch_mla, w_o, w_ff1, w_ff2, g_final, _debug=None, _nlayers=DEPTH):
    f = lambda a: np.ascontiguousarray(np.asarray(a, dtype=np.float32))
    consts = _consts()
    consts["gmask"] = _gmask()
    consts["namask"] = NA_MASK_REV
    J = np.zeros((128, 128), np.float32)
    J[np.arange(128), 127 - np.arange(128)] = 1.0
    consts["J128"] = J
    J2 = np.zeros((128, 128), np.float32)
    for p_ in range(64):
        J2[p_, 63 - p_] = 1.0
        J2[64 + p_, 127 - p_] = 1.0
    consts["J2"] = J2
    rpb = f(na_rpb)
    nat = np.zeros((DEPTH, 4, 31, 128), np.float32)
    nat[:, :, 8:23, 48:79] = rpb[:, :, ::-1, ::-1]
    shared = {
        "w_mod": f(w_mod), "b_mod": f(b_mod), "g_attn": f(g_attn), "g_mlp": f(g_mlp), "w_in": f(w_in),
        "w_conv": f(w_conv), "gqa_sink": f(gqa_sink), "nat": nat, "mla_g_q": f(mla_g_q),
        "mla_w_uq": f(mla_w_uq), "mla_g_kv": f(mla_g_kv), "mla_w_ukv": f(mla_w_ukv),
        "w_branch_conv": f(w_branch_conv), "w_branch_gqa": f(w_branch_gqa), "w_branch_na": f(w_branch_na),
        "w_branch_mla": f(w_branch_mla), "w_o": f(w_o), "w_ff1": f(w_ff1), "w_ff2": f(w_ff2), "g_final": f(g_final),
    }
    import ml_dtypes
    for k_, v_ in consts.items():
        shared["c_" + k_] = np.ascontiguousarray(v_.astype(ml_dtypes.bfloat16)) if k_ in CONST_BF16 else f(v_)
    xp, xs = f(x_prompt), f(x_sample)
    cg = {"cgk": f(cache_gqa_k).reshape(8, DEPTH, PAST, 128), "cgv": f(cache_gqa_v).reshape(8, DEPTH, PAST, 128),
          "cnk": f(cache_na_k).reshape(8, DEPTH, PAST, 256), "cnv": f(cache_na_v).reshape(8, DEPTH, PAST, 256),
          "cckv": f(cache_mla_ckv), "ckr": f(cache_mla_krope)}
    cc, cx = f(c), f(c_ctx)
    in_maps = []
    for i in range(NCORES):
        d = dict(shared)
        d["xp"] = np.ascontiguousarray(xp[4 * i:4 * i + 4].reshape(T, D))
        d["xs"] = np.ascontiguousarray(xs[i])
        for k_, v_ in cg.items():
            d[k_] = np.ascontiguousarray(v_[i])
        d["cvec"] = np.ascontiguousarray(np.stack([cx, cc[i]]))
        in_maps.append(d)
    if _debug == "prep":
        return in_maps
    key = (repr(_debug), _nlayers)
    if key not in _PROG:
        po = []
        build(debug=_debug, nlayers=_nlayers, plan_out=po)
        _PROG[key] = build(debug=_debug, nlayers=_nlayers, plan=po)
    res = run_bass_kernel_spmd(_PROG[key], in_maps, core_ids=list(range(NCORES)))
    R = res.results
    y_prompt = np.concatenate([R[i]["yp"].reshape(4, 256, D) for i in range(NCORES)], axis=0)
    y_sample = np.stack([R[i]["ys"] for i in range(NCORES)], axis=0)
    cat = lambda n, shp: np.concatenate([R[i][n] for i in range(NCORES)], axis=0).reshape(shp)
    outs = (y_prompt.astype(np.float32), y_sample.astype(np.float32),
            cat("ogk", (32, DEPTH, 256, 2, 64)), cat("ogv", (32, DEPTH, 256, 2, 64)),
            cat("onk", (32, DEPTH, 256, 4, 64)), cat("onv", (32, DEPTH, 256, 4, 64)),
            cat("ockv", (32, DEPTH, 256, 128)), cat("okr", (32, DEPTH, 256, 32)))
    if _debug:
        kernel.dbg = [{n: R[i]["dbg_" + n] for n in _debug} for i in range(NCORES)]
    return outs
```
